# Optimizing a Trainium2 kernel written in Bass

```python
import jax
import jax.numpy as jnp
from jax import lax
import numpy as np

D_MODEL = 1024
BATCH = 32
SEQ = 2048
DEPTH = 2

N_MIXERS = 2
MIX_WIDTH = D_MODEL
X_WIDTH = D_MODEL // 4
N_X_HEADS = 4
X_HEAD_DIM = X_WIDTH // N_X_HEADS
SEQ_MIX_WIDTH = MIX_WIDTH - X_WIDTH
LIN_HEAD_DIM = 128
N_LIN_HEADS = SEQ_MIX_WIDTH // LIN_HEAD_DIM
CONV_WIDTH = 4
CHUNK = 64
SB_HEAD_DIM = 64
N_SB_HEADS = SEQ_MIX_WIDTH // SB_HEAD_DIM
SB_BLOCK = 128
N_MEM = 256
D_FF = 4 * D_MODEL
EPS = 1e-6
N_A_LAYERS = (DEPTH + 1) // 2
N_B_LAYERS = DEPTH // 2
IN_A = 4 * SEQ_MIX_WIDTH + 2 * N_LIN_HEADS + X_WIDTH
IN_B = 3 * SEQ_MIX_WIDTH + X_WIDTH

kernel_name = 'hybrid_gdn_stickbreak_memory_trunk'


def rms_norm(x, g):
    xf = x.astype(jnp.float32)
    y = xf * lax.rsqrt(jnp.mean(xf * xf, axis=-1, keepdims=True) + EPS)
    return (y * g.astype(jnp.float32)).astype(x.dtype)


def l2norm(x):
    return x * lax.rsqrt(jnp.sum(x * x, axis=-1, keepdims=True) + EPS)


def causal_conv(x, w):
    c = x.shape[-1]
    return lax.conv_general_dilated(
        x, w[:, None, :].astype(x.dtype), window_strides=(1,),
        padding=[(CONV_WIDTH - 1, 0)], dimension_numbers=('NWC', 'WIO', 'NWC'),
        feature_group_count=c)


def gated_deltanet(p, conv_w, a_log, dt_bias, o_gain):
    B, S, _ = p.shape
    H, Dh, C, W = N_LIN_HEADS, LIN_HEAD_DIM, CHUNK, SEQ_MIX_WIDTH
    nc = S // C
    qkv = jax.nn.silu(causal_conv(p[..., :3 * W], conv_w)).astype(jnp.float32)
    gate = p[..., 3 * W:4 * W].astype(jnp.float32)
    beta = jax.nn.sigmoid(p[..., 4 * W:4 * W + H].astype(jnp.float32))
    g = -jnp.exp(a_log.astype(jnp.float32)) * jax.nn.softplus(
        p[..., 4 * W + H:].astype(jnp.float32) + dt_bias.astype(jnp.float32))
    q = l2norm(qkv[..., :W].reshape(B, S, H, Dh)) * (Dh ** -0.5)
    k = l2norm(qkv[..., W:2 * W].reshape(B, S, H, Dh))
    v = qkv[..., 2 * W:].reshape(B, S, H, Dh)

    def chunked(t):
        t = t.reshape((B, nc, C) + t.shape[2:])
        return jnp.moveaxis(jnp.moveaxis(t, 1, 0), 3, 2)

    q, k, v, beta = chunked(q), chunked(k), chunked(v), chunked(beta)
    gc = jnp.cumsum(chunked(g), axis=-1)
    idx = jnp.arange(C)
    causal = idx[:, None] >= idx[None, :]
    strict = idx[:, None] > idx[None, :]
    decay = jnp.exp(jnp.where(causal, gc[..., :, None] - gc[..., None, :], -jnp.inf))
    kb = k * beta[..., None]
    lower = jnp.where(strict, jnp.einsum('nbhcd,nbhsd->nbhcs', kb, k) * decay, 0.0)
    eye = jnp.eye(C, dtype=jnp.float32)
    rhs = jnp.concatenate([v * beta[..., None], kb * jnp.exp(gc)[..., None]], axis=-1)
    sol = lax.linalg.triangular_solve(eye + lower, rhs, left_side=True, lower=True,
                                      unit_diagonal=True)
    u, w = sol[..., :Dh], sol[..., Dh:]
    intra = jnp.einsum('nbhcd,nbhsd->nbhcs', q, k) * decay
    q_dec = q * jnp.exp(gc)[..., None]
    k_dec = k * jnp.exp(gc[..., -1:] - gc)[..., None]
    chunk_decay = jnp.exp(gc[..., -1])

    def step(state, inp):
        u_c, w_c, q_c, k_c, a_c, d_c = inp
        v_new = u_c - jnp.einsum('bhcd,bhde->bhce', w_c, state)
        o_c = jnp.einsum('bhcd,bhde->bhce', q_c, state) + jnp.einsum('bhcs,bhse->bhce', a_c, v_new)
        state = state * d_c[..., None, None] + jnp.einsum('bhcd,bhce->bhde', k_c, v_new)
        return state, o_c

    state0 = jnp.zeros((B, H, Dh, Dh), jnp.float32)
    _, o = lax.scan(step, state0, (u, w, q_dec, k_dec, intra, chunk_decay))
    o = jnp.swapaxes(jnp.moveaxis(o, 0, 1), 2, 3).reshape(B, S, H, Dh)
    o = o * lax.rsqrt(jnp.mean(o * o, axis=-1, keepdims=True) + EPS) * o_gain.astype(jnp.float32)
    o = o * jax.nn.silu(gate.reshape(B, S, H, Dh))
    return o.reshape(B, S, W)


def stick_breaking_attention(q, k, v):
    B, S, H, Dh = q.shape
    scale = Dh ** -0.5
    outs = []
    for blk in range(S // SB_BLOCK):
        t0 = blk * SB_BLOCK
        t1 = t0 + SB_BLOCK
        kb, vb = k[:, :t1], v[:, :t1]
        z = jnp.einsum('bthd,bshd->bhts', q[:, t0:t1], kb,
                       preferred_element_type=jnp.float32) * scale
        t_idx = t0 + jnp.arange(SB_BLOCK)[:, None]
        s_idx = jnp.arange(t1)[None, :]
        before = s_idx < t_idx
        log_beta = jax.nn.log_sigmoid(z)
        log_1m_beta = jnp.where(before, jax.nn.log_sigmoid(-z), 0.0)
        tail = lax.cumsum(log_1m_beta, axis=3, reverse=True) - log_1m_beta
        a = jnp.where(before, jnp.exp(log_beta + tail), 0.0)
        outs.append(jnp.einsum('bhts,bshd->bthd', a.astype(vb.dtype), vb))
    return jnp.concatenate(outs, axis=1).reshape(B, S, H * Dh)


def memory_attention(q, mem_kv):
    B, S, _ = q.shape
    q = q.reshape(B, S, N_X_HEADS, X_HEAD_DIM)
    k = mem_kv[..., :X_WIDTH].reshape(B, N_MEM, N_X_HEADS, X_HEAD_DIM)
    v = mem_kv[..., X_WIDTH:].reshape(B, N_MEM, N_X_HEADS, X_HEAD_DIM)
    s = jnp.einsum('bshd,bmhd->bhsm', q, k, preferred_element_type=jnp.float32) * (X_HEAD_DIM ** -0.5)
    p = jax.nn.softmax(s, axis=-1).astype(v.dtype)
    return jnp.einsum('bhsm,bmhd->bshd', p, v).reshape(B, S, X_WIDTH)


def setup_inputs(seed: int = 0) -> dict:
    key = jax.random.key(seed)
    ks = jax.random.split(key, 18)
    f32 = jnp.float32
    nrm = lambda k, shape, scale: jax.random.normal(k, shape, f32) * scale
    gain = lambda k, shape: 1.0 + 0.1 * jax.random.normal(k, shape, f32)
    return {
        'x': nrm(ks[0], (BATCH, SEQ, D_MODEL), 1.0),
        'mem': nrm(ks[1], (BATCH, N_MEM, D_MODEL), 1.0),
        'mem_norm': gain(ks[2], (D_MODEL,)),
        'norm_pre_mix': gain(ks[3], (DEPTH, D_MODEL)),
        'norm_post_mix': gain(ks[4], (DEPTH, D_MODEL)),
        'norm_pre_mlp': gain(ks[5], (DEPTH, D_MODEL)),
        'norm_post_mlp': gain(ks[6], (DEPTH, D_MODEL)),
        'w_in_a': nrm(ks[7], (N_A_LAYERS, D_MODEL, IN_A), D_MODEL ** -0.5),
        'conv_w_a': nrm(ks[8], (N_A_LAYERS, CONV_WIDTH, 3 * SEQ_MIX_WIDTH), CONV_WIDTH ** -0.5),
        'a_log_a': jnp.log(jax.random.uniform(ks[9], (N_A_LAYERS, N_LIN_HEADS), f32, 0.01, 1.0)),
        'dt_bias_a': nrm(ks[10], (N_A_LAYERS, N_LIN_HEADS), 0.1),
        'onorm_a': gain(ks[11], (N_A_LAYERS, LIN_HEAD_DIM)),
        'w_in_b': nrm(ks[12], (N_B_LAYERS, D_MODEL, IN_B), D_MODEL ** -0.5),
        'w_mem_kv': nrm(ks[13], (DEPTH, D_MODEL, 2 * X_WIDTH), D_MODEL ** -0.5),
        'w_out': nrm(ks[14], (DEPTH, MIX_WIDTH, D_MODEL), MIX_WIDTH ** -0.5),
        'w_up': nrm(ks[15], (DEPTH, D_MODEL, D_FF), D_MODEL ** -0.5),
        'w_down': nrm(ks[16], (DEPTH, D_FF, D_MODEL), D_FF ** -0.5),
    }


def reference(x, mem, mem_norm, norm_pre_mix, norm_post_mix, norm_pre_mlp, norm_post_mlp,
              w_in_a, conv_w_a, a_log_a, dt_bias_a, onorm_a, w_in_b, w_mem_kv, w_out,
              w_up, w_down):
    B, S, _ = x.shape
    mem_n = rms_norm(mem, mem_norm)
    for i in range(DEPTH):
        j = i // N_MIXERS
        h = rms_norm(x, norm_pre_mix[i])
        if i % N_MIXERS == 0:
            proj = h @ w_in_a[j]
            mix = gated_deltanet(proj[..., :IN_A - X_WIDTH], conv_w_a[j], a_log_a[j],
                                 dt_bias_a[j], onorm_a[j])
            mem_q = proj[..., IN_A - X_WIDTH:]
        else:
            proj = h @ w_in_b[j]
            W = SEQ_MIX_WIDTH
            q = proj[..., :W].reshape(B, S, N_SB_HEADS, SB_HEAD_DIM)
            k = proj[..., W:2 * W].reshape(B, S, N_SB_HEADS, SB_HEAD_DIM)
            v = proj[..., 2 * W:3 * W].reshape(B, S, N_SB_HEADS, SB_HEAD_DIM)
            mix = stick_breaking_attention(q, k, v)
            mem_q = proj[..., 3 * W:]
        cross = memory_attention(mem_q, mem_n @ w_mem_kv[i])
        y = jnp.concatenate([mix.astype(x.dtype), cross.astype(x.dtype)], axis=-1) @ w_out[i]
        x = x + rms_norm(y, norm_post_mix[i])
        h = rms_norm(x, norm_pre_mlp[i])
        y = jnp.square(jax.nn.relu(h @ w_up[i])) @ w_down[i]
        x = x + rms_norm(y, norm_post_mlp[i])
    return x
```

```python
import numpy as np
import concourse.bass as bass
import concourse.mybir as mybir
from concourse.bass_utils import run_bass_kernel_spmd
from contextlib import ExitStack

F32 = mybir.dt.float32
BF16 = mybir.dt.bfloat16
ALU = mybir.AluOpType
AF = mybir.ActivationFunctionType
AX = mybir.AxisListType

EPOCH = 30000
N_DMA_SEMS = 24
SB_BASE = 16512
SB_END = 229344

D = 1024
SEQ = 2048
NMEM = 256
DFF = 4096
IN_A = 3340
IN_B = 2560
EPS = 1e-6
NEG = -30000.0


class Buf:
    __slots__ = ("name", "last_w", "readers")

    def __init__(self, name=""):
        self.name = name
        self.last_w = None
        self.readers = []


class Op:
    __slots__ = ("eng", "fn", "raw", "war", "signal", "sem", "val", "is_dma", "ndma")

    def __init__(self, eng, fn, is_dma=False, ndma=1):
        self.eng = eng
        self.fn = fn
        self.raw = []
        self.war = []
        self.signal = False
        self.sem = None
        self.val = None
        self.is_dma = is_dma
        self.ndma = ndma


class Sched:
    ENGS = ("pe", "act", "dve", "pool", "sp")

    def __init__(self, nc, es):
        self.nc = nc
        self.es = es
        self.ops = {e: [] for e in self.ENGS}
        self.n_ops = 0
        self.dma_rr = 0
        self.dma_last = [None] * N_DMA_SEMS
        self.dma_tot = [0] * N_DMA_SEMS

    def add(self, eng, fn, reads=(), writes=(), is_dma=False, ndma=1):
        op = Op(eng, fn, is_dma, ndma)
        raw = set()
        war = set()
        for b in reads:
            if b.last_w is not None:
                raw.add(b.last_w)
        for b in writes:
            if b.last_w is not None:
                war.add(b.last_w)
            for r in b.readers:
                war.add(r)
        for b in writes:
            b.last_w = op
            b.readers = []
        for b in reads:
            b.readers.append(op)
        if is_dma:
            k = self.dma_rr
            self.dma_rr = (k + 1) % N_DMA_SEMS
            if self.dma_last[k] is not None:
                raw.add(self.dma_last[k])
            self.dma_last[k] = op
            self.dma_tot[k] += 16 * ndma
            op.sem = k
            op.val = self.dma_tot[k]
            op.signal = True
        for x in raw:
            if x.is_dma or x.eng != eng or eng != "pe":
                op.raw.append(x)
        for x in war:
            if x in raw:
                continue
            if x.is_dma or x.eng != eng:
                op.war.append(x)
        for x in op.raw:
            x.signal = True
        for x in op.war:
            x.signal = True
        self.ops[eng].append(op)
        self.n_ops += 1
        return op

    def barrier(self):
        lasts = []
        for e in self.ENGS:
            for op in reversed(self.ops[e]):
                if op.fn is not None and not op.is_dma:
                    lasts.append(op)
                    break
        dmas = [op for op in self.dma_last if op is not None]
        for e in self.ENGS:
            op = Op(e, None)
            for x in lasts:
                if x.eng != e:
                    op.raw.append(x)
                    x.signal = True
            for x in dmas:
                op.raw.append(x)
            self.ops[e].append(op)

    def emit(self):
        nc = self.nc
        es = self.es
        dma_sems = [es.enter_context(nc.semaphore(f"dsem{k}")) for k in range(N_DMA_SEMS)]
        eng_sems = {}
        for e in self.ENGS:
            cnt = 0
            for op in self.ops[e]:
                if op.is_dma:
                    op.sem = dma_sems[op.sem]
                    continue
                if op.signal:
                    key = (e, cnt // EPOCH)
                    if key not in eng_sems:
                        eng_sems[key] = es.enter_context(nc.semaphore(f"s_{e}_{key[1]}"))
                    op.sem = eng_sems[key]
                    op.val = cnt % EPOCH + 1
                    cnt += 1

        def run(e, eng):
            seen = {}
            for op in self.ops[e]:
                need = {}
                for x in op.raw + op.war:
                    k = id(x.sem)
                    if k not in need or need[k][1] < x.val:
                        need[k] = (x.sem, x.val)
                for k, (sem, val) in need.items():
                    if seen.get(k, 0) >= val:
                        continue
                    seen[k] = val
                    eng.wait_ge(sem, val)
                if op.fn is None:
                    continue
                r = op.fn(eng)
                if op.is_dma:
                    if not isinstance(r, (list, tuple)):
                        r = [r]
                    assert len(r) == op.ndma
                    for ins in r:
                        ins.then_inc(op.sem, 16)
                elif op.signal:
                    r.then_inc(op.sem, 1)
            if e == "sp":
                for k in range(N_DMA_SEMS):
                    if self.dma_tot[k] > 0:
                        eng.wait_ge(dma_sems[k], self.dma_tot[k])

        with nc.Block() as block:
            @block.tensor
            def _(eng):
                run("pe", eng)

            @block.scalar
            def _(eng):
                run("act", eng)

            @block.vector
            def _(eng):
                run("dve", eng)

            @block.gpsimd
            def _(eng):
                run("pool", eng)

            @block.sync
            def _(eng):
                run("sp", eng)


V_EPS = 0
V_ONE = 1
V_GAIN = 2
V_MEMG = V_GAIN + 64
V_CONV = V_MEMG + 8
V_ALOG = V_CONV + 72
V_DTB = V_ALOG + 1
NV = V_DTB + 1


CB_SEL = 1408
CB_MY = CB_SEL + 768
CB_MP = CB_MY + 512
CB_MI = CB_MP + 512
CB_I4 = CB_MI + 512
CB_N = CB_I4 + 512


def host_consts(inp):
    vec = np.zeros((128, NV), np.float32)
    vec[:, V_EPS] = EPS
    vec[:, V_ONE] = 1.0
    for layer in range(2):
        for j, nm in enumerate(["norm_pre_mix", "norm_post_mix", "norm_pre_mlp", "norm_post_mlp"]):
            g = np.asarray(inp[nm], np.float32)[layer]
            vec[:, V_GAIN + (layer * 4 + j) * 8:V_GAIN + (layer * 4 + j) * 8 + 8] = g.reshape(8, 128).T
    vec[:, V_MEMG:V_MEMG + 8] = np.asarray(inp["mem_norm"], np.float32).reshape(8, 128).T
    cw = np.asarray(inp["conv_w_a"], np.float32)[0]
    vec[:, V_CONV:V_CONV + 72] = cw.reshape(4, 18, 128).transpose(2, 1, 0).reshape(128, 72)
    vec[0:6, V_ALOG] = np.asarray(inp["a_log_a"], np.float32)[0]
    vec[0:6, V_DTB] = np.asarray(inp["dt_bias_a"], np.float32)[0]
    cf = np.zeros((128, 384), np.float32)
    cf[:, 256:384] = 1.0
    cf[:, 0:128] = np.eye(128, dtype=np.float32)
    cf[:, 128:256] = np.tile(np.asarray(inp["onorm_a"], np.float32)[0][None, :], (128, 1))
    p = np.arange(128)[:, None]
    c = np.arange(128)[None, :]
    cb = np.zeros((128, CB_N), np.float32)
    cb[:, 0:128] = np.eye(128)
    cb[:, 128:256] = 1.0
    cb[:, 256:384] = -1.0 * (p >= c)
    cb[:, 384:512] = -1.0
    cc = np.arange(896)[None, :]
    cb[:, 512:512 + 896] = np.where(cc - 384 <= p, NEG, 0.0)
    for h in range(6):
        cb[h, CB_SEL + h * 128:CB_SEL + (h + 1) * 128] = 1.0
    same = (p // 64) == (c // 64)
    mY = -1.0 * ((p > c) & same)
    mP = -1.0 * ((c > p) & same)
    mI = 1.0 * ((c >= p) & same)
    cb[:, CB_MY:CB_MY + 512] = np.tile(mY, (1, 4))
    cb[:, CB_MP:CB_MP + 512] = np.tile(mP, (1, 4))
    cb[:, CB_MI:CB_MI + 512] = np.tile(mI, (1, 4))
    cb[:, CB_I4:CB_I4 + 512] = np.tile(np.eye(128), (1, 4))
    return vec, cf, cb


class Prog:
    def __init__(self, nseq, stages=("mem", "mix0", "l0", "mix1", "l1"), dbg=None):
        self.nseq = nseq
        self.stages = stages
        self.dbg = dbg
        self.nc = nc = bass.Bass("TRN2", target_bir_lowering=False)
        self.es = ExitStack()
        self.S = Sched(nc, self.es)
        di = lambda n, s: nc.dram_tensor(n, list(s), F32, kind="ExternalInput").ap()
        self.x = di("x", [nseq, SEQ, D])
        self.mem = di("mem", [nseq, NMEM, D])
        self.w_in_a = di("w_in_a", [1, D, IN_A])
        self.w_in_b = di("w_in_b", [1, D, IN_B])
        self.w_mem_kv = di("w_mem_kv", [2, D, 512])
        self.w_out = di("w_out", [2, D, D])
        self.w_up = di("w_up", [2, D, DFF])
        self.w_down = di("w_down", [2, DFF, D])
        self.vecs_d = di("vecs", [128, NV])
        self.cf_d = di("cf", [128, 384])
        self.cb_d = di("cb", [128, CB_N])
        self.out = nc.dram_tensor("out", [nseq, SEQ, D], F32, kind="ExternalOutput").ap()
        if dbg is not None:
            self.dbg_out = nc.dram_tensor("dbg", list(dbg), F32, kind="ExternalOutput").ap()
        self.cur = SB_BASE
        self.reserved = set()
        self.rr = 0
        self.evac_rr = 0

    def alloc(self, name, shape, dtype, at=None):
        esz = 4 if dtype == F32 else 2
        n = 1
        for s in shape[1:]:
            n *= s
        nbytes = (n * esz + 31) // 32 * 32
        if at is None:
            at = self.cur
            self.cur += nbytes
        assert at % 32 == 0 and at + nbytes <= SB_END, (name, at, nbytes)
        t = self.nc.alloc_sbuf_tensor_at(name, list(shape), dtype, offset=at)
        return t

    def A(self, eng, name, reads, writes, *args, **kw):
        return self.S.add(eng, lambda e: getattr(e, name)(*args, **kw), reads, writes)

    def dma(self, out, in_, reads, writes, eng="sp"):
        return self.S.add(eng, lambda e: e.dma_start(out=out, in_=in_), reads, writes, is_dma=True)

    def bank(self):
        while True:
            k = self.rr
            self.rr = (k + 1) % 7
            if k not in self.reserved:
                return self.banks[k], self.bankB[k]

    def reserve(self):
        pb, pbB = self.bank()
        self.reserved.add(self.banks.index(pb))
        return pb, pbB

    def release(self, pb):
        self.reserved.discard(self.banks.index(pb))

    def evac_eng(self):
        self.evac_rr ^= 1
        return "act" if self.evac_rr else "dve"

    def build(self):
        self.setup()
        for s in range(self.nseq):
            self.load_x(s)
            if "mem" in self.stages:
                self.mem_prep(s)
            for layer in range(2):
                if f"mix{layer}" in self.stages:
                    if layer == 0:
                        self.mixer_a(layer)
                    else:
                        self.mixer_b(layer)
                if f"l{layer}" in self.stages:
                    self.mlp(layer)
            self.S.barrier()
            self.store_x(s)
        self.S.emit()
        return self.nc

    def setup(self):
        nc = self.nc
        self.banks = [self.es.enter_context(nc.psum_tensor(f"pb{k}", [128, 512], F32)) for k in range(7)]
        self.bankB = [Buf(f"pb{k}") for k in range(7)]
        self.pbh = self.es.enter_context(nc.psum_tensor("pbh", [128, 1024], BF16))
        self.pbhB = Buf("pbh")
        self.XT = self.alloc("XT", [128, 8, SEQ], F32)
        self.XB = [[Buf(f"X{f}_{g}") for g in range(4)] for f in range(8)]
        self.vec = self.alloc("vec", [128, NV], F32)
        self.cf = self.alloc("cf", [128, 384], F32)
        self.cb = self.alloc("cb", [128, CB_SEL], BF16)
        self.Bc = Buf("consts")
        self.ident_f = self.cf[:, 0:128]
        self.onorm_bc = self.cf[:, 128:256]
        self.ident_b = self.cb[:, 0:128]
        self.ones_b = self.cb[:, 128:256]
        self.negU_b = self.cb[:, 256:384]
        self.negones_b = self.cb[:, 384:512]
        self.maskbase = self.cb[:, 512:512 + 896]
        self.memnT = self.alloc("memnT", [128, 8, NMEM], BF16)
        self.BmemnT = Buf("memnT")
        self.sq = [self.alloc(f"sq{i}", [128, 512], BF16) for i in range(3)]
        self.sqB = [Buf(f"sq{i}") for i in range(3)]
        self.sq_rr = 0
        self.lnb = self.alloc("lnb", [128, 512], F32)
        self.BlnB = Buf("lnb")
        self.rstd_at = self.cur
        self.rstd = [self.alloc(f"rstd{i}", [128, 512], F32) for i in range(2)]
        self.rstdB = [Buf(f"rstd{i}") for i in range(2)]
        self.rstd_rr = 0
        self.tmpf_at = self.cur
        self.tmpf = [self.alloc(f"tmpf{i}", [128, 512], F32) for i in range(2)]
        self.tmpfB = [Buf(f"tmpf{i}") for i in range(2)]
        self.tmpf_rr = 0
        self.R0 = self.cur
        self.xin = [self.nc.alloc_sbuf_tensor_at("xin0", [128, D], F32, offset=self.tmpf_at),
                    self.nc.alloc_sbuf_tensor_at("xin1", [128, D], F32, offset=self.rstd_at)]
        self.xinB = [self.tmpfB, self.rstdB]
        self.dma(self.vec[:], self.vecs_d[:, :], [], [self.Bc])
        self.dma(self.cf[:], self.cf_d[:, :], [], [self.Bc])
        self.dma(self.cb[:], self.cb_d[:, 0:CB_SEL], [], [self.Bc], eng="pool")

    def load_x(self, s):
        for tt in range(16):
            xi, xiB = self.xin[tt % 2], self.xinB[tt % 2]
            self.dma(xi[:], self.x[s, tt * 128:(tt + 1) * 128, :], [], xiB)
            g = tt // 4
            for half in range(2):
                pb, pbB = self.bank()
                for j in range(4):
                    f = half * 4 + j
                    self.A("pe", "transpose", xiB + [self.Bc], [pbB], pb[:, j * 128:(j + 1) * 128],
                           xi[:, f * 128:(f + 1) * 128], self.ident_f)
                dst = self.XT[:, half * 4:half * 4 + 4, tt * 128:(tt + 1) * 128]
                src = pb[:, :].rearrange("p (j t) -> p j t", j=4)
                wr = [self.XB[half * 4 + j][g] for j in range(4)]
                self.A("dve", "tensor_copy", [pbB], wr, out=dst, in_=src)

    def store_x(self, s):
        for tt in range(16):
            xi, xiB = self.xin[tt % 2], self.xinB[tt % 2]
            g = tt // 4
            for half in range(2):
                pb, pbB = self.bank()
                for j in range(4):
                    f = half * 4 + j
                    self.A("pe", "transpose", [self.XB[f][g], self.Bc], [pbB], pb[:, j * 128:(j + 1) * 128],
                           self.XT[:, f, tt * 128:(tt + 1) * 128], self.ident_f)
                dst = xi[:, half * 512:(half + 1) * 512]
                if self.evac_eng() == "act":
                    self.A("act", "activation", [pbB], xiB, out=dst, in_=pb[:, :], func=AF.Identity)
                else:
                    self.A("dve", "tensor_copy", [pbB], xiB, out=dst, in_=pb[:, :])
            self.dma(self.out[s, tt * 128:(tt + 1) * 128, :], xi[:], xiB, [])

    def rstd_from_ss(self, pb, pbB):
        k = self.rstd_rr
        self.rstd_rr ^= 1
        r, rB = self.rstd[k], self.rstdB[k]
        self.A("act", "activation", [pbB, self.Bc], [self.BlnB], out=self.lnb[:], in_=pb[:, :], func=AF.Ln,
               scale=1.0 / D, bias=self.vec[:, V_EPS:V_EPS + 1])
        self.A("act", "activation", [self.BlnB], [rB], out=r[:], in_=self.lnb[:], func=AF.Exp, scale=-0.5)
        return r, rB

    def prenorm(self, g, gcol, dst, dstB):
        ts = slice(g * 512, (g + 1) * 512)
        pb, pbB = self.bank()
        for f in range(8):
            k = self.sq_rr
            self.sq_rr = (k + 1) % 3
            self.A("act", "activation", [self.XB[f][g]], [self.sqB[k]], out=self.sq[k][:], in_=self.XT[:, f, ts],
                   func=AF.Square)
            self.A("pe", "matmul", [self.sqB[k], self.Bc], [pbB], pb[:, :], lhsT=self.ones_b, rhs=self.sq[k][:],
                   start=(f == 0), stop=(f == 7))
        r, rB = self.rstd_from_ss(pb, pbB)
        for f in range(8):
            self.A("dve", "scalar_tensor_tensor", [self.XB[f][g], rB, self.Bc], [dstB(f)], out=dst(f),
                   in0=self.XT[:, f, ts], scalar=self.vec[:, gcol + f:gcol + f + 1], in1=r[:],
                   op0=ALU.mult, op1=ALU.mult)

    def postnorm_apply(self, g, gcol, ys, ysB, ssb, ssbB):
        ts = slice(g * 512, (g + 1) * 512)
        r, rB = self.rstd_from_ss(ssb, ssbB)
        for f in range(8):
            k = self.tmpf_rr
            self.tmpf_rr ^= 1
            t, tB = self.tmpf[k], self.tmpfB[k]
            self.A("dve", "tensor_tensor", [ysB(f), rB], [tB], out=t[:], in0=ys(f), in1=r[:], op=ALU.mult)
            self.A("dve", "scalar_tensor_tensor", [tB, self.XB[f][g], self.Bc], [self.XB[f][g]],
                   out=self.XT[:, f, ts], in0=t[:], scalar=self.vec[:, gcol + f:gcol + f + 1],
                   in1=self.XT[:, f, ts], op0=ALU.mult, op1=ALU.add)


    def mem_prep(self, s):
        self.S.barrier()
        R0 = self.R0
        memT = self.alloc_once("mp_memT", [128, 8, NMEM], F32, R0)
        mB = Buf()
        for mt in range(2):
            xi, xiB = self.xin[mt], self.xinB[mt]
            self.dma(xi[:], self.mem[s, mt * 128:(mt + 1) * 128, :], [], xiB)
            for half in range(2):
                pb, pbB = self.bank()
                for j in range(4):
                    f = half * 4 + j
                    self.A("pe", "transpose", xiB + [self.Bc], [pbB], pb[:, j * 128:(j + 1) * 128],
                           xi[:, f * 128:(f + 1) * 128], self.ident_f)
                self.A("dve", "tensor_copy", [pbB], [mB], out=memT[:, half * 4:half * 4 + 4, mt * 128:(mt + 1) * 128],
                       in_=pb[:, :].rearrange("p (j t) -> p j t", j=4))
        self.S.barrier()
        pb, pbB = self.bank()
        for f in range(8):
            k = self.sq_rr
            self.sq_rr = (k + 1) % 3
            self.A("act", "activation", [mB], [self.sqB[k]], out=self.sq[k][:, 0:NMEM], in_=memT[:, f, :], func=AF.Square)
            self.A("pe", "matmul", [self.sqB[k], self.Bc], [pbB], pb[:, 0:NMEM], lhsT=self.ones_b, rhs=self.sq[k][:, 0:NMEM],
                   start=(f == 0), stop=(f == 7))
        self.A("act", "activation", [pbB, self.Bc], [self.BlnB], out=self.lnb[:, 0:NMEM], in_=pb[:, 0:NMEM], func=AF.Ln,
               scale=1.0 / D, bias=self.vec[:, V_EPS:V_EPS + 1])
        r, rB = self.rstd[0], self.rstdB[0]
        self.A("act", "activation", [self.BlnB], [rB], out=r[:, 0:NMEM], in_=self.lnb[:, 0:NMEM], func=AF.Exp, scale=-0.5)
        for f in range(8):
            self.A("dve", "scalar_tensor_tensor", [mB, rB, self.Bc], [self.BmemnT], out=self.memnT[:, f, :],
                   in0=memT[:, f, :], scalar=self.vec[:, V_MEMG + f:V_MEMG + f + 1], in1=r[:, 0:NMEM],
                   op0=ALU.mult, op1=ALU.mult)

    def alloc_once(self, name, shape, dtype, at):
        if not hasattr(self, "_once"):
            self._once = {}
        if name not in self._once:
            self._once[name] = self.alloc(name, shape, dtype, at=at)
        return self._once[name]

    def copy_evac(self, reads, writes, out, in_, scale=None):
        if self.evac_eng() == "act":
            if scale is None:
                self.A("act", "activation", reads, writes, out=out, in_=in_, func=AF.Identity)
            else:
                self.A("act", "activation", reads, writes, out=out, in_=in_, func=AF.Identity, scale=scale)
        else:
            if scale is None:
                self.A("dve", "tensor_copy", reads, writes, out=out, in_=in_)
            else:
                self.A("dve", "tensor_scalar", reads, writes, out=out, in0=in_, scalar1=scale, scalar2=None, op0=ALU.mult)

    def mem_kv(self, layer, kTm, vm, kvB, w, wB):
        wd = self.w_mem_kv[layer].rearrange("(k p) n -> p k n", p=128)
        self.dma(w[0][:], wd[:, :, 0:256], [], [wB[0]], eng="pool")
        self.dma(w[1][:], wd[:, :, 256:512], [], [wB[1]], eng="pool")
        for x in range(4):
            pb, pbB = self.bank()
            po = 64 * (x % 2)
            for kc in range(8):
                self.A("pe", "matmul", [wB[0], self.BmemnT], [pbB], pb[po:po + 64, 0:NMEM],
                       lhsT=w[0][:, kc, x * 64:(x + 1) * 64], rhs=self.memnT[:, kc, :], start=(kc == 0), stop=(kc == 7))
            self.copy_evac([pbB], [kvB], kTm[po:po + 64, x // 2, :], pb[po:po + 64, 0:NMEM])
        for mt in range(2):
            pb, pbB = self.bank()
            for kc in range(8):
                self.A("pe", "matmul", [wB[1], self.BmemnT], [pbB], pb[:, 0:256],
                       lhsT=self.memnT[:, kc, mt * 128:(mt + 1) * 128], rhs=w[1][:, kc, :], start=(kc == 0), stop=(kc == 7))
            self.copy_evac([pbB], [kvB], vm[:, mt, :], pb[:, 0:256])

    def mem_attn(self, memqT, mqB, kTm, vm, kvB, pT, pTB, rec, recB):
        for g in range(4):
            ts = slice(g * 512, (g + 1) * 512)
            for x in range(4):
                po = 64 * (x % 2)
                tl = x // 2
                ob, obB = self.reserve()
                db, dbB = self.reserve()
                for mt in range(2):
                    zb, zbB = self.bank()
                    self.A("pe", "matmul", [kvB, mqB[tl][g]], [zbB], zb[:, :],
                           lhsT=kTm[po:po + 64, tl, mt * 128:(mt + 1) * 128], rhs=memqT[po:po + 64, tl, ts],
                           start=True, stop=True)
                    self.A("act", "activation", [zbB], [pTB[mt]], out=pT[mt][:], in_=zb[:, :], func=AF.Exp, scale=0.125)
                    self.A("pe", "matmul", [kvB, pTB[mt]], [obB], ob[po:po + 64, :],
                           lhsT=vm[:, mt, x * 64:(x + 1) * 64], rhs=pT[mt][:], start=(mt == 0), stop=(mt == 1))
                    self.A("pe", "matmul", [self.Bc, pTB[mt]], [dbB], db[po:po + 64, :],
                           lhsT=self.ones_b[:, 0:64], rhs=pT[mt][:], start=(mt == 0), stop=(mt == 1))
                self.A("dve", "reciprocal", [dbB], [recB], out=rec[po:po + 64, :], in_=db[po:po + 64, :])
                self.A("dve", "tensor_tensor", [obB, recB, mqB[tl][g]], [mqB[tl][g]], out=memqT[po:po + 64, tl, ts],
                       in0=ob[po:po + 64, :], in1=rec[po:po + 64, :], op=ALU.mult)
                self.release(ob)
                self.release(db)

    def out_proj(self, layer, mixK, ys, ysB, w, wB):
        gpost = V_GAIN + (layer * 4 + 1) * 8
        wd = self.w_out[layer].rearrange("(k p) n -> p k n", p=128)
        ssb = [self.reserve() for _ in range(4)]
        for fq in range(4):
            k = fq % 2
            self.dma(w[k][:], wd[:, :, fq * 256:(fq + 1) * 256], [], [wB[k]], eng="pool")
            for ff in range(2):
                f = fq * 2 + ff
                for g in range(4):
                    ts = slice(g * 512, (g + 1) * 512)
                    pb, pbB = self.bank()
                    for kc in range(8):
                        ap, bufs = mixK(kc)
                        self.A("pe", "matmul", [wB[k], bufs[g]], [pbB], pb[:, :],
                               lhsT=w[k][:, kc, ff * 128:(ff + 1) * 128], rhs=ap[:, ts], start=(kc == 0), stop=(kc == 7))
                    self.A("dve", "tensor_copy", [pbB], [ysB[f][g]], out=ys[:, f, ts], in_=pb[:, :])
                    ks = self.sq_rr
                    self.sq_rr = (ks + 1) % 3
                    self.A("act", "activation", [ysB[f][g]], [self.sqB[ks]], out=self.sq[ks][:], in_=ys[:, f, ts], func=AF.Square)
                    self.A("pe", "matmul", [self.sqB[ks], self.Bc], [ssb[g][1]], ssb[g][0][:, :],
                           lhsT=self.ones_b, rhs=self.sq[ks][:], start=(f == 0), stop=(f == 7))
        for g in range(4):
            self.postnorm_apply(g, gpost, lambda f, g=g: ys[:, f, g * 512:(g + 1) * 512], lambda f, g=g: ysB[f][g],
                                ssb[g][0], ssb[g][1])
        for g in range(4):
            self.release(ssb[g][0])


    def mixer_a(self, layer):
        self.S.barrier()
        A = self.A
        R0 = self.R0
        o = R0
        hT = self.alloc_once("a_hT", [128, 8, 1024], BF16, o); o += 16384
        mixT = self.alloc_once("a_mix", [128, 6, SEQ], BF16, o); o += 24576
        mq = self.alloc_once("a_mq", [128, 2, SEQ], BF16, o); o += 8192
        w = [self.alloc_once(f"a_w{i}", [128, 8, 256], BF16, o + 4096 * i) for i in range(2)]; o += 8192
        rows = {n: self.alloc_once("a_r" + n, [6, 1024], BF16, o + 2048 * i) for i, n in enumerate(["Em", "Ep", "KD", "BE", "be"])}
        o += 10240
        rtmp = [self.alloc_once(f"a_rt{i}", [6, 1024], F32, o + 4096 * i) for i in range(3)]; o += 12288
        raw = self.alloc_once("a_raw", [128, 3, 1028], BF16, o); o += 6176
        hist = self.alloc_once("a_hist", [128, 18, 4], BF16, o); o += 160
        dg = self.alloc_once("a_dg", [128, 12, 128], BF16, o); o += 3072
        cqkv = self.alloc_once("a_c", [128, 3, 512], BF16, o); o += 3072
        scl = self.alloc_once("a_s", [128, 5, 512], BF16, o); o += 5120
        tok = self.alloc_once("a_tok", [128, 2, 512], BF16, o); o += 2048
        self.yp_at = o
        YP = [self.alloc_once(f"a_yp{i}", [128, 4, 512], BF16, o + 4096 * i) for i in range(2)]; o += 8192
        itT = self.alloc_once("a_it", [128, 512], BF16, o); o += 1024
        Emb = self.alloc_once("a_emb", [128, 512], F32, o); o += 2048
        gs = self.alloc_once("a_gs", [128, 512], BF16, o); o += 1024
        S32a = self.alloc_once("a_S32", [128, 6, 128], F32, o); o += 3072
        Sbfa = self.alloc_once("a_Sb", [128, 6, 128], BF16, o); o += 1536
        rhsb = self.alloc_once("a_rhs", [128, 128], BF16, o); o += 256
        vn = self.alloc_once("a_vn", [128, 128], BF16, o); o += 256
        ont = self.alloc_once("a_on", [128, 4, 128], BF16, o); o += 1024
        st = self.alloc_once("a_st", [128, 8], F32, o); o += 32
        nA = self.alloc_once("a_nA", [6, 8], F32, o); o += 32
        assert o <= SB_END, o
        o2 = self.yp_at
        kTm = self.alloc_once("a_kTm", [128, 2, NMEM], BF16, o2); o2 += 1024
        vm = self.alloc_once("a_vm", [128, 2, 256], BF16, o2); o2 += 1024
        pT = [self.alloc_once(f"a_p{i}", [128, 512], BF16, o2 + 1024 * i) for i in range(2)]; o2 += 2048
        rec = self.alloc_once("a_rec", [128, 512], F32, o2); o2 += 2048
        B = lambda: Buf()
        wB = [B(), B()]; hB = [[B() for _ in range(2)] for _ in range(8)]
        mixB = [[B() for _ in range(4)] for _ in range(6)]; mqB = [[B() for _ in range(4)] for _ in range(2)]
        rowB = B(); rtB = [B(), B(), B()]; rawB = B(); histB = B(); dgB = B(); cB = B(); sclB = B(); tokB = B()
        ypB = [B(), B()]; itB = B(); embB = B(); gsB = B(); S32B = B(); SbB = B(); rhsB = B(); vnB = B(); onB = B()
        stB = B(); nAB = B(); kvB = B(); pTB = [B(), B()]; recB = B()
        cbt = self.alloc_once("a_cb2", [128, CB_N - CB_SEL], BF16, o)
        o += (CB_N - CB_SEL) * 2
        assert o <= SB_END, o
        self.dma(cbt[:], self.cb_d[:, CB_SEL:CB_N], [], [self.Bc], eng="pool")
        sel = lambda h: cbt[0:6, h * 128:(h + 1) * 128]
        mY = cbt[:, CB_MY - CB_SEL:CB_MY - CB_SEL + 512]; mP = cbt[:, CB_MP - CB_SEL:CB_MP - CB_SEL + 512]
        mI = cbt[:, CB_MI - CB_SEL:CB_MI - CB_SEL + 512]
        I4 = cbt[:, CB_I4 - CB_SEL:CB_I4 - CB_SEL + 512]
        gpre = V_GAIN + (layer * 4 + 0) * 8
        wd = self.w_in_a[0].rearrange("(k p) n -> p k n", p=128)
        W = 768
        vec = self.vec
        A("act", "activation", [self.Bc], [nAB], out=nA[:, 0:1], in_=vec[0:6, V_ALOG:V_ALOG + 1], func=AF.Exp)
        A("pool", "memset", [], [histB], hist[:], 0.0)
        wc = 0
        for half in range(2):
            for gg in range(2):
                self.prenorm(half * 2 + gg, gpre, lambda f, gg=gg: hT[:, f, gg * 512:(gg + 1) * 512], lambda f, gg=gg: hB[f][gg])
            k = wc % 2; wc += 1
            self.dma(w[k][:], wd[:, :, 3084:3340], [], [wB[k]], eng="pool")
            for mm in range(2):
                for gg in range(2):
                    g = half * 2 + gg
                    pb, pbB = self.bank()
                    for kc in range(8):
                        A("pe", "matmul", [wB[k], hB[kc][gg]], [pbB], pb[:, :], lhsT=w[k][:, kc, mm * 128:(mm + 1) * 128],
                          rhs=hT[:, kc, gg * 512:(gg + 1) * 512], start=(kc == 0), stop=(kc == 7))
                    self.copy_evac([pbB], [mqB[mm][g]], mq[:, mm, g * 512:(g + 1) * 512], pb[:, :])
            k = wc % 2; wc += 1
            self.dma(w[k][:, :, 0:12], wd[:, :, 3072:3084], [], [wB[k]], eng="pool")
            for gg in range(2):
                gsl = slice(gg * 512, (gg + 1) * 512)
                pb, pbB = self.bank()
                pa, paB = self.bank()
                for kc in range(8):
                    A("pe", "matmul", [wB[k], hB[kc][gg]], [pbB], pb[0:6, :], lhsT=w[k][:, kc, 0:6], rhs=hT[:, kc, gsl],
                      start=(kc == 0), stop=(kc == 7))
                for kc in range(8):
                    A("pe", "matmul", [wB[k], hB[kc][gg]], [paB], pa[0:6, :], lhsT=w[k][:, kc, 6:12], rhs=hT[:, kc, gsl],
                      start=(kc == 0), stop=(kc == 7))
                A("act", "activation", [pbB], [rtB[0]], out=rtmp[0][:, gsl], in_=pb[0:6, :], func=AF.Exp, scale=-1.0)
                A("dve", "tensor_scalar", [rtB[0]], [rtB[0]], out=rtmp[0][:, gsl], in0=rtmp[0][:, gsl], scalar1=1.0, scalar2=None, op0=ALU.add)
                A("dve", "reciprocal", [rtB[0]], [rtB[0]], out=rtmp[0][:, gsl], in_=rtmp[0][:, gsl])
                A("act", "activation", [paB, self.Bc], [rtB[1]], out=rtmp[1][:, gsl], in_=pa[0:6, :], func=AF.Exp,
                  bias=vec[0:6, V_DTB:V_DTB + 1])
                A("act", "activation", [rtB[1], self.Bc], [rtB[1]], out=rtmp[1][:, gsl], in_=rtmp[1][:, gsl], func=AF.Ln,
                  bias=vec[0:6, V_ONE:V_ONE + 1])
                A("dve", "tensor_scalar", [rtB[1], nAB], [rtB[1]], out=rtmp[1][:, gsl], in0=rtmp[1][:, gsl], scalar1=nA[:, 0:1],
                  scalar2=-1.0, op0=ALU.mult, op1=ALU.mult)
            for c in range(16):
                cs = slice(c * 64, (c + 1) * 64)
                A("dve", "tensor_tensor_scan", [rtB[1], self.Bc], [rtB[2]], out=rtmp[2][:, cs], data0=self.cf[0:6, 256:320],
                  data1=rtmp[1][:, cs], initial=0.0, op0=ALU.mult, op1=ALU.add)
            gc3 = rtmp[2][:, :].rearrange("p (c t) -> p c t", t=64)
            A("act", "activation", [rtB[2]], [rowB], out=rows["Em"][:], in_=rtmp[2][:], func=AF.Exp)
            A("act", "activation", [rtB[2]], [rowB], out=rows["Ep"][:], in_=rtmp[2][:], func=AF.Exp, scale=-1.0)
            A("dve", "tensor_copy", [rtB[0]], [rowB], out=rows["be"][:], in_=rtmp[0][:])
            A("act", "activation", [rtB[2]], [rtB[1]], out=rtmp[1][:], in_=rtmp[2][:], func=AF.Exp)
            A("dve", "tensor_tensor", [rtB[0], rtB[1]], [rowB], out=rows["BE"][:], in0=rtmp[0][:], in1=rtmp[1][:], op=ALU.mult)
            A("dve", "tensor_tensor", [rtB[2]], [rtB[1]], out=rtmp[1][:, :].rearrange("p (c t) -> p c t", t=64),
              in0=gc3[:, :, 63:64].to_broadcast([6, 16, 64]), in1=gc3, op=ALU.subtract)
            A("act", "activation", [rtB[1]], [rowB], out=rows["KD"][:], in_=rtmp[1][:], func=AF.Exp)
            for h in range(6):
                S32 = S32a[:, h, :]
                Sbf = Sbfa[:, h, :]
                for ti, tile in enumerate([h, 6 + h, 12 + h]):
                    for tap in range(4):
                        col = V_CONV + tile * 4 + tap
                        A("dve", "tensor_scalar", [self.Bc], [dgB], out=dg[:, ti * 4 + tap, :], in0=self.ident_b,
                          scalar1=vec[:, col:col + 1], scalar2=None, op0=ALU.mult)
                for ti in range(3):
                    A("dve", "tensor_copy", [histB], [rawB], out=raw[:, ti, 0:3], in_=hist[:, h * 3 + ti, 0:3])
                for ti, tile in enumerate([h, 6 + h, 12 + h]):
                    k = wc % 2; wc += 1
                    self.dma(w[k][:, :, 0:128], wd[:, :, tile * 128:(tile + 1) * 128], [], [wB[k]], eng="pool")
                    for gg in range(2):
                        pb, pbB = self.bank()
                        for kc in range(8):
                            A("pe", "matmul", [wB[k], hB[kc][gg]], [pbB], pb[:, :], lhsT=w[k][:, kc, 0:128],
                              rhs=hT[:, kc, gg * 512:(gg + 1) * 512], start=(kc == 0), stop=(kc == 7))
                        self.copy_evac([pbB], [rawB], raw[:, ti, 3 + gg * 512:3 + (gg + 1) * 512], pb[:, :])
                for ti in range(3):
                    A("dve", "tensor_copy", [rawB], [histB], out=hist[:, h * 3 + ti, 0:3], in_=raw[:, ti, 1024:1027])
                for gg in range(2):
                    g = half * 2 + gg
                    gsl = slice(gg * 512, (gg + 1) * 512)
                    for ti in range(3):
                        pb, pbB = self.bank()
                        for tap in range(4):
                            A("pe", "matmul", [dgB, rawB], [pbB], pb[:, :], lhsT=dg[:, ti * 4 + tap, :],
                              rhs=raw[:, ti, gg * 512 + tap:gg * 512 + tap + 512], start=(tap == 0), stop=(tap == 3))
                        A("act", "activation", [pbB], [cB], out=cqkv[:, ti, :], in_=pb[:, :], func=AF.Silu)
                    for ti in range(2):
                        ks = self.sq_rr
                        self.sq_rr = (ks + 1) % 3
                        A("dve", "tensor_tensor", [cB], [self.sqB[ks]], out=self.sq[ks][:], in0=cqkv[:, ti, :], in1=cqkv[:, ti, :], op=ALU.mult)
                        pb, pbB = self.bank()
                        A("pe", "matmul", [self.sqB[ks], self.Bc], [pbB], pb[:, :], lhsT=self.ones_b, rhs=self.sq[ks][:], start=True, stop=True)
                        A("act", "activation", [pbB, self.Bc], [self.BlnB], out=self.lnb[:], in_=pb[:, :], func=AF.Ln,
                          bias=vec[:, V_EPS:V_EPS + 1])
                        A("act", "activation", [self.BlnB], [self.rstdB[ti]], out=self.rstd[ti][:], in_=self.lnb[:], func=AF.Exp, scale=-0.5)
                    def bc(name):
                        pb, pbB = self.bank()
                        A("pe", "matmul", [rowB, self.Bc], [pbB], pb[:, :], lhsT=sel(h), rhs=rows[name][:, gsl], start=True, stop=True)
                        return pb, pbB
                    pb, pbB = bc("Em")
                    A("act", "activation", [pbB], [embB], out=Emb[:], in_=pb[:, :], func=AF.Identity)
                    A("dve", "tensor_tensor", [pbB, self.rstdB[0]], [self.tmpfB[0]], out=self.tmpf[0][:], in0=self.rstd[0][:], in1=pb[:, :], op=ALU.mult)
                    A("dve", "scalar_tensor_tensor", [cB, self.tmpfB[0]], [sclB], out=scl[:, 0, :], in0=cqkv[:, 0, :], scalar=128.0 ** -0.5,
                      in1=self.tmpf[0][:], op0=ALU.mult, op1=ALU.mult)
                    for si, name in [(1, "Ep"), (2, "BE"), (3, "KD")]:
                        pb, pbB = bc(name)
                        tk = si % 2
                        A("dve", "tensor_tensor", [pbB, self.rstdB[1]], [self.tmpfB[tk]], out=self.tmpf[tk][:], in0=self.rstd[1][:], in1=pb[:, :], op=ALU.mult)
                        A("dve", "tensor_tensor", [cB, self.tmpfB[tk]], [sclB], out=scl[:, si, :], in0=cqkv[:, 1, :], in1=self.tmpf[tk][:], op=ALU.mult)
                    pb, pbB = bc("be")
                    A("dve", "tensor_tensor", [cB, pbB], [sclB], out=scl[:, 4, :], in0=cqkv[:, 2, :], in1=pb[:, :], op=ALU.mult)
                    for j, si in enumerate([3, 4]):
                        for blk in range(4):
                            A("pe", "transpose", [sclB, self.Bc], [self.pbhB], self.pbh[:, j * 512 + blk * 128:j * 512 + (blk + 1) * 128],
                              scl[:, si, blk * 128:(blk + 1) * 128], self.ident_b)
                    A("dve", "tensor_copy", [self.pbhB], [tokB], out=tok[:, :, :], in_=self.pbh[:, :].rearrange("p (j t) -> p j t", j=2))
                    Y, P, R, Q = 0, 1, 2, 3
                    cur, nxt = 0, 1
                    for (la, ra, msk, dst, dB) in [(2, 1, mY, YP[cur][:, Y, :], ypB[cur]), (1, 2, mP, YP[cur][:, P, :], ypB[cur]),
                                                   (1, 0, mI, itT[:, :], itB)]:
                        pb, pbB = self.bank()
                        for blk in range(4):
                            bs = slice(blk * 128, (blk + 1) * 128)
                            A("pe", "matmul", [sclB], [pbB], pb[:, bs], lhsT=scl[:, la, bs], rhs=scl[:, ra, bs], start=True, stop=True)
                        A("dve", "tensor_tensor", [pbB, self.Bc], [dB], out=dst, in0=pb[:, :], in1=msk, op=ALU.mult)
                    A("dve", "tensor_tensor", [ypB[cur], self.Bc], [ypB[cur]], out=YP[cur][:, R, :], in0=YP[cur][:, P, :], in1=I4, op=ALU.add)
                    A("dve", "tensor_tensor", [ypB[cur], self.Bc], [ypB[cur]], out=YP[cur][:, Q, :], in0=YP[cur][:, Y, :], in1=I4, op=ALU.add)
                    for lev in range(1, 6):
                        c_, n_ = YP[cur], YP[nxt]
                        for (la, ra, dsti) in [(P, Y, Y), (Y, P, P)]:
                            pb, pbB = self.bank()
                            for blk in range(4):
                                bs = slice(blk * 128, (blk + 1) * 128)
                                A("pe", "matmul", [ypB[cur]], [pbB], pb[:, bs], lhsT=c_[:, la, bs], rhs=c_[:, ra, bs], start=True, stop=True)
                            self.copy_evac([pbB], [ypB[nxt]], n_[:, dsti, :], pb[:, :])
                        for (la, rb, acc, dsti) in [(Q, P, R, R), (R, Y, Q, Q)]:
                            if dsti == Q and lev == 5:
                                continue
                            pb, pbB = self.bank()
                            for blk in range(4):
                                bs = slice(blk * 128, (blk + 1) * 128)
                                A("pe", "matmul", [ypB[cur], ypB[nxt]], [pbB], pb[:, bs], lhsT=c_[:, la, bs], rhs=n_[:, rb, bs], start=True, stop=True)
                            A("dve", "tensor_tensor", [pbB, ypB[cur]], [ypB[nxt]], out=n_[:, dsti, :], in0=pb[:, :], in1=c_[:, acc, :], op=ALU.add)
                        cur, nxt = nxt, cur
                    TT = YP[cur]
                    k = wc % 2; wc += 1
                    self.dma(w[k][:, :, 0:128], wd[:, :, 3 * W + h * 128:3 * W + (h + 1) * 128], [], [wB[k]], eng="pool")
                    pb, pbB = self.bank()
                    for kc in range(8):
                        A("pe", "matmul", [wB[k], hB[kc][gg]], [pbB], pb[:, :], lhsT=w[k][:, kc, 0:128], rhs=hT[:, kc, gsl],
                          start=(kc == 0), stop=(kc == 7))
                    A("act", "activation", [pbB], [gsB], out=gs[:], in_=pb[:, :], func=AF.Silu)
                    if half == 0 and gg == 0:
                        A("pool", "memset", [], [S32B], S32, 0.0)
                        A("pool", "memset", [], [SbB], Sbf, 0.0)
                    for ci in range(8):
                        blk, hb = ci // 2, ci % 2
                        po = 64 * hb
                        cs = slice(ci * 64, (ci + 1) * 64)
                        bcs = slice(blk * 128 + po, blk * 128 + po + 64)
                        p1, p1B = self.bank()
                        A("pe", "matmul", [sclB, SbB], [p1B], p1[po:po + 64, 0:128], lhsT=scl[:, 2, cs], rhs=Sbf, start=True, stop=True)
                        A("dve", "tensor_tensor", [tokB, p1B], [rhsB], out=rhsb[po:po + 64, :], in0=tok[po:po + 64, 1, blk * 128:(blk + 1) * 128],
                          in1=p1[po:po + 64, 0:128], op=ALU.subtract)
                        p2, p2B = self.bank()
                        A("pe", "matmul", [ypB[cur], rhsB], [p2B], p2[po:po + 64, 0:128], lhsT=TT[po:po + 64, R, bcs], rhs=rhsb[po:po + 64, :],
                          start=True, stop=True)
                        A("act", "activation", [p2B], [vnB], out=vn[po:po + 64, :], in_=p2[po:po + 64, 0:128], func=AF.Identity)
                        p3, p3B = self.bank()
                        A("pe", "matmul", [sclB, SbB], [p3B], p3[po:po + 64, 0:128], lhsT=scl[:, 0, cs], rhs=Sbf, start=True, stop=False)
                        A("pe", "matmul", [itB, vnB], [p3B], p3[po:po + 64, 0:128], lhsT=itT[po:po + 64, bcs], rhs=vn[po:po + 64, :],
                          start=False, stop=True)
                        p4, p4B = self.bank()
                        A("pe", "matmul", [tokB, vnB], [p4B], p4[:, 0:128], lhsT=tok[po:po + 64, 0, blk * 128:(blk + 1) * 128], rhs=vn[po:po + 64, :],
                          start=True, stop=True)
                        A("dve", "scalar_tensor_tensor", [S32B, embB, p4B], [S32B], out=S32, in0=S32, scalar=Emb[:, ci * 64 + 63:ci * 64 + 64],
                          in1=p4[:, 0:128], op0=ALU.mult, op1=ALU.add)
                        A("act", "activation", [S32B], [SbB], out=Sbf, in_=S32, func=AF.Identity)
                        A("act", "activation", [p3B], [self.sqB[0], stB], out=self.sq[0][po:po + 64, 0:128], in_=p3[po:po + 64, 0:128], func=AF.Square,
                          accum_out=st[po:po + 64, 0:1])
                        A("act", "activation", [stB, self.Bc], [stB], out=st[po:po + 64, 1:2], in_=st[po:po + 64, 0:1], func=AF.Ln, scale=1.0 / 128,
                          bias=vec[po:po + 64, V_EPS:V_EPS + 1])
                        A("act", "activation", [stB], [stB], out=st[po:po + 64, 2:3], in_=st[po:po + 64, 1:2], func=AF.Exp, scale=-0.5)
                        A("dve", "scalar_tensor_tensor", [p3B, stB, self.Bc], [onB], out=ont[po:po + 64, blk, :], in0=p3[po:po + 64, 0:128],
                          scalar=st[po:po + 64, 2:3], in1=self.onorm_bc[po:po + 64, :], op0=ALU.mult, op1=ALU.mult)
                    for blk in range(4):
                        A("pe", "transpose", [onB, self.Bc], [self.pbhB], self.pbh[:, blk * 128:(blk + 1) * 128], ont[:, blk, :], self.ident_b)
                    A("dve", "tensor_tensor", [self.pbhB, gsB], [mixB[h][g]], out=mixT[:, h, g * 512:(g + 1) * 512], in0=self.pbh[:, 0:512],
                      in1=gs[:], op=ALU.mult)
        self.S.barrier()
        self.mem_kv(layer, kTm, vm, kvB, w, wB)
        self.mem_attn(mq, mqB, kTm, vm, kvB, pT, pTB, rec, recB)
        self.S.barrier()
        ys = self.alloc_once("a_ys", [128, 8, SEQ], BF16, self.R0 + 16384 + 24576 + 8192 + 8192)
        ysB = [[Buf() for _ in range(4)] for _ in range(8)]

        def mixK(kc):
            if kc < 6:
                return mixT[:, kc, :], mixB[kc]
            return mq[:, kc - 6, :], mqB[kc - 6]
        self.out_proj(layer, mixK, ys, ysB, w, wB)

    def mixer_b(self, layer):
        self.S.barrier()
        R0 = self.R0
        o = R0
        hT = self.alloc_once("b_hT", [128, 8, SEQ], BF16, o); o += 32768
        qT = self.alloc_once("b_qT", [128, 6, SEQ], BF16, o); o += 24576
        kT = self.alloc_once("b_kT", [128, 6, SEQ], BF16, o); o += 24576
        vt = self.alloc_once("b_v", [128, 16, 768], BF16, o); o += 24576
        mq = self.alloc_once("b_mq", [128, 2, SEQ], BF16, o); o += 8192
        w = [self.alloc_once(f"b_w{i}", [128, 8, 256], BF16, o + 4096 * i) for i in range(2)]
        o += 8192
        self.mix_end = o
        assert o <= SB_END
        wB = [Buf(), Buf()]
        hB = [[Buf() for _ in range(4)] for _ in range(8)]
        qB = [[Buf() for _ in range(4)] for _ in range(6)]
        kB = [[Buf() for _ in range(4)] for _ in range(6)]
        vB = [Buf() for _ in range(16)]
        mqB = [[Buf() for _ in range(4)] for _ in range(2)]
        gpre = V_GAIN + (layer * 4 + 0) * 8
        for g in range(4):
            self.prenorm(g, gpre, lambda f, g=g: hT[:, f, g * 512:(g + 1) * 512], lambda f, g=g: hB[f][g])
        wd = self.w_in_b[0].rearrange("(k p) n -> p k n", p=128)
        wc = 0
        for piece in list(range(0, 6)) + [9]:
            k = wc % 2
            wc += 1
            self.dma(w[k][:], wd[:, :, piece * 256:(piece + 1) * 256], [], [wB[k]], eng="pool")
            for mm in range(2):
                tile = piece * 2 + mm
                for g in range(4):
                    ts = slice(g * 512, (g + 1) * 512)
                    pb, pbB = self.bank()
                    for kc in range(8):
                        self.A("pe", "matmul", [wB[k], hB[kc][g]], [pbB], pb[:, :], lhsT=w[k][:, kc, mm * 128:(mm + 1) * 128],
                               rhs=hT[:, kc, ts], start=(kc == 0), stop=(kc == 7))
                    if tile < 6:
                        self.copy_evac([pbB], [qB[tile][g]], qT[:, tile, ts], pb[:, :], scale=0.125)
                    elif tile < 12:
                        self.copy_evac([pbB], [kB[tile - 6][g]], kT[:, tile - 6, ts], pb[:, :])
                    else:
                        self.copy_evac([pbB], [mqB[tile - 18][g]], mq[:, tile - 18, ts], pb[:, :])
        for piece in range(6, 9):
            k = wc % 2
            wc += 1
            self.dma(w[k][:], wd[:, :, piece * 256:(piece + 1) * 256], [], [wB[k]], eng="pool")
            for tt in range(16):
                pb, pbB = self.bank()
                for kc in range(8):
                    self.A("pe", "matmul", [wB[k], hB[kc][tt // 4]], [pbB], pb[:, 0:256],
                           lhsT=hT[:, kc, tt * 128:(tt + 1) * 128], rhs=w[k][:, kc, :], start=(kc == 0), stop=(kc == 7))
                self.copy_evac([pbB], [vB[tt]], vt[:, tt, (piece - 6) * 256:(piece - 5) * 256], pb[:, 0:256])
        self.S.barrier()
        o = R0
        eT = [self.alloc_once(f"b_e{i}", [128, 512], F32, o + 2048 * i) for i in range(2)]; o += 4096
        nl = [self.alloc_once(f"b_nl{i}", [128, 512], BF16, o + 1024 * i) for i in range(2)]; o += 2048
        Sb = [self.alloc_once(f"b_S{i}", [128, 512], BF16, o + 1024 * i) for i in range(2)]; o += 2048
        aT = [self.alloc_once(f"b_a{i}", [128, 512], BF16, o + 1024 * i) for i in range(2)]; o += 2048
        pT = [self.alloc_once(f"b_p{i}", [128, 512], BF16, o + 1024 * i) for i in range(2)]; o += 2048
        rec = self.alloc_once("b_rec", [128, 512], F32, o); o += 2048
        kTm = self.alloc_once("b_kTm", [128, 2, NMEM], BF16, o); o += 1024
        vm = self.alloc_once("b_vm", [128, 2, 256], BF16, o); o += 1024
        eB = [Buf(), Buf()]; nlB = [Buf(), Buf()]; SbB = [Buf(), Buf()]; aB = [Buf(), Buf()]
        pTB = [Buf(), Buf()]; recB = Buf(); kvB = Buf()
        self.mem_kv(layer, kTm, vm, kvB, w, wB)
        self.mem_attn(mq, mqB, kTm, vm, kvB, pT, pTB, rec, recB)
        cnt = 0
        for h in range(12):
            tl = h // 2
            po = 64 * (h % 2)
            for g in range(4):
                ts = slice(g * 512, (g + 1) * 512)
                bmax = 4 * g + 3
                ob, obB = self.reserve()
                Sp, SpB = self.reserve()
                for b in range(bmax, -1, -1):
                    i = cnt % 2
                    cnt += 1
                    r = b - 4 * g
                    bs = slice(b * 128, (b + 1) * 128)
                    kb = kB[tl][b // 4]
                    zb, zbB = self.bank()
                    self.A("pe", "matmul", [kb, qB[tl][g]], [zbB], zb[:, :], lhsT=kT[po:po + 64, tl, bs],
                           rhs=qT[po:po + 64, tl, ts], start=True, stop=(r < 0))
                    if r >= 0:
                        self.A("pe", "matmul", [self.Bc], [zbB], zb[:, :], lhsT=self.ident_b,
                               rhs=self.maskbase[:, 384 - 128 * r:384 - 128 * r + 512], start=False, stop=True)
                    self.A("act", "activation", [zbB], [eB[i]], out=eT[i][:], in_=zb[:, :], func=AF.Exp)
                    self.A("act", "activation", [eB[i], self.Bc], [nlB[i]], out=nl[i][:], in_=eT[i][:], func=AF.Ln,
                           bias=self.vec[:, V_ONE:V_ONE + 1])
                    ab, abB = self.bank()
                    self.A("pe", "matmul", [kb, qB[tl][g]], [abB], ab[:, :], lhsT=kT[po:po + 64, tl, bs],
                           rhs=qT[po:po + 64, tl, ts], start=True, stop=False)
                    if r >= 0:
                        self.A("pe", "matmul", [self.Bc], [abB], ab[:, :], lhsT=self.ident_b,
                               rhs=self.maskbase[:, 384 - 128 * r:384 - 128 * r + 512], start=False, stop=False)
                    self.A("pe", "matmul", [self.Bc, nlB[i]], [abB], ab[:, :], lhsT=self.negU_b, rhs=nl[i][:],
                           start=False, stop=(b == bmax))
                    if b < bmax:
                        j = (cnt) % 2
                        self.A("pe", "matmul", [self.Bc, SbB[0]], [abB], ab[:, :], lhsT=self.negones_b, rhs=Sb[0][:],
                               start=False, stop=True)
                    self.A("act", "activation", [abB], [aB[i]], out=aT[i][:], in_=ab[:, :], func=AF.Exp)
                    self.A("pe", "matmul", [vB[b], aB[i]], [obB], ob[po:po + 64, :], lhsT=vt[:, b, h * 64:(h + 1) * 64],
                           rhs=aT[i][:], start=(b == bmax), stop=(b == 0))
                    if b > 0:
                        self.A("pe", "matmul", [self.Bc, nlB[i]], [SpB], Sp[:, :], lhsT=self.ident_b, rhs=nl[i][:],
                               start=(b == bmax), stop=(b == 1))
                        self.A("dve", "tensor_copy", [SpB], [SbB[0]], out=Sb[0][:], in_=Sp[:, :])
                self.A("dve", "tensor_copy", [obB, qB[tl][g]], [qB[tl][g]], out=qT[po:po + 64, tl, ts], in_=ob[po:po + 64, :])
                self.release(ob)
                self.release(Sp)
        self.S.barrier()
        ys = hT
        ysB = hB

        def mixK(kc):
            if kc < 6:
                return qT[:, kc, :], qB[kc]
            return mq[:, kc - 6, :], mqB[kc - 6]
        self.out_proj(layer, mixK, ys, ysB, w, wB)

    def mlp(self, layer):
        self.S.barrier()
        R0 = self.R0
        if not hasattr(self, "mlp_bufs"):
            o = R0
            hT = self.alloc("m_hT", [128, 8, 1024], BF16, at=o); o += 16384
            uT = self.alloc("m_uT", [128, 32, 1024], BF16, at=o); o += 65536
            wu = []
            for i in range(2):
                wu.append(self.alloc(f"m_wu{i}", [128, 8, 512], BF16, at=o)); o += 8192
            wd = []
            for i in range(2):
                wd.append(self.alloc(f"m_wd{i}", [128, 32, 128], BF16, at=o)); o += 8192
            rl = []
            for i in range(2):
                rl.append(self.alloc(f"m_rl{i}", [128, 512], BF16, at=o)); o += 1024
            self.mlp_bufs = dict(hT=hT, uT=uT, wu=wu, wd=wd, rl=rl,
                                 hTB=[[Buf() for _ in range(2)] for _ in range(8)],
                                 uTB=[[Buf() for _ in range(2)] for _ in range(32)],
                                 wuB=[Buf(), Buf()], wdB=[Buf(), Buf()], rlB=[Buf(), Buf()])
            self.mlp_end = o
        mb = self.mlp_bufs
        hT, uT = mb["hT"], mb["uT"]
        gpre = V_GAIN + (layer * 4 + 2) * 8
        gpost = V_GAIN + (layer * 4 + 3) * 8
        wu_d = self.w_up[layer].rearrange("(k p) n -> p k n", p=128)
        wd_d = self.w_down[layer].rearrange("(k p) n -> p k n", p=128)
        wcnt = 0
        rcnt = 0
        for half in range(2):
            for gg in range(2):
                g = half * 2 + gg
                self.prenorm(g, gpre, lambda f, gg=gg: hT[:, f, gg * 512:(gg + 1) * 512],
                             lambda f, gg=gg: mb["hTB"][f][gg])
            for jq in range(8):
                k = wcnt % 2
                wcnt += 1
                w, wB = mb["wu"][k], mb["wuB"][k]
                self.dma(w[:], wu_d[:, :, jq * 512:(jq + 1) * 512], [], [wB], eng="pool")
                for jj in range(4):
                    j = jq * 4 + jj
                    for gg in range(2):
                        pb, pbB = self.bank()
                        for kc in range(8):
                            self.A("pe", "matmul", [wB, mb["hTB"][kc][gg]], [pbB], pb[:, :],
                                   lhsT=w[:, kc, jj * 128:(jj + 1) * 128],
                                   rhs=hT[:, kc, gg * 512:(gg + 1) * 512], start=(kc == 0), stop=(kc == 7))
                        r = rcnt % 2
                        rcnt += 1
                        self.A("act", "activation", [pbB], [mb["rlB"][r]], out=mb["rl"][r][:], in_=pb[:, :],
                               func=AF.Relu)
                        self.A("dve", "tensor_tensor", [mb["rlB"][r]], [mb["uTB"][j][gg]],
                               out=uT[:, j, gg * 512:(gg + 1) * 512], in0=mb["rl"][r][:], in1=mb["rl"][r][:],
                               op=ALU.mult)
            ssb = [self.reserve(), self.reserve()]
            for fq in range(8):
                k = wcnt % 2
                wcnt += 1
                w, wB = mb["wd"][k], mb["wdB"][k]
                self.dma(w[:], wd_d[:, :, fq * 128:(fq + 1) * 128], [], [wB], eng="pool")
                for ff in range(1):
                    f = fq
                    for gg in range(2):
                        pb, pbB = self.bank()
                        for kc in range(32):
                            self.A("pe", "matmul", [wB, mb["uTB"][kc][gg]], [pbB], pb[:, :],
                                   lhsT=w[:, kc, ff * 128:(ff + 1) * 128],
                                   rhs=uT[:, kc, gg * 512:(gg + 1) * 512], start=(kc == 0), stop=(kc == 31))
                        ks = self.sq_rr
                        self.sq_rr = (ks + 1) % 3
                        self.A("dve", "tensor_copy", [pbB], [mb["hTB"][f][gg]],
                               out=hT[:, f, gg * 512:(gg + 1) * 512], in_=pb[:, :])
                        self.A("act", "activation", [mb["hTB"][f][gg]], [self.sqB[ks]], out=self.sq[ks][:],
                               in_=hT[:, f, gg * 512:(gg + 1) * 512], func=AF.Square)
                        self.A("pe", "matmul", [self.sqB[ks], self.Bc], [ssb[gg][1]], ssb[gg][0][:, :],
                               lhsT=self.ones_b, rhs=self.sq[ks][:], start=(f == 0), stop=(f == 7))
            for gg in range(2):
                g = half * 2 + gg
                self.postnorm_apply(g, gpost, lambda f, gg=gg: hT[:, f, gg * 512:(gg + 1) * 512],
                                    lambda f, gg=gg: mb["hTB"][f][gg], ssb[gg][0], ssb[gg][1])
            for gg in range(2):
                self.release(ssb[gg][0])


def make_inputs_common(inp):
    vec, cf, cb = host_consts(inp)
    com = {k: np.ascontiguousarray(np.asarray(inp[k], np.float32)) for k in
           ["w_in_a", "w_in_b", "w_mem_kv", "w_out", "w_up", "w_down"]}
    com["vecs"] = vec
    com["cf"] = cf
    com["cb"] = cb
    return com


def kernel(**inputs):
    n = 8
    x = np.asarray(inputs["x"], np.float32)
    mem = np.asarray(inputs["mem"], np.float32)
    nseq = x.shape[0] // n
    prog = Prog(nseq)
    nc = prog.build()
    com = make_inputs_common(inputs)
    in_maps = []
    for c in range(n):
        m = dict(com)
        m["x"] = np.ascontiguousarray(x[c * nseq:(c + 1) * nseq])
        m["mem"] = np.ascontiguousarray(mem[c * nseq:(c + 1) * nseq])
        in_maps.append(m)
    res = run_bass_kernel_spmd(nc, in_maps, core_ids=list(range(n)))
    return np.concatenate([r["out"] for r in res.results], axis=0)
```

```python
import numpy as np
import concourse.bass as bass
import concourse.mybir as mybir
from concourse.bass_utils import run_bass_kernel_spmd
from contextlib import ExitStack

F32 = mybir.dt.float32
BF16 = mybir.dt.bfloat16
ALU = mybir.AluOpType
AF = mybir.ActivationFunctionType
AX = mybir.AxisListType

EPOCH = 30000
N_DMA_SEMS = 24
SB_BASE = 16512
SB_END = 229344

D = 1024
SEQ = 2048
NMEM = 256
DFF = 4096
IN_A = 3340
IN_B = 2560
EPS = 1e-6
NEG = -30000.0


class Buf:
    __slots__ = ("name", "last_w", "readers")

    def __init__(self, name=""):
        self.name = name
        self.last_w = None
        self.readers = []


class Op:
    __slots__ = ("eng", "fn", "raw", "war", "signal", "sem", "val", "is_dma", "ndma")

    def __init__(self, eng, fn, is_dma=False, ndma=1):
        self.eng = eng
        self.fn = fn
        self.raw = []
        self.war = []
        self.signal = False
        self.sem = None
        self.val = None
        self.is_dma = is_dma
        self.ndma = ndma


class Sched:
    ENGS = ("pe", "act", "dve", "pool", "sp")

    def __init__(self, nc, es):
        self.nc = nc
        self.es = es
        self.ops = {e: [] for e in self.ENGS}
        self.n_ops = 0
        self.dma_rr = 0
        self.dma_last = [None] * N_DMA_SEMS
        self.dma_tot = [0] * N_DMA_SEMS

    def add(self, eng, fn, reads=(), writes=(), is_dma=False, ndma=1):
        op = Op(eng, fn, is_dma, ndma)
        raw = set()
        war = set()
        for b in reads:
            if b.last_w is not None:
                raw.add(b.last_w)
        for b in writes:
            if b.last_w is not None:
                war.add(b.last_w)
            for r in b.readers:
                war.add(r)
        for b in writes:
            b.last_w = op
            b.readers = []
        for b in reads:
            b.readers.append(op)
        if is_dma:
            k = self.dma_rr
            self.dma_rr = (k + 1) % N_DMA_SEMS
            if self.dma_last[k] is not None:
                raw.add(self.dma_last[k])
            self.dma_last[k] = op
            self.dma_tot[k] += 16 * ndma
            op.sem = k
            op.val = self.dma_tot[k]
            op.signal = True
        for x in raw:
            if x.is_dma or x.eng != eng or eng != "pe":
                op.raw.append(x)
        for x in war:
            if x in raw:
                continue
            if x.is_dma or x.eng != eng:
                op.war.append(x)
        for x in op.raw:
            x.signal = True
        for x in op.war:
            x.signal = True
        self.ops[eng].append(op)
        self.n_ops += 1
        return op

    def barrier(self):
        lasts = []
        for e in self.ENGS:
            for op in reversed(self.ops[e]):
                if op.fn is not None and not op.is_dma:
                    lasts.append(op)
                    break
        dmas = [op for op in self.dma_last if op is not None]
        for e in self.ENGS:
            op = Op(e, None)
            for x in lasts:
                if x.eng != e:
                    op.raw.append(x)
                    x.signal = True
            for x in dmas:
                op.raw.append(x)
            self.ops[e].append(op)

    def emit(self):
        nc = self.nc
        es = self.es
        dma_sems = [es.enter_context(nc.semaphore(f"dsem{k}")) for k in range(N_DMA_SEMS)]
        eng_sems = {}
        for e in self.ENGS:
            cnt = 0
            for op in self.ops[e]:
                if op.is_dma:
                    op.sem = dma_sems[op.sem]
                    continue
                if op.signal:
                    key = (e, cnt // EPOCH)
                    if key not in eng_sems:
                        eng_sems[key] = es.enter_context(nc.semaphore(f"s_{e}_{key[1]}"))
                    op.sem = eng_sems[key]
                    op.val = cnt % EPOCH + 1
                    cnt += 1

        def run(e, eng):
            seen = {}
            for op in self.ops[e]:
                need = {}
                for x in op.raw + op.war:
                    k = id(x.sem)
                    if k not in need or need[k][1] < x.val:
                        need[k] = (x.sem, x.val)
                for k, (sem, val) in need.items():
                    if seen.get(k, 0) >= val:
                        continue
                    seen[k] = val
                    eng.wait_ge(sem, val)
                if op.fn is None:
                    continue
                r = op.fn(eng)
                if op.is_dma:
                    if not isinstance(r, (list, tuple)):
                        r = [r]
                    assert len(r) == op.ndma
                    for ins in r:
                        ins.then_inc(op.sem, 16)
                elif op.signal:
                    r.then_inc(op.sem, 1)
            if e == "sp":
                for k in range(N_DMA_SEMS):
                    if self.dma_tot[k] > 0:
                        eng.wait_ge(dma_sems[k], self.dma_tot[k])

        with nc.Block() as block:
            @block.tensor
            def _(eng):
                run("pe", eng)

            @block.scalar
            def _(eng):
                run("act", eng)

            @block.vector
            def _(eng):
                run("dve", eng)

            @block.gpsimd
            def _(eng):
                run("pool", eng)

            @block.sync
            def _(eng):
                run("sp", eng)


V_EPS = 0
V_ONE = 1
V_GAIN = 2
V_MEMG = V_GAIN + 64
V_CONV = V_MEMG + 8
V_ALOG = V_CONV + 72
V_DTB = V_ALOG + 1
NV = V_DTB + 1


CB_SEL = 1408
CB_MY = CB_SEL + 768
CB_MP = CB_MY + 512
CB_MI = CB_MP + 512
CB_I4 = CB_MI + 512
CB_N = CB_I4 + 512


def host_consts(inp):
    vec = np.zeros((128, NV), np.float32)
    vec[:, V_EPS] = EPS
    vec[:, V_ONE] = 1.0
    for layer in range(2):
        for j, nm in enumerate(["norm_pre_mix", "norm_post_mix", "norm_pre_mlp", "norm_post_mlp"]):
            g = np.asarray(inp[nm], np.float32)[layer]
            vec[:, V_GAIN + (layer * 4 + j) * 8:V_GAIN + (layer * 4 + j) * 8 + 8] = g.reshape(8, 128).T
    vec[:, V_MEMG:V_MEMG + 8] = np.asarray(inp["mem_norm"], np.float32).reshape(8, 128).T
    cw = np.asarray(inp["conv_w_a"], np.float32)[0]
    vec[:, V_CONV:V_CONV + 72] = cw.reshape(4, 18, 128).transpose(2, 1, 0).reshape(128, 72)
    vec[0:6, V_ALOG] = np.asarray(inp["a_log_a"], np.float32)[0]
    vec[0:6, V_DTB] = np.asarray(inp["dt_bias_a"], np.float32)[0]
    cf = np.zeros((128, 384), np.float32)
    cf[:, 256:384] = 1.0
    cf[:, 0:128] = np.eye(128, dtype=np.float32)
    cf[:, 128:256] = np.tile(np.asarray(inp["onorm_a"], np.float32)[0][None, :], (128, 1))
    p = np.arange(128)[:, None]
    c = np.arange(128)[None, :]
    cb = np.zeros((128, CB_N), np.float32)
    cb[:, 0:128] = np.eye(128)
    cb[:, 128:256] = 1.0
    cb[:, 256:384] = -1.0 * (p >= c)
    cb[:, 384:512] = -1.0
    cc = np.arange(896)[None, :]
    cb[:, 512:512 + 896] = np.where(cc - 384 <= p, NEG, 0.0)
    for h in range(6):
        cb[h, CB_SEL + h * 128:CB_SEL + (h + 1) * 128] = 1.0
    same = (p // 64) == (c // 64)
    mY = -1.0 * ((p > c) & same)
    mP = -1.0 * ((c > p) & same)
    mI = 1.0 * ((c >= p) & same)
    cb[:, CB_MY:CB_MY + 512] = np.tile(mY, (1, 4))
    cb[:, CB_MP:CB_MP + 512] = np.tile(mP, (1, 4))
    cb[:, CB_MI:CB_MI + 512] = np.tile(mI, (1, 4))
    cb[:, CB_I4:CB_I4 + 512] = np.tile(np.eye(128), (1, 4))
    return vec, cf, cb


class Prog:
    def __init__(self, nseq, stages=("mem", "mix0", "l0", "mix1", "l1"), dbg=None):
        self.nseq = nseq
        self.stages = stages
        self.dbg = dbg
        self.nc = nc = bass.Bass("TRN2", target_bir_lowering=False)
        self.es = ExitStack()
        self.S = Sched(nc, self.es)
        di = lambda n, s: nc.dram_tensor(n, list(s), F32, kind="ExternalInput").ap()
        self.x = di("x", [nseq, SEQ, D])
        self.mem = di("mem", [nseq, NMEM, D])
        self.w_in_a = di("w_in_a", [1, D, IN_A])
        self.w_in_b = di("w_in_b", [1, D, IN_B])
        self.w_mem_kv = di("w_mem_kv", [2, D, 512])
        self.w_out = di("w_out", [2, D, D])
        self.w_up = di("w_up", [2, D, DFF])
        self.w_down = di("w_down", [2, DFF, D])
        self.vecs_d = di("vecs", [128, NV])
        self.cf_d = di("cf", [128, 384])
        self.cb_d = di("cb", [128, CB_N])
        self.out = nc.dram_tensor("out", [nseq, SEQ, D], F32, kind="ExternalOutput").ap()
        if dbg is not None:
            self.dbg_out = nc.dram_tensor("dbg", list(dbg), F32, kind="ExternalOutput").ap()
        self.cur = SB_BASE
        self.reserved = set()
        self.rr = 0
        self.evac_rr = 0

    def alloc(self, name, shape, dtype, at=None):
        esz = 4 if dtype == F32 else 2
        n = 1
        for s in shape[1:]:
            n *= s
        nbytes = (n * esz + 31) // 32 * 32
        if at is None:
            at = self.cur
            self.cur += nbytes
        assert at % 32 == 0 and at + nbytes <= SB_END, (name, at, nbytes)
        t = self.nc.alloc_sbuf_tensor_at(name, list(shape), dtype, offset=at)
        return t

    def A(self, eng, name, reads, writes, *args, **kw):
        return self.S.add(eng, lambda e: getattr(e, name)(*args, **kw), reads, writes)

    def dma(self, out, in_, reads, writes, eng="sp"):
        return self.S.add(eng, lambda e: e.dma_start(out=out, in_=in_), reads, writes, is_dma=True)

    def bank(self):
        while True:
            k = self.rr
            self.rr = (k + 1) % 7
            if k not in self.reserved:
                return self.banks[k], self.bankB[k]

    def reserve(self):
        pb, pbB = self.bank()
        self.reserved.add(self.banks.index(pb))
        return pb, pbB

    def release(self, pb):
        self.reserved.discard(self.banks.index(pb))

    def evac_eng(self):
        self.evac_rr ^= 1
        return "act" if self.evac_rr else "dve"

    def build(self):
        self.setup()
        for s in range(self.nseq):
            self.load_x(s)
            if "mem" in self.stages:
                self.mem_prep(s)
            for layer in range(2):
                if f"mix{layer}" in self.stages:
                    if layer == 0:
                        self.mixer_a(layer)
                    else:
                        self.mixer_b(layer)
                if f"l{layer}" in self.stages:
                    self.mlp(layer)
            self.S.barrier()
            self.store_x(s)
        self.S.emit()
        return self.nc

    def setup(self):
        nc = self.nc
        self.banks = [self.es.enter_context(nc.psum_tensor(f"pb{k}", [128, 512], F32)) for k in range(7)]
        self.bankB = [Buf(f"pb{k}") for k in range(7)]
        self.pbh = self.es.enter_context(nc.psum_tensor("pbh", [128, 1024], BF16))
        self.pbhB = Buf("pbh")
        self.XT = self.alloc("XT", [128, 8, SEQ], F32)
        self.XB = [[Buf(f"X{f}_{g}") for g in range(4)] for f in range(8)]
        self.vec = self.alloc("vec", [128, NV], F32)
        self.cf = self.alloc("cf", [128, 384], F32)
        self.cb = self.alloc("cb", [128, CB_SEL], BF16)
        self.Bc = Buf("consts")
        self.ident_f = self.cf[:, 0:128]
        self.onorm_bc = self.cf[:, 128:256]
        self.ident_b = self.cb[:, 0:128]
        self.ones_b = self.cb[:, 128:256]
        self.negU_b = self.cb[:, 256:384]
        self.negones_b = self.cb[:, 384:512]
        self.maskbase = self.cb[:, 512:512 + 896]
        self.memnT = self.alloc("memnT", [128, 8, NMEM], BF16)
        self.BmemnT = Buf("memnT")
        self.sq = [self.alloc(f"sq{i}", [128, 512], BF16) for i in range(3)]
        self.sqB = [Buf(f"sq{i}") for i in range(3)]
        self.sq_rr = 0
        self.lnb = self.alloc("lnb", [128, 512], F32)
        self.BlnB = Buf("lnb")
        self.rstd_at = self.cur
        self.rstd = [self.alloc(f"rstd{i}", [128, 512], F32) for i in range(2)]
        self.rstdB = [Buf(f"rstd{i}") for i in range(2)]
        self.rstd_rr = 0
        self.tmpf_at = self.cur
        self.tmpf = [self.alloc(f"tmpf{i}", [128, 512], F32) for i in range(2)]
        self.tmpfB = [Buf(f"tmpf{i}") for i in range(2)]
        self.tmpf_rr = 0
        self.R0 = self.cur
        self.xin = [self.nc.alloc_sbuf_tensor_at("xin0", [128, D], F32, offset=self.tmpf_at),
                    self.nc.alloc_sbuf_tensor_at("xin1", [128, D], F32, offset=self.rstd_at)]
        self.xinB = [self.tmpfB, self.rstdB]
        self.dma(self.vec[:], self.vecs_d[:, :], [], [self.Bc])
        self.dma(self.cf[:], self.cf_d[:, :], [], [self.Bc])
        self.dma(self.cb[:], self.cb_d[:, 0:CB_SEL], [], [self.Bc], eng="pool")

    def load_x(self, s):
        for tt in range(16):
            xi, xiB = self.xin[tt % 2], self.xinB[tt % 2]
            self.dma(xi[:], self.x[s, tt * 128:(tt + 1) * 128, :], [], xiB)
            g = tt // 4
            for half in range(2):
                pb, pbB = self.bank()
                for j in range(4):
                    f = half * 4 + j
                    self.A("pe", "transpose", xiB + [self.Bc], [pbB], pb[:, j * 128:(j + 1) * 128],
                           xi[:, f * 128:(f + 1) * 128], self.ident_f)
                dst = self.XT[:, half * 4:half * 4 + 4, tt * 128:(tt + 1) * 128]
                src = pb[:, :].rearrange("p (j t) -> p j t", j=4)
                wr = [self.XB[half * 4 + j][g] for j in range(4)]
                self.A("dve", "tensor_copy", [pbB], wr, out=dst, in_=src)

    def store_x(self, s):
        for tt in range(16):
            xi, xiB = self.xin[tt % 2], self.xinB[tt % 2]
            g = tt // 4
            for half in range(2):
                pb, pbB = self.bank()
                for j in range(4):
                    f = half * 4 + j
                    self.A("pe", "transpose", [self.XB[f][g], self.Bc], [pbB], pb[:, j * 128:(j + 1) * 128],
                           self.XT[:, f, tt * 128:(tt + 1) * 128], self.ident_f)
                dst = xi[:, half * 512:(half + 1) * 512]
                if self.evac_eng() == "act":
                    self.A("act", "activation", [pbB], xiB, out=dst, in_=pb[:, :], func=AF.Identity)
                else:
                    self.A("dve", "tensor_copy", [pbB], xiB, out=dst, in_=pb[:, :])
            self.dma(self.out[s, tt * 128:(tt + 1) * 128, :], xi[:], xiB, [])

    def rstd_from_ss(self, pb, pbB):
        k = self.rstd_rr
        self.rstd_rr ^= 1
        r, rB = self.rstd[k], self.rstdB[k]
        self.A("act", "activation", [pbB, self.Bc], [self.BlnB], out=self.lnb[:], in_=pb[:, :], func=AF.Ln,
               scale=1.0 / D, bias=self.vec[:, V_EPS:V_EPS + 1])
        self.A("act", "activation", [self.BlnB], [rB], out=r[:], in_=self.lnb[:], func=AF.Exp, scale=-0.5)
        return r, rB

    def prenorm(self, g, gcol, dst, dstB):
        ts = slice(g * 512, (g + 1) * 512)
        pb, pbB = self.bank()
        for f in range(8):
            k = self.sq_rr
            self.sq_rr = (k + 1) % 3
            self.A("act", "activation", [self.XB[f][g]], [self.sqB[k]], out=self.sq[k][:], in_=self.XT[:, f, ts],
                   func=AF.Square)
            self.A("pe", "matmul", [self.sqB[k], self.Bc], [pbB], pb[:, :], lhsT=self.ones_b, rhs=self.sq[k][:],
                   start=(f == 0), stop=(f == 7))
        r, rB = self.rstd_from_ss(pb, pbB)
        for f in range(8):
            self.A("dve", "scalar_tensor_tensor", [self.XB[f][g], rB, self.Bc], [dstB(f)], out=dst(f),
                   in0=self.XT[:, f, ts], scalar=self.vec[:, gcol + f:gcol + f + 1], in1=r[:],
                   op0=ALU.mult, op1=ALU.mult)

    def postnorm_apply(self, g, gcol, ys, ysB, ssb, ssbB):
        ts = slice(g * 512, (g + 1) * 512)
        r, rB = self.rstd_from_ss(ssb, ssbB)
        for f in range(8):
            k = self.tmpf_rr
            self.tmpf_rr ^= 1
            t, tB = self.tmpf[k], self.tmpfB[k]
            self.A("dve", "tensor_tensor", [ysB(f), rB], [tB], out=t[:], in0=ys(f), in1=r[:], op=ALU.mult)
            self.A("dve", "scalar_tensor_tensor", [tB, self.XB[f][g], self.Bc], [self.XB[f][g]],
                   out=self.XT[:, f, ts], in0=t[:], scalar=self.vec[:, gcol + f:gcol + f + 1],
                   in1=self.XT[:, f, ts], op0=ALU.mult, op1=ALU.add)


    def mem_prep(self, s):
        self.S.barrier()
        R0 = self.R0
        memT = self.alloc_once("mp_memT", [128, 8, NMEM], F32, R0)
        mB = Buf()
        for mt in range(2):
            xi, xiB = self.xin[mt], self.xinB[mt]
            self.dma(xi[:], self.mem[s, mt * 128:(mt + 1) * 128, :], [], xiB)
            for half in range(2):
                pb, pbB = self.bank()
                for j in range(4):
                    f = half * 4 + j
                    self.A("pe", "transpose", xiB + [self.Bc], [pbB], pb[:, j * 128:(j + 1) * 128],
                           xi[:, f * 128:(f + 1) * 128], self.ident_f)
                self.A("dve", "tensor_copy", [pbB], [mB], out=memT[:, half * 4:half * 4 + 4, mt * 128:(mt + 1) * 128],
                       in_=pb[:, :].rearrange("p (j t) -> p j t", j=4))
        self.S.barrier()
        pb, pbB = self.bank()
        for f in range(8):
            k = self.sq_rr
            self.sq_rr = (k + 1) % 3
            self.A("act", "activation", [mB], [self.sqB[k]], out=self.sq[k][:, 0:NMEM], in_=memT[:, f, :], func=AF.Square)
            self.A("pe", "matmul", [self.sqB[k], self.Bc], [pbB], pb[:, 0:NMEM], lhsT=self.ones_b, rhs=self.sq[k][:, 0:NMEM],
                   start=(f == 0), stop=(f == 7))
        self.A("act", "activation", [pbB, self.Bc], [self.BlnB], out=self.lnb[:, 0:NMEM], in_=pb[:, 0:NMEM], func=AF.Ln,
               scale=1.0 / D, bias=self.vec[:, V_EPS:V_EPS + 1])
        r, rB = self.rstd[0], self.rstdB[0]
        self.A("act", "activation", [self.BlnB], [rB], out=r[:, 0:NMEM], in_=self.lnb[:, 0:NMEM], func=AF.Exp, scale=-0.5)
        for f in range(8):
            self.A("dve", "scalar_tensor_tensor", [mB, rB, self.Bc], [self.BmemnT], out=self.memnT[:, f, :],
                   in0=memT[:, f, :], scalar=self.vec[:, V_MEMG + f:V_MEMG + f + 1], in1=r[:, 0:NMEM],
                   op0=ALU.mult, op1=ALU.mult)

    def alloc_once(self, name, shape, dtype, at):
        if not hasattr(self, "_once"):
            self._once = {}
        if name not in self._once:
            self._once[name] = self.alloc(name, shape, dtype, at=at)
        return self._once[name]

    def copy_evac(self, reads, writes, out, in_, scale=None):
        if self.evac_eng() == "act":
            if scale is None:
                self.A("act", "activation", reads, writes, out=out, in_=in_, func=AF.Identity)
            else:
                self.A("act", "activation", reads, writes, out=out, in_=in_, func=AF.Identity, scale=scale)
        else:
            if scale is None:
                self.A("dve", "tensor_copy", reads, writes, out=out, in_=in_)
            else:
                self.A("dve", "tensor_scalar", reads, writes, out=out, in0=in_, scalar1=scale, scalar2=None, op0=ALU.mult)

    def mem_kv(self, layer, kTm, vm, kvB, w, wB):
        wd = self.w_mem_kv[layer].rearrange("(k p) n -> p k n", p=128)
        self.dma(w[0][:], wd[:, :, 0:256], [], [wB[0]], eng="pool")
        self.dma(w[1][:], wd[:, :, 256:512], [], [wB[1]], eng="pool")
        for x in range(4):
            pb, pbB = self.bank()
            po = 64 * (x % 2)
            for kc in range(8):
                self.A("pe", "matmul", [wB[0], self.BmemnT], [pbB], pb[po:po + 64, 0:NMEM],
                       lhsT=w[0][:, kc, x * 64:(x + 1) * 64], rhs=self.memnT[:, kc, :], start=(kc == 0), stop=(kc == 7))
            self.copy_evac([pbB], [kvB], kTm[po:po + 64, x // 2, :], pb[po:po + 64, 0:NMEM])
        for mt in range(2):
            pb, pbB = self.bank()
            for kc in range(8):
                self.A("pe", "matmul", [wB[1], self.BmemnT], [pbB], pb[:, 0:256],
                       lhsT=self.memnT[:, kc, mt * 128:(mt + 1) * 128], rhs=w[1][:, kc, :], start=(kc == 0), stop=(kc == 7))
            self.copy_evac([pbB], [kvB], vm[:, mt, :], pb[:, 0:256])

    def mem_attn(self, memqT, mqB, kTm, vm, kvB, pT, pTB, rec, recB):
        for g in range(4):
            ts = slice(g * 512, (g + 1) * 512)
            for x in range(4):
                po = 64 * (x % 2)
                tl = x // 2
                ob, obB = self.reserve()
                db, dbB = self.reserve()
                for mt in range(2):
                    zb, zbB = self.bank()
                    self.A("pe", "matmul", [kvB, mqB[tl][g]], [zbB], zb[:, :],
                           lhsT=kTm[po:po + 64, tl, mt * 128:(mt + 1) * 128], rhs=memqT[po:po + 64, tl, ts],
                           start=True, stop=True)
                    self.A("act", "activation", [zbB], [pTB[mt]], out=pT[mt][:], in_=zb[:, :], func=AF.Exp, scale=0.125)
                    self.A("pe", "matmul", [kvB, pTB[mt]], [obB], ob[po:po + 64, :],
                           lhsT=vm[:, mt, x * 64:(x + 1) * 64], rhs=pT[mt][:], start=(mt == 0), stop=(mt == 1))
                    self.A("pe", "matmul", [self.Bc, pTB[mt]], [dbB], db[po:po + 64, :],
                           lhsT=self.ones_b[:, 0:64], rhs=pT[mt][:], start=(mt == 0), stop=(mt == 1))
                self.A("dve", "reciprocal", [dbB], [recB], out=rec[po:po + 64, :], in_=db[po:po + 64, :])
                self.A("dve", "tensor_tensor", [obB, recB, mqB[tl][g]], [mqB[tl][g]], out=memqT[po:po + 64, tl, ts],
                       in0=ob[po:po + 64, :], in1=rec[po:po + 64, :], op=ALU.mult)
                self.release(ob)
                self.release(db)

    def out_proj(self, layer, mixK, ys, ysB, w, wB):
        gpost = V_GAIN + (layer * 4 + 1) * 8
        wd = self.w_out[layer].rearrange("(k p) n -> p k n", p=128)
        ssb = [self.reserve() for _ in range(4)]
        for fq in range(4):
            k = fq % 2
            self.dma(w[k][:], wd[:, :, fq * 256:(fq + 1) * 256], [], [wB[k]], eng="pool")
            for ff in range(2):
                f = fq * 2 + ff
                for g in range(4):
                    ts = slice(g * 512, (g + 1) * 512)
                    pb, pbB = self.bank()
                    for kc in range(8):
                        ap, bufs = mixK(kc)
                        self.A("pe", "matmul", [wB[k], bufs[g]], [pbB], pb[:, :],
                               lhsT=w[k][:, kc, ff * 128:(ff + 1) * 128], rhs=ap[:, ts], start=(kc == 0), stop=(kc == 7))
                    self.A("dve", "tensor_copy", [pbB], [ysB[f][g]], out=ys[:, f, ts], in_=pb[:, :])
                    ks = self.sq_rr
                    self.sq_rr = (ks + 1) % 3
                    self.A("act", "activation", [ysB[f][g]], [self.sqB[ks]], out=self.sq[ks][:], in_=ys[:, f, ts], func=AF.Square)
                    self.A("pe", "matmul", [self.sqB[ks], self.Bc], [ssb[g][1]], ssb[g][0][:, :],
                           lhsT=self.ones_b, rhs=self.sq[ks][:], start=(f == 0), stop=(f == 7))
        for g in range(4):
            self.postnorm_apply(g, gpost, lambda f, g=g: ys[:, f, g * 512:(g + 1) * 512], lambda f, g=g: ysB[f][g],
                                ssb[g][0], ssb[g][1])
        for g in range(4):
            self.release(ssb[g][0])


    def mixer_a(self, layer):
        self.S.barrier()
        A = self.A
        R0 = self.R0
        o = R0
        hT = self.alloc_once("a_hT", [128, 8, 1024], BF16, o); o += 16384
        mixT = self.alloc_once("a_mix", [128, 6, SEQ], BF16, o); o += 24576
        mq = self.alloc_once("a_mq", [128, 2, SEQ], BF16, o); o += 8192
        w = [self.alloc_once(f"a_w{i}", [128, 8, 256], BF16, o + 4096 * i) for i in range(2)]; o += 8192
        rows = {n: self.alloc_once("a_r" + n, [6, 1024], BF16, o + 2048 * i) for i, n in enumerate(["Em", "Ep", "KD", "BE", "be"])}
        o += 10240
        rt_at = o
        rtmp = [self.alloc_once(f"a_rt{i}", [6, 1024], F32, o + 4096 * i) for i in range(3)]; o += 12288
        raw = self.alloc_once("a_raw", [128, 3, 1028], BF16, o); o += 6176
        hist = self.alloc_once("a_hist", [128, 18, 4], BF16, o); o += 160
        dg = self.alloc_once("a_dg", [128, 12, 128], BF16, o); o += 3072
        c_at = o
        cqkv = self.alloc_once("a_c", [128, 3, 512], BF16, o); o += 3072
        scl = self.alloc_once("a_s", [128, 5, 512], BF16, o); o += 5120
        tok = self.alloc_once("a_tok", [128, 3, 512], BF16, o); o += 3072
        UWb = self.alloc_once("a_uw", [128, 4, 256], BF16, o); o += 2048
        MT = self.alloc_once("a_mt", [128, 8, 128], BF16, c_at)
        QpT = self.alloc_once("a_qp", [128, 512], BF16, c_at + 2048)
        self.yp_at = o
        YP = [self.alloc_once(f"a_yp{i}", [128, 4, 512], BF16, o + 4096 * i) for i in range(2)]; o += 8192
        itT = self.alloc_once("a_it", [128, 512], BF16, o); o += 1024
        Emb = self.alloc_once("a_emb", [128, 512], F32, o); o += 2048
        gs = self.alloc_once("a_gs", [128, 512], BF16, o); o += 1024
        Sbfa = self.alloc_once("a_Sb", [128, 12, 128], BF16, o); o += 3072
        spar = [0] * 6
        ont = self.alloc_once("a_on", [128, 4, 128], BF16, o); o += 1024
        st = self.alloc_once("a_st", [128, 8], F32, o); o += 32
        nA = self.alloc_once("a_nA", [6, 8], F32, o); o += 32
        assert o <= SB_END, o
        o2 = self.yp_at
        kTm = self.alloc_once("a_kTm", [128, 2, NMEM], BF16, o2); o2 += 1024
        vm = self.alloc_once("a_vm", [128, 2, 256], BF16, o2); o2 += 1024
        pT = [self.alloc_once(f"a_p{i}", [128, 512], BF16, o2 + 1024 * i) for i in range(2)]; o2 += 2048
        rec = self.alloc_once("a_rec", [128, 512], F32, o2); o2 += 2048
        B = lambda: Buf()
        wB = [B(), B()]; hB = [[B() for _ in range(2)] for _ in range(8)]
        mixB = [[B() for _ in range(4)] for _ in range(6)]; mqB = [[B() for _ in range(4)] for _ in range(2)]
        rowB = B(); rtB = [B(), B(), B()]; rawB = B(); histB = B(); dgB = B(); cB = B(); sclB = B(); tokB = B()
        ypB = [B(), B()]; itB = B(); embB = B(); gsB = B(); S32B = B(); SbB = B(); rhsB = B(); vnB = B(); onB = B()
        stB = B(); nAB = B(); kvB = B(); pTB = [B(), B()]; recB = B()
        cbt = self.alloc_once("a_cb2", [128, CB_N - CB_SEL], BF16, o)
        o += (CB_N - CB_SEL) * 2
        assert o <= SB_END, o
        self.dma(cbt[:], self.cb_d[:, CB_SEL:CB_N], [], [self.Bc], eng="pool")
        sel = lambda h: cbt[0:6, h * 128:(h + 1) * 128]
        mY = cbt[:, CB_MY - CB_SEL:CB_MY - CB_SEL + 512]; mP = cbt[:, CB_MP - CB_SEL:CB_MP - CB_SEL + 512]
        mI = cbt[:, CB_MI - CB_SEL:CB_MI - CB_SEL + 512]
        I4 = cbt[:, CB_I4 - CB_SEL:CB_I4 - CB_SEL + 512]
        gpre = V_GAIN + (layer * 4 + 0) * 8
        wd = self.w_in_a[0].rearrange("(k p) n -> p k n", p=128)
        W = 768
        vec = self.vec
        A("act", "activation", [self.Bc], [nAB], out=nA[:, 0:1], in_=vec[0:6, V_ALOG:V_ALOG + 1], func=AF.Exp)
        A("pool", "memset", [], [histB], hist[:], 0.0)
        wc = 0
        for half in range(2):
            for gg in range(2):
                self.prenorm(half * 2 + gg, gpre, lambda f, gg=gg: hT[:, f, gg * 512:(gg + 1) * 512], lambda f, gg=gg: hB[f][gg])
            k = wc % 2; wc += 1
            self.dma(w[k][:], wd[:, :, 3084:3340], [], [wB[k]], eng="pool")
            for mm in range(2):
                for gg in range(2):
                    g = half * 2 + gg
                    pb, pbB = self.bank()
                    for kc in range(8):
                        A("pe", "matmul", [wB[k], hB[kc][gg]], [pbB], pb[:, :], lhsT=w[k][:, kc, mm * 128:(mm + 1) * 128],
                          rhs=hT[:, kc, gg * 512:(gg + 1) * 512], start=(kc == 0), stop=(kc == 7))
                    self.copy_evac([pbB], [mqB[mm][g]], mq[:, mm, g * 512:(g + 1) * 512], pb[:, :])
            k = wc % 2; wc += 1
            self.dma(w[k][:, :, 0:12], wd[:, :, 3072:3084], [], [wB[k]], eng="pool")
            for gg in range(2):
                gsl = slice(gg * 512, (gg + 1) * 512)
                pb, pbB = self.bank()
                pa, paB = self.bank()
                for kc in range(8):
                    A("pe", "matmul", [wB[k], hB[kc][gg]], [pbB], pb[0:6, :], lhsT=w[k][:, kc, 0:6], rhs=hT[:, kc, gsl],
                      start=(kc == 0), stop=(kc == 7))
                for kc in range(8):
                    A("pe", "matmul", [wB[k], hB[kc][gg]], [paB], pa[0:6, :], lhsT=w[k][:, kc, 6:12], rhs=hT[:, kc, gsl],
                      start=(kc == 0), stop=(kc == 7))
                A("act", "activation", [pbB], [rtB[0]], out=rtmp[0][:, gsl], in_=pb[0:6, :], func=AF.Exp, scale=-1.0)
                A("dve", "tensor_scalar", [rtB[0]], [rtB[0]], out=rtmp[0][:, gsl], in0=rtmp[0][:, gsl], scalar1=1.0, scalar2=None, op0=ALU.add)
                A("dve", "reciprocal", [rtB[0]], [rtB[0]], out=rtmp[0][:, gsl], in_=rtmp[0][:, gsl])
                A("act", "activation", [paB, self.Bc], [rtB[1]], out=rtmp[1][:, gsl], in_=pa[0:6, :], func=AF.Exp,
                  bias=vec[0:6, V_DTB:V_DTB + 1])
                A("act", "activation", [rtB[1], self.Bc], [rtB[1]], out=rtmp[1][:, gsl], in_=rtmp[1][:, gsl], func=AF.Ln,
                  bias=vec[0:6, V_ONE:V_ONE + 1])
                A("dve", "tensor_scalar", [rtB[1], nAB], [rtB[1]], out=rtmp[1][:, gsl], in0=rtmp[1][:, gsl], scalar1=nA[:, 0:1],
                  scalar2=-1.0, op0=ALU.mult, op1=ALU.mult)
            for c in range(16):
                cs = slice(c * 64, (c + 1) * 64)
                A("dve", "tensor_tensor_scan", [rtB[1], self.Bc], [rtB[2]], out=rtmp[2][:, cs], data0=self.cf[0:6, 256:320],
                  data1=rtmp[1][:, cs], initial=0.0, op0=ALU.mult, op1=ALU.add)
            gc3 = rtmp[2][:, :].rearrange("p (c t) -> p c t", t=64)
            A("act", "activation", [rtB[2]], [rowB], out=rows["Em"][:], in_=rtmp[2][:], func=AF.Exp)
            A("act", "activation", [rtB[2]], [rowB], out=rows["Ep"][:], in_=rtmp[2][:], func=AF.Exp, scale=-1.0)
            A("dve", "tensor_copy", [rtB[0]], [rowB], out=rows["be"][:], in_=rtmp[0][:])
            A("act", "activation", [rtB[2]], [rtB[1]], out=rtmp[1][:], in_=rtmp[2][:], func=AF.Exp)
            A("dve", "tensor_tensor", [rtB[0], rtB[1]], [rowB], out=rows["BE"][:], in0=rtmp[0][:], in1=rtmp[1][:], op=ALU.mult)
            A("dve", "tensor_tensor", [rtB[2]], [rtB[1]], out=rtmp[1][:, :].rearrange("p (c t) -> p c t", t=64),
              in0=gc3[:, :, 63:64].to_broadcast([6, 16, 64]), in1=gc3, op=ALU.subtract)
            A("act", "activation", [rtB[1]], [rowB], out=rows["KD"][:], in_=rtmp[1][:], func=AF.Exp)
            for h in range(6):
                for ti, tile in enumerate([h, 6 + h, 12 + h]):
                    for tap in range(4):
                        col = V_CONV + tile * 4 + tap
                        A("dve", "tensor_scalar", [self.Bc], [dgB], out=dg[:, ti * 4 + tap, :], in0=self.ident_b,
                          scalar1=vec[:, col:col + 1], scalar2=None, op0=ALU.mult)
                for ti in range(3):
                    A("dve", "tensor_copy", [histB], [rawB], out=raw[:, ti, 0:3], in_=hist[:, h * 3 + ti, 0:3])
                for ti, tile in enumerate([h, 6 + h, 12 + h]):
                    k = wc % 2; wc += 1
                    self.dma(w[k][:, :, 0:128], wd[:, :, tile * 128:(tile + 1) * 128], [], [wB[k]], eng="pool")
                    for gg in range(2):
                        pb, pbB = self.bank()
                        for kc in range(8):
                            A("pe", "matmul", [wB[k], hB[kc][gg]], [pbB], pb[:, :], lhsT=w[k][:, kc, 0:128],
                              rhs=hT[:, kc, gg * 512:(gg + 1) * 512], start=(kc == 0), stop=(kc == 7))
                        self.copy_evac([pbB], [rawB], raw[:, ti, 3 + gg * 512:3 + (gg + 1) * 512], pb[:, :])
                for ti in range(3):
                    A("dve", "tensor_copy", [rawB], [histB], out=hist[:, h * 3 + ti, 0:3], in_=raw[:, ti, 1024:1027])
                for gg in range(2):
                    g = half * 2 + gg
                    gsl = slice(gg * 512, (gg + 1) * 512)
                    for ti in range(3):
                        pb, pbB = self.bank()
                        for tap in range(4):
                            A("pe", "matmul", [dgB, rawB], [pbB], pb[:, :], lhsT=dg[:, ti * 4 + tap, :],
                              rhs=raw[:, ti, gg * 512 + tap:gg * 512 + tap + 512], start=(tap == 0), stop=(tap == 3))
                        A("act", "activation", [pbB], [cB], out=cqkv[:, ti, :], in_=pb[:, :], func=AF.Silu)
                    for ti in range(2):
                        ks = self.sq_rr
                        self.sq_rr = (ks + 1) % 3
                        A("dve", "tensor_tensor", [cB], [self.sqB[ks]], out=self.sq[ks][:], in0=cqkv[:, ti, :], in1=cqkv[:, ti, :], op=ALU.mult)
                        pb, pbB = self.bank()
                        A("pe", "matmul", [self.sqB[ks], self.Bc], [pbB], pb[:, :], lhsT=self.ones_b, rhs=self.sq[ks][:], start=True, stop=True)
                        A("act", "activation", [pbB, self.Bc], [self.BlnB], out=self.lnb[:], in_=pb[:, :], func=AF.Ln,
                          bias=vec[:, V_EPS:V_EPS + 1])
                        A("act", "activation", [self.BlnB], [self.rstdB[ti]], out=self.rstd[ti][:], in_=self.lnb[:], func=AF.Exp, scale=-0.5)
                    def bc(name):
                        pb, pbB = self.bank()
                        A("pe", "matmul", [rowB, self.Bc], [pbB], pb[:, :], lhsT=sel(h), rhs=rows[name][:, gsl], start=True, stop=True)
                        return pb, pbB
                    pb, pbB = bc("Em")
                    A("act", "activation", [pbB], [embB], out=Emb[:], in_=pb[:, :], func=AF.Identity)
                    A("dve", "tensor_tensor", [pbB, self.rstdB[0]], [self.tmpfB[0]], out=self.tmpf[0][:], in0=self.rstd[0][:], in1=pb[:, :], op=ALU.mult)
                    A("dve", "scalar_tensor_tensor", [cB, self.tmpfB[0]], [sclB], out=scl[:, 0, :], in0=cqkv[:, 0, :], scalar=128.0 ** -0.5,
                      in1=self.tmpf[0][:], op0=ALU.mult, op1=ALU.mult)
                    for si, name in [(1, "Ep"), (2, "BE"), (3, "KD")]:
                        pb, pbB = bc(name)
                        tk = si % 2
                        A("dve", "tensor_tensor", [pbB, self.rstdB[1]], [self.tmpfB[tk]], out=self.tmpf[tk][:], in0=self.rstd[1][:], in1=pb[:, :], op=ALU.mult)
                        A("dve", "tensor_tensor", [cB, self.tmpfB[tk]], [sclB], out=scl[:, si, :], in0=cqkv[:, 1, :], in1=self.tmpf[tk][:], op=ALU.mult)
                    pb, pbB = bc("be")
                    A("dve", "tensor_tensor", [cB, pbB], [sclB], out=scl[:, 4, :], in0=cqkv[:, 2, :], in1=pb[:, :], op=ALU.mult)
                    for j, si in enumerate([3, 4]):
                        for blk in range(4):
                            A("pe", "transpose", [sclB, self.Bc], [self.pbhB], self.pbh[:, j * 512 + blk * 128:j * 512 + (blk + 1) * 128],
                              scl[:, si, blk * 128:(blk + 1) * 128], self.ident_b)
                    A("dve", "tensor_copy", [self.pbhB], [tokB], out=tok[:, 0:2, :], in_=self.pbh[:, :].rearrange("p (j t) -> p j t", j=2))
                    for blk in range(4):
                        A("pe", "transpose", [sclB, self.Bc], [self.pbhB], self.pbh[:, blk * 128:(blk + 1) * 128],
                          scl[:, 2, blk * 128:(blk + 1) * 128], self.ident_b)
                    A("dve", "tensor_copy", [self.pbhB], [tokB], out=tok[:, 2, :], in_=self.pbh[:, 0:512])
                    Y, P, R, Q = 0, 1, 2, 3
                    cur, nxt = 0, 1
                    for (la, ra, msk, dst, dB) in [(2, 1, mY, YP[cur][:, Y, :], ypB[cur]), (1, 2, mP, YP[cur][:, P, :], ypB[cur]),
                                                   (1, 0, mI, itT[:, :], itB)]:
                        pb, pbB = self.bank()
                        for blk in range(4):
                            bs = slice(blk * 128, (blk + 1) * 128)
                            A("pe", "matmul", [sclB], [pbB], pb[:, bs], lhsT=scl[:, la, bs], rhs=scl[:, ra, bs], start=True, stop=True)
                        A("dve", "tensor_tensor", [pbB, self.Bc], [dB], out=dst, in0=pb[:, :], in1=msk, op=ALU.mult)
                    A("dve", "tensor_tensor", [ypB[cur], self.Bc], [ypB[cur]], out=YP[cur][:, R, :], in0=YP[cur][:, P, :], in1=I4, op=ALU.add)
                    A("dve", "tensor_tensor", [ypB[cur], self.Bc], [ypB[cur]], out=YP[cur][:, Q, :], in0=YP[cur][:, Y, :], in1=I4, op=ALU.add)
                    for lev in range(1, 6):
                        c_, n_ = YP[cur], YP[nxt]
                        for (la, ra, dsti) in [(P, Y, Y), (Y, P, P)]:
                            pb, pbB = self.bank()
                            for blk in range(4):
                                bs = slice(blk * 128, (blk + 1) * 128)
                                A("pe", "matmul", [ypB[cur]], [pbB], pb[:, bs], lhsT=c_[:, la, bs], rhs=c_[:, ra, bs], start=True, stop=True)
                            self.copy_evac([pbB], [ypB[nxt]], n_[:, dsti, :], pb[:, :])
                        for (la, rb, acc, dsti) in [(Q, P, R, R), (R, Y, Q, Q)]:
                            if dsti == Q and lev == 5:
                                continue
                            pb, pbB = self.bank()
                            for blk in range(4):
                                bs = slice(blk * 128, (blk + 1) * 128)
                                A("pe", "matmul", [ypB[cur], ypB[nxt]], [pbB], pb[:, bs], lhsT=c_[:, la, bs], rhs=n_[:, rb, bs], start=True, stop=True)
                            A("dve", "tensor_tensor", [pbB, ypB[cur]], [ypB[nxt]], out=n_[:, dsti, :], in0=pb[:, :], in1=c_[:, acc, :], op=ALU.add)
                        cur, nxt = nxt, cur
                    TT = YP[cur]
                    k = wc % 2; wc += 1
                    self.dma(w[k][:, :, 0:128], wd[:, :, 3 * W + h * 128:3 * W + (h + 1) * 128], [], [wB[k]], eng="pool")
                    pb, pbB = self.bank()
                    for kc in range(8):
                        A("pe", "matmul", [wB[k], hB[kc][gg]], [pbB], pb[:, :], lhsT=w[k][:, kc, 0:128], rhs=hT[:, kc, gsl],
                          start=(kc == 0), stop=(kc == 7))
                    A("act", "activation", [pbB], [gsB], out=gs[:], in_=pb[:, :], func=AF.Silu)
                    if half == 0 and gg == 0:
                        spar[h] = 0
                        A("pool", "memset", [], [SbB], Sbfa[:, 2 * h, :], 0.0)
                    uwB = rhsB
                    qpB = cB
                    mtB = cB
                    import os
                    SK2 = os.environ.get("GDN_SKIP", "")
                    for half2 in range(2 if "uw" not in SK2 else 0):
                        pb, pbB = self.bank()
                        for b2 in range(2):
                            blk = half2 * 2 + b2
                            bs = slice(blk * 128, (blk + 1) * 128)
                            A("pe", "matmul", [ypB[cur], tokB], [pbB], pb[:, b2 * 256:b2 * 256 + 128], lhsT=TT[:, R, bs], rhs=tok[:, 1, bs],
                              start=True, stop=True)
                            A("pe", "matmul", [ypB[cur], tokB], [pbB], pb[:, b2 * 256 + 128:b2 * 256 + 256], lhsT=TT[:, R, bs], rhs=tok[:, 2, bs],
                              start=True, stop=True)
                        self.copy_evac([pbB], [uwB], UWb[:, half2 * 2:half2 * 2 + 2, :].rearrange("p a b -> p (a b)"), pb[:, :])
                    pb, pbB = self.bank()
                    for blk in range(4 if "qp" not in SK2 else 0):
                        bs = slice(blk * 128, (blk + 1) * 128)
                        A("pe", "matmul", [uwB, itB], [pbB], pb[:, bs], lhsT=UWb[:, blk, 128:256], rhs=itT[:, bs], start=True, stop=True)
                    if "qp" not in SK2:
                        A("dve", "tensor_tensor", [pbB, sclB], [qpB], out=QpT[:, :], in0=scl[:, 0, :], in1=pb[:, :], op=ALU.subtract)
                    for par in range(2):
                        pb, pbB = self.bank()
                        po = 64 * par
                        for c2 in range(4):
                            ci = c2 * 2 + par
                            blk = ci // 2
                            A("pe", "matmul", [uwB, tokB], [pbB], pb[:, c2 * 128:(c2 + 1) * 128], lhsT=UWb[po:po + 64, blk, 128:256],
                              rhs=tok[po:po + 64, 0, blk * 128:(blk + 1) * 128], start=True, stop=True)
                        for c2 in range(4):
                            ci = c2 * 2 + par
                            A("dve", "scalar_tensor_tensor", [pbB, embB, self.Bc], [mtB], out=MT[:, ci, :], in0=self.ident_b,
                              scalar=Emb[:, ci * 64 + 63:ci * 64 + 64], in1=pb[:, c2 * 128:(c2 + 1) * 128], op0=ALU.mult, op1=ALU.subtract)
                    for ci in range(8 if "chain" not in SK2 else 0):
                        blk, hb = ci // 2, ci % 2
                        po = 64 * hb
                        cs = slice(ci * 64, (ci + 1) * 64)
                        bcs = slice(blk * 128 + po, blk * 128 + po + 64)
                        Sold = Sbfa[:, 2 * h + spar[h], :]
                        Snew = Sbfa[:, 2 * h + 1 - spar[h], :]
                        spar[h] = 1 - spar[h]
                        import os
                        SK = os.environ.get("GDN_SKIP", "")
                        p3, p3B = self.bank()
                        A("pe", "matmul", [qpB, SbB], [p3B], p3[po:po + 64, 0:128], lhsT=QpT[:, cs], rhs=Sold, start=True, stop=False)
                        A("pe", "matmul", [itB, uwB], [p3B], p3[po:po + 64, 0:128], lhsT=itT[po:po + 64, bcs], rhs=UWb[po:po + 64, blk, 0:128],
                          start=False, stop=True)
                        if "state" not in SK:
                            p4, p4B = self.bank()
                            A("pe", "matmul", [mtB, SbB], [p4B], p4[:, 0:128], lhsT=MT[:, ci, :], rhs=Sold, start=True, stop=False)
                            A("pe", "matmul", [tokB, uwB], [p4B], p4[:, 0:128], lhsT=tok[po:po + 64, 0, blk * 128:(blk + 1) * 128],
                              rhs=UWb[po:po + 64, blk, 0:128], start=False, stop=True)
                            A("act", "activation", [p4B], [SbB], out=Snew, in_=p4[:, 0:128], func=AF.Identity)
                        if "out" in SK:
                            continue
                        A("act", "activation", [p3B], [self.sqB[0], stB], out=self.sq[0][po:po + 64, 0:128], in_=p3[po:po + 64, 0:128], func=AF.Square,
                          accum_out=st[po:po + 64, 0:1])
                        A("act", "activation", [stB, self.Bc], [stB], out=st[po:po + 64, 1:2], in_=st[po:po + 64, 0:1], func=AF.Ln, scale=1.0 / 128,
                          bias=vec[po:po + 64, V_EPS:V_EPS + 1])
                        A("act", "activation", [stB], [stB], out=st[po:po + 64, 2:3], in_=st[po:po + 64, 1:2], func=AF.Exp, scale=-0.5)
                        A("dve", "scalar_tensor_tensor", [p3B, stB, self.Bc], [onB], out=ont[po:po + 64, blk, :], in0=p3[po:po + 64, 0:128],
                          scalar=st[po:po + 64, 2:3], in1=self.onorm_bc[po:po + 64, :], op0=ALU.mult, op1=ALU.mult)
                    for blk in range(4):
                        A("pe", "transpose", [onB, self.Bc], [self.pbhB], self.pbh[:, blk * 128:(blk + 1) * 128], ont[:, blk, :], self.ident_b)
                    A("dve", "tensor_tensor", [self.pbhB, gsB], [mixB[h][g]], out=mixT[:, h, g * 512:(g + 1) * 512], in0=self.pbh[:, 0:512],
                      in1=gs[:], op=ALU.mult)
        self.S.barrier()
        self.mem_kv(layer, kTm, vm, kvB, w, wB)
        self.mem_attn(mq, mqB, kTm, vm, kvB, pT, pTB, rec, recB)
        self.S.barrier()
        ys = self.alloc_once("a_ys", [128, 8, SEQ], BF16, self.R0 + 16384 + 24576 + 8192 + 8192)
        ysB = [[Buf() for _ in range(4)] for _ in range(8)]

        def mixK(kc):
            if kc < 6:
                return mixT[:, kc, :], mixB[kc]
            return mq[:, kc - 6, :], mqB[kc - 6]
        self.out_proj(layer, mixK, ys, ysB, w, wB)

    def mixer_b(self, layer):
        self.S.barrier()
        R0 = self.R0
        o = R0
        hT = self.alloc_once("b_hT", [128, 8, SEQ], BF16, o); o += 32768
        qT = self.alloc_once("b_qT", [128, 6, SEQ], BF16, o); o += 24576
        kT = self.alloc_once("b_kT", [128, 6, SEQ], BF16, o); o += 24576
        vt = self.alloc_once("b_v", [128, 16, 768], BF16, o); o += 24576
        mq = self.alloc_once("b_mq", [128, 2, SEQ], BF16, o); o += 8192
        w = [self.alloc_once(f"b_w{i}", [128, 8, 256], BF16, o + 4096 * i) for i in range(2)]
        o += 8192
        self.mix_end = o
        assert o <= SB_END
        wB = [Buf(), Buf()]
        hB = [[Buf() for _ in range(4)] for _ in range(8)]
        qB = [[Buf() for _ in range(4)] for _ in range(6)]
        kB = [[Buf() for _ in range(4)] for _ in range(6)]
        vB = [Buf() for _ in range(16)]
        mqB = [[Buf() for _ in range(4)] for _ in range(2)]
        gpre = V_GAIN + (layer * 4 + 0) * 8
        for g in range(4):
            self.prenorm(g, gpre, lambda f, g=g: hT[:, f, g * 512:(g + 1) * 512], lambda f, g=g: hB[f][g])
        wd = self.w_in_b[0].rearrange("(k p) n -> p k n", p=128)
        wc = 0
        for piece in list(range(0, 6)) + [9]:
            k = wc % 2
            wc += 1
            self.dma(w[k][:], wd[:, :, piece * 256:(piece + 1) * 256], [], [wB[k]], eng="pool")
            for mm in range(2):
                tile = piece * 2 + mm
                for g in range(4):
                    ts = slice(g * 512, (g + 1) * 512)
                    pb, pbB = self.bank()
                    for kc in range(8):
                        self.A("pe", "matmul", [wB[k], hB[kc][g]], [pbB], pb[:, :], lhsT=w[k][:, kc, mm * 128:(mm + 1) * 128],
                               rhs=hT[:, kc, ts], start=(kc == 0), stop=(kc == 7))
                    if tile < 6:
                        self.copy_evac([pbB], [qB[tile][g]], qT[:, tile, ts], pb[:, :], scale=0.125)
                    elif tile < 12:
                        self.copy_evac([pbB], [kB[tile - 6][g]], kT[:, tile - 6, ts], pb[:, :])
                    else:
                        self.copy_evac([pbB], [mqB[tile - 18][g]], mq[:, tile - 18, ts], pb[:, :])
        for piece in range(6, 9):
            k = wc % 2
            wc += 1
            self.dma(w[k][:], wd[:, :, piece * 256:(piece + 1) * 256], [], [wB[k]], eng="pool")
            for tt in range(16):
                pb, pbB = self.bank()
                for kc in range(8):
                    self.A("pe", "matmul", [wB[k], hB[kc][tt // 4]], [pbB], pb[:, 0:256],
                           lhsT=hT[:, kc, tt * 128:(tt + 1) * 128], rhs=w[k][:, kc, :], start=(kc == 0), stop=(kc == 7))
                self.copy_evac([pbB], [vB[tt]], vt[:, tt, (piece - 6) * 256:(piece - 5) * 256], pb[:, 0:256])
        self.S.barrier()
        o = R0
        NH = 4
        eT = [self.alloc_once(f"b_e{i}", [128, 512], F32, o + 2048 * i) for i in range(NH)]; o += 2048 * NH
        nl = [self.alloc_once(f"b_nl{i}", [128, 512], BF16, o + 1024 * i) for i in range(NH)]; o += 1024 * NH
        Sb = [self.alloc_once(f"b_S{i}", [128, 512], BF16, o + 1024 * i) for i in range(NH)]; o += 1024 * NH
        aT = [self.alloc_once(f"b_a{i}", [128, 512], BF16, o + 1024 * i) for i in range(NH)]; o += 1024 * NH
        pT = [self.alloc_once(f"b_p{i}", [128, 512], BF16, o + 1024 * i) for i in range(2)]; o += 2048
        rec = self.alloc_once("b_rec", [128, 512], F32, o); o += 2048
        kTm = self.alloc_once("b_kTm", [128, 2, NMEM], BF16, o); o += 1024
        vm = self.alloc_once("b_vm", [128, 2, 256], BF16, o); o += 1024
        assert o <= R0 + 32768
        eB = [Buf() for _ in range(NH)]; nlB = [Buf() for _ in range(NH)]; SbB = [Buf() for _ in range(NH)]
        aB = [Buf() for _ in range(NH)]
        pTB = [Buf(), Buf()]; recB = Buf(); kvB = Buf()
        self.mem_kv(layer, kTm, vm, kvB, w, wB)
        self.mem_attn(mq, mqB, kTm, vm, kvB, pT, pTB, rec, recB)
        A = self.A
        for hg in range(12 // NH):
            heads = [hg * NH + i for i in range(NH)]
            for g in range(4):
                ts = slice(g * 512, (g + 1) * 512)
                bmax = 4 * g + 3
                obs = {}
                for h in heads:
                    if h // 2 not in obs:
                        obs[h // 2] = self.reserve()
                for i in range(NH):
                    A("pool", "memset", [], [SbB[i]], Sb[i][:], 0.0)
                for b in range(bmax, -1, -1):
                    r = b - 4 * g
                    bs = slice(b * 128, (b + 1) * 128)
                    c0 = 128 * r if r > 0 else 0
                    cs = slice(c0, 512)
                    tsc = slice(g * 512 + c0, (g + 1) * 512)
                    msk = self.maskbase[:, 384 - 128 * r + c0:384 - 128 * r + 512] if r >= 0 else None
                    zbs = []
                    for i, h in enumerate(heads):
                        tl, po = h // 2, 64 * (h % 2)
                        zb, zbB = self.bank()
                        zbs.append((zb, zbB))
                        A("pe", "matmul", [kB[tl][b // 4], qB[tl][g]], [zbB], zb[:, cs], lhsT=kT[po:po + 64, tl, bs],
                          rhs=qT[po:po + 64, tl, tsc], start=True, stop=(r < 0))
                        if r >= 0:
                            A("pe", "matmul", [self.Bc], [zbB], zb[:, cs], lhsT=self.ident_b, rhs=msk, start=False, stop=True)
                    for i, h in enumerate(heads):
                        A("act", "activation", [zbs[i][1]], [eB[i]], out=eT[i][:, cs], in_=zbs[i][0][:, cs], func=AF.Exp)
                    for i, h in enumerate(heads):
                        A("act", "activation", [eB[i], self.Bc], [nlB[i]], out=nl[i][:, cs], in_=eT[i][:, cs], func=AF.Ln,
                          bias=self.vec[:, V_ONE:V_ONE + 1])
                    abs_ = []
                    for i, h in enumerate(heads):
                        tl, po = h // 2, 64 * (h % 2)
                        ab, abB = self.bank()
                        abs_.append((ab, abB))
                        A("pe", "matmul", [kB[tl][b // 4], qB[tl][g]], [abB], ab[:, cs], lhsT=kT[po:po + 64, tl, bs],
                          rhs=qT[po:po + 64, tl, tsc], start=True, stop=False)
                        if r >= 0:
                            A("pe", "matmul", [self.Bc], [abB], ab[:, cs], lhsT=self.ident_b, rhs=msk, start=False, stop=False)
                        A("pe", "matmul", [self.Bc, nlB[i]], [abB], ab[:, cs], lhsT=self.negU_b, rhs=nl[i][:, cs],
                          start=False, stop=(b == bmax))
                        if b < bmax:
                            A("pe", "matmul", [self.Bc, SbB[i]], [abB], ab[:, cs], lhsT=self.negones_b, rhs=Sb[i][:, cs],
                              start=False, stop=True)
                    for i, h in enumerate(heads):
                        A("act", "activation", [abs_[i][1]], [aB[i]], out=aT[i][:, cs], in_=abs_[i][0][:, cs], func=AF.Exp)
                    for i, h in enumerate(heads):
                        tl, po = h // 2, 64 * (h % 2)
                        ob, obB = obs[tl]
                        A("pe", "matmul", [vB[b], aB[i]], [obB], ob[po:po + 64, cs], lhsT=vt[:, b, h * 64:(h + 1) * 64],
                          rhs=aT[i][:, cs], start=(b == bmax), stop=(b == 0))
                        if b > 0:
                            A("dve", "tensor_tensor", [nlB[i], SbB[i]], [SbB[i]], out=Sb[i][:, cs], in0=Sb[i][:, cs], in1=nl[i][:, cs],
                              op=ALU.add)
                for tl, (ob, obB) in obs.items():
                    A("dve", "tensor_copy", [obB, qB[tl][g]], [qB[tl][g]], out=qT[:, tl, ts], in_=ob[:, :])
                    self.release(ob)
        self.S.barrier()
        ys = hT
        ysB = hB

        def mixK(kc):
            if kc < 6:
                return qT[:, kc, :], qB[kc]
            return mq[:, kc - 6, :], mqB[kc - 6]
        self.out_proj(layer, mixK, ys, ysB, w, wB)

    def mlp(self, layer):
        self.S.barrier()
        R0 = self.R0
        if not hasattr(self, "mlp_bufs"):
            o = R0
            hT = self.alloc("m_hT", [128, 8, 1024], BF16, at=o); o += 16384
            uT = self.alloc("m_uT", [128, 32, 1024], BF16, at=o); o += 65536
            wu = []
            for i in range(2):
                wu.append(self.alloc(f"m_wu{i}", [128, 8, 512], BF16, at=o)); o += 8192
            wd = []
            for i in range(2):
                wd.append(self.alloc(f"m_wd{i}", [128, 32, 128], BF16, at=o)); o += 8192
            rl = []
            for i in range(2):
                rl.append(self.alloc(f"m_rl{i}", [128, 512], BF16, at=o)); o += 1024
            self.mlp_bufs = dict(hT=hT, uT=uT, wu=wu, wd=wd, rl=rl,
                                 hTB=[[Buf() for _ in range(2)] for _ in range(8)],
                                 uTB=[[Buf() for _ in range(2)] for _ in range(32)],
                                 wuB=[Buf(), Buf()], wdB=[Buf(), Buf()], rlB=[Buf(), Buf()])
            self.mlp_end = o
        mb = self.mlp_bufs
        hT, uT = mb["hT"], mb["uT"]
        gpre = V_GAIN + (layer * 4 + 2) * 8
        gpost = V_GAIN + (layer * 4 + 3) * 8
        wu_d = self.w_up[layer].rearrange("(k p) n -> p k n", p=128)
        wd_d = self.w_down[layer].rearrange("(k p) n -> p k n", p=128)
        wcnt = 0
        rcnt = 0
        for half in range(2):
            for gg in range(2):
                g = half * 2 + gg
                self.prenorm(g, gpre, lambda f, gg=gg: hT[:, f, gg * 512:(gg + 1) * 512],
                             lambda f, gg=gg: mb["hTB"][f][gg])
            for jq in range(8):
                k = wcnt % 2
                wcnt += 1
                w, wB = mb["wu"][k], mb["wuB"][k]
                self.dma(w[:], wu_d[:, :, jq * 512:(jq + 1) * 512], [], [wB], eng="pool")
                for jj in range(4):
                    j = jq * 4 + jj
                    for gg in range(2):
                        pb, pbB = self.bank()
                        for kc in range(8):
                            self.A("pe", "matmul", [wB, mb["hTB"][kc][gg]], [pbB], pb[:, :],
                                   lhsT=w[:, kc, jj * 128:(jj + 1) * 128],
                                   rhs=hT[:, kc, gg * 512:(gg + 1) * 512], start=(kc == 0), stop=(kc == 7))
                        r = rcnt % 2
                        rcnt += 1
                        self.A("act", "activation", [pbB], [mb["rlB"][r]], out=mb["rl"][r][:], in_=pb[:, :],
                               func=AF.Relu)
                        self.A("dve", "tensor_tensor", [mb["rlB"][r]], [mb["uTB"][j][gg]],
                               out=uT[:, j, gg * 512:(gg + 1) * 512], in0=mb["rl"][r][:], in1=mb["rl"][r][:],
                               op=ALU.mult)
            ssb = [self.reserve(), self.reserve()]
            for fq in range(8):
                k = wcnt % 2
                wcnt += 1
                w, wB = mb["wd"][k], mb["wdB"][k]
                self.dma(w[:], wd_d[:, :, fq * 128:(fq + 1) * 128], [], [wB], eng="pool")
                for ff in range(1):
                    f = fq
                    for gg in range(2):
                        pb, pbB = self.bank()
                        for kc in range(32):
                            self.A("pe", "matmul", [wB, mb["uTB"][kc][gg]], [pbB], pb[:, :],
                                   lhsT=w[:, kc, ff * 128:(ff + 1) * 128],
                                   rhs=uT[:, kc, gg * 512:(gg + 1) * 512], start=(kc == 0), stop=(kc == 31))
                        ks = self.sq_rr
                        self.sq_rr = (ks + 1) % 3
                        self.A("dve", "tensor_copy", [pbB], [mb["hTB"][f][gg]],
                               out=hT[:, f, gg * 512:(gg + 1) * 512], in_=pb[:, :])
                        self.A("act", "activation", [mb["hTB"][f][gg]], [self.sqB[ks]], out=self.sq[ks][:],
                               in_=hT[:, f, gg * 512:(gg + 1) * 512], func=AF.Square)
                        self.A("pe", "matmul", [self.sqB[ks], self.Bc], [ssb[gg][1]], ssb[gg][0][:, :],
                               lhsT=self.ones_b, rhs=self.sq[ks][:], start=(f == 0), stop=(f == 7))
            for gg in range(2):
                g = half * 2 + gg
                self.postnorm_apply(g, gpost, lambda f, gg=gg: hT[:, f, gg * 512:(gg + 1) * 512],
                                    lambda f, gg=gg: mb["hTB"][f][gg], ssb[gg][0], ssb[gg][1])
            for gg in range(2):
                self.release(ssb[gg][0])


def make_inputs_common(inp):
    vec, cf, cb = host_consts(inp)
    com = {k: np.ascontiguousarray(np.asarray(inp[k], np.float32)) for k in
           ["w_in_a", "w_in_b", "w_mem_kv", "w_out", "w_up", "w_down"]}
    com["vecs"] = vec
    com["cf"] = cf
    com["cb"] = cb
    return com


def kernel(**inputs):
    n = 8
    x = np.asarray(inputs["x"], np.float32)
    mem = np.asarray(inputs["mem"], np.float32)
    nseq = x.shape[0] // n
    prog = Prog(nseq)
    nc = prog.build()
    com = make_inputs_common(inputs)
    in_maps = []
    for c in range(n):
        m = dict(com)
        m["x"] = np.ascontiguousarray(x[c * nseq:(c + 1) * nseq])
        m["mem"] = np.ascontiguousarray(mem[c * nseq:(c + 1) * nseq])
        in_maps.append(m)
    res = run_bass_kernel_spmd(nc, in_maps, core_ids=list(range(n)))
    return np.concatenate([r["out"] for r in res.results], axis=0)
```

```python
import numpy as np
import concourse.bass as bass
import concourse.mybir as mybir
from concourse.bass_utils import run_bass_kernel_spmd
from contextlib import ExitStack

F32 = mybir.dt.float32
BF16 = mybir.dt.bfloat16
ALU = mybir.AluOpType
AF = mybir.ActivationFunctionType
AX = mybir.AxisListType

EPOCH = 30000
N_DMA_SEMS = 24
SB_BASE = 16512
SB_END = 229344

D = 1024
SEQ = 2048
NMEM = 256
DFF = 4096
IN_A = 3340
IN_B = 2560
EPS = 1e-6
NEG = -30000.0


class Buf:
    __slots__ = ("name", "last_w", "readers")

    def __init__(self, name=""):
        self.name = name
        self.last_w = None
        self.readers = []


class Op:
    __slots__ = ("eng", "fn", "raw", "war", "signal", "sem", "val", "is_dma", "ndma")

    def __init__(self, eng, fn, is_dma=False, ndma=1):
        self.eng = eng
        self.fn = fn
        self.raw = []
        self.war = []
        self.signal = False
        self.sem = None
        self.val = None
        self.is_dma = is_dma
        self.ndma = ndma


class Sched:
    ENGS = ("pe", "act", "dve", "pool", "sp")

    def __init__(self, nc, es):
        self.nc = nc
        self.es = es
        self.ops = {e: [] for e in self.ENGS}
        self.n_ops = 0
        self.dma_rr = 0
        self.dma_last = [None] * N_DMA_SEMS
        self.dma_tot = [0] * N_DMA_SEMS

    def add(self, eng, fn, reads=(), writes=(), is_dma=False, ndma=1):
        op = Op(eng, fn, is_dma, ndma)
        raw = set()
        war = set()
        for b in reads:
            if b.last_w is not None:
                raw.add(b.last_w)
        for b in writes:
            if b.last_w is not None:
                war.add(b.last_w)
            for r in b.readers:
                war.add(r)
        for b in writes:
            b.last_w = op
            b.readers = []
        for b in reads:
            b.readers.append(op)
        if is_dma:
            k = self.dma_rr
            self.dma_rr = (k + 1) % N_DMA_SEMS
            if self.dma_last[k] is not None:
                raw.add(self.dma_last[k])
            self.dma_last[k] = op
            self.dma_tot[k] += 16 * ndma
            op.sem = k
            op.val = self.dma_tot[k]
            op.signal = True
        for x in raw:
            if x.is_dma or x.eng != eng or eng != "pe":
                op.raw.append(x)
        for x in war:
            if x in raw:
                continue
            if x.is_dma or x.eng != eng:
                op.war.append(x)
        for x in op.raw:
            x.signal = True
        for x in op.war:
            x.signal = True
        self.ops[eng].append(op)
        self.n_ops += 1
        return op

    def barrier(self):
        lasts = []
        for e in self.ENGS:
            for op in reversed(self.ops[e]):
                if op.fn is not None and not op.is_dma:
                    lasts.append(op)
                    break
        dmas = [op for op in self.dma_last if op is not None]
        for e in self.ENGS:
            op = Op(e, None)
            for x in lasts:
                if x.eng != e:
                    op.raw.append(x)
                    x.signal = True
            for x in dmas:
                op.raw.append(x)
            self.ops[e].append(op)

    def emit(self):
        nc = self.nc
        es = self.es
        dma_sems = [es.enter_context(nc.semaphore(f"dsem{k}")) for k in range(N_DMA_SEMS)]
        eng_sems = {}
        for e in self.ENGS:
            cnt = 0
            for op in self.ops[e]:
                if op.is_dma:
                    op.sem = dma_sems[op.sem]
                    continue
                if op.signal:
                    key = (e, cnt // EPOCH)
                    if key not in eng_sems:
                        eng_sems[key] = es.enter_context(nc.semaphore(f"s_{e}_{key[1]}"))
                    op.sem = eng_sems[key]
                    op.val = cnt % EPOCH + 1
                    cnt += 1

        def run(e, eng):
            seen = {}
            for op in self.ops[e]:
                need = {}
                for x in op.raw + op.war:
                    k = id(x.sem)
                    if k not in need or need[k][1] < x.val:
                        need[k] = (x.sem, x.val)
                for k, (sem, val) in need.items():
                    if seen.get(k, 0) >= val:
                        continue
                    seen[k] = val
                    eng.wait_ge(sem, val)
                if op.fn is None:
                    continue
                r = op.fn(eng)
                if op.is_dma:
                    if not isinstance(r, (list, tuple)):
                        r = [r]
                    assert len(r) == op.ndma
                    for ins in r:
                        ins.then_inc(op.sem, 16)
                elif op.signal:
                    r.then_inc(op.sem, 1)
            if e == "sp":
                for k in range(N_DMA_SEMS):
                    if self.dma_tot[k] > 0:
                        eng.wait_ge(dma_sems[k], self.dma_tot[k])

        with nc.Block() as block:
            @block.tensor
            def _(eng):
                run("pe", eng)

            @block.scalar
            def _(eng):
                run("act", eng)

            @block.vector
            def _(eng):
                run("dve", eng)

            @block.gpsimd
            def _(eng):
                run("pool", eng)

            @block.sync
            def _(eng):
                run("sp", eng)


V_EPS = 0
V_ONE = 1
V_GAIN = 2
V_MEMG = V_GAIN + 64
V_CONV = V_MEMG + 8
V_ALOG = V_CONV + 72
V_DTB = V_ALOG + 1
NV = V_DTB + 1


CB_SEL = 1408
CB_MY = CB_SEL + 768
CB_MP = CB_MY + 512
CB_MI = CB_MP + 512
CB_I4 = CB_MI + 512
CB_N = CB_I4 + 512


def host_consts(inp):
    vec = np.zeros((128, NV), np.float32)
    vec[:, V_EPS] = EPS
    vec[:, V_ONE] = 1.0
    for layer in range(2):
        for j, nm in enumerate(["norm_pre_mix", "norm_post_mix", "norm_pre_mlp", "norm_post_mlp"]):
            g = np.asarray(inp[nm], np.float32)[layer]
            vec[:, V_GAIN + (layer * 4 + j) * 8:V_GAIN + (layer * 4 + j) * 8 + 8] = g.reshape(8, 128).T
    vec[:, V_MEMG:V_MEMG + 8] = np.asarray(inp["mem_norm"], np.float32).reshape(8, 128).T
    cw = np.asarray(inp["conv_w_a"], np.float32)[0]
    vec[:, V_CONV:V_CONV + 72] = cw.reshape(4, 18, 128).transpose(2, 1, 0).reshape(128, 72)
    vec[0:6, V_ALOG] = np.asarray(inp["a_log_a"], np.float32)[0]
    vec[0:6, V_DTB] = np.asarray(inp["dt_bias_a"], np.float32)[0]
    cf = np.zeros((128, 384), np.float32)
    cf[:, 256:384] = 1.0
    cf[:, 0:128] = np.eye(128, dtype=np.float32)
    cf[:, 128:256] = np.tile(np.asarray(inp["onorm_a"], np.float32)[0][None, :], (128, 1))
    p = np.arange(128)[:, None]
    c = np.arange(128)[None, :]
    cb = np.zeros((128, CB_N), np.float32)
    cb[:, 0:128] = np.eye(128)
    cb[:, 128:256] = 1.0
    cb[:, 256:384] = -1.0 * (p >= c)
    cb[:, 384:512] = -1.0
    cc = np.arange(896)[None, :]
    cb[:, 512:512 + 896] = np.where(cc - 384 <= p, NEG, 0.0)
    for h in range(6):
        cb[h, CB_SEL + h * 128:CB_SEL + (h + 1) * 128] = 1.0
    same = (p // 64) == (c // 64)
    mY = -1.0 * ((p > c) & same)
    mP = -1.0 * ((c > p) & same)
    mI = 1.0 * ((c >= p) & same)
    cb[:, CB_MY:CB_MY + 512] = np.tile(mY, (1, 4))
    cb[:, CB_MP:CB_MP + 512] = np.tile(mP, (1, 4))
    cb[:, CB_MI:CB_MI + 512] = np.tile(mI, (1, 4))
    cb[:, CB_I4:CB_I4 + 512] = np.tile(np.eye(128), (1, 4))
    return vec, cf, cb


class Prog:
    def __init__(self, nseq, stages=("mem", "mix0", "l0", "mix1", "l1"), dbg=None):
        self.nseq = nseq
        self.stages = stages
        self.dbg = dbg
        self.nc = nc = bass.Bass("TRN2", target_bir_lowering=False)
        self.es = ExitStack()
        self.S = Sched(nc, self.es)
        di = lambda n, s: nc.dram_tensor(n, list(s), F32, kind="ExternalInput").ap()
        self.x = di("x", [nseq, SEQ, D])
        self.mem = di("mem", [nseq, NMEM, D])
        self.w_in_a = di("w_in_a", [1, D, IN_A])
        self.w_in_b = di("w_in_b", [1, D, IN_B])
        self.w_mem_kv = di("w_mem_kv", [2, D, 512])
        self.w_out = di("w_out", [2, D, D])
        self.w_up = di("w_up", [2, D, DFF])
        self.w_down = di("w_down", [2, DFF, D])
        self.vecs_d = di("vecs", [128, NV])
        self.cf_d = di("cf", [128, 384])
        self.cb_d = di("cb", [128, CB_N])
        self.out = nc.dram_tensor("out", [nseq, SEQ, D], F32, kind="ExternalOutput").ap()
        if dbg is not None:
            self.dbg_out = nc.dram_tensor("dbg", list(dbg), F32, kind="ExternalOutput").ap()
        self.cur = SB_BASE
        self.reserved = set()
        self.rr = 0
        self.evac_rr = 0

    def alloc(self, name, shape, dtype, at=None):
        esz = 4 if dtype == F32 else 2
        n = 1
        for s in shape[1:]:
            n *= s
        nbytes = (n * esz + 31) // 32 * 32
        if at is None:
            at = self.cur
            self.cur += nbytes
        assert at % 32 == 0 and at + nbytes <= SB_END, (name, at, nbytes)
        t = self.nc.alloc_sbuf_tensor_at(name, list(shape), dtype, offset=at)
        return t

    def A(self, eng, name, reads, writes, *args, **kw):
        return self.S.add(eng, lambda e: getattr(e, name)(*args, **kw), reads, writes)

    def dma(self, out, in_, reads, writes, eng="sp"):
        return self.S.add(eng, lambda e: e.dma_start(out=out, in_=in_), reads, writes, is_dma=True)

    def bank(self):
        while True:
            k = self.rr
            self.rr = (k + 1) % 7
            if k not in self.reserved:
                return self.banks[k], self.bankB[k]

    def reserve(self):
        pb, pbB = self.bank()
        self.reserved.add(self.banks.index(pb))
        return pb, pbB

    def release(self, pb):
        self.reserved.discard(self.banks.index(pb))

    def evac_eng(self):
        self.evac_rr ^= 1
        return "act" if self.evac_rr else "dve"

    def build(self):
        self.setup()
        for s in range(self.nseq):
            self.load_x(s)
            if "mem" in self.stages:
                self.mem_prep(s)
            for layer in range(2):
                if f"mix{layer}" in self.stages:
                    if layer == 0:
                        self.mixer_a(layer)
                    else:
                        self.mixer_b(layer)
                if f"l{layer}" in self.stages:
                    self.mlp(layer)
            self.S.barrier()
            self.store_x(s)
        self.S.emit()
        return self.nc

    def setup(self):
        nc = self.nc
        self.banks = [self.es.enter_context(nc.psum_tensor(f"pb{k}", [128, 512], F32)) for k in range(7)]
        self.bankB = [Buf(f"pb{k}") for k in range(7)]
        self.pbh = self.es.enter_context(nc.psum_tensor("pbh", [128, 1024], BF16))
        self.pbhB = Buf("pbh")
        self.XT = self.alloc("XT", [128, 8, SEQ], F32)
        self.XB = [[Buf(f"X{f}_{g}") for g in range(4)] for f in range(8)]
        self.vec = self.alloc("vec", [128, NV], F32)
        self.cf = self.alloc("cf", [128, 384], F32)
        self.cb = self.alloc("cb", [128, CB_SEL], BF16)
        self.Bc = Buf("consts")
        self.ident_f = self.cf[:, 0:128]
        self.onorm_bc = self.cf[:, 128:256]
        self.ident_b = self.cb[:, 0:128]
        self.ones_b = self.cb[:, 128:256]
        self.negU_b = self.cb[:, 256:384]
        self.negones_b = self.cb[:, 384:512]
        self.maskbase = self.cb[:, 512:512 + 896]
        self.memnT = self.alloc("memnT", [128, 8, NMEM], BF16)
        self.BmemnT = Buf("memnT")
        self.sq = [self.alloc(f"sq{i}", [128, 512], BF16) for i in range(3)]
        self.sqB = [Buf(f"sq{i}") for i in range(3)]
        self.sq_rr = 0
        self.lnb = self.alloc("lnb", [128, 512], F32)
        self.BlnB = Buf("lnb")
        self.rstd_at = self.cur
        self.rstd = [self.alloc(f"rstd{i}", [128, 512], F32) for i in range(2)]
        self.rstdB = [Buf(f"rstd{i}") for i in range(2)]
        self.rstd_rr = 0
        self.tmpf_at = self.cur
        self.tmpf = [self.alloc(f"tmpf{i}", [128, 512], F32) for i in range(2)]
        self.tmpfB = [Buf(f"tmpf{i}") for i in range(2)]
        self.tmpf_rr = 0
        self.R0 = self.cur
        self.xin = [self.nc.alloc_sbuf_tensor_at("xin0", [128, D], F32, offset=self.tmpf_at),
                    self.nc.alloc_sbuf_tensor_at("xin1", [128, D], F32, offset=self.rstd_at)]
        self.xinB = [self.tmpfB, self.rstdB]
        self.dma(self.vec[:], self.vecs_d[:, :], [], [self.Bc])
        self.dma(self.cf[:], self.cf_d[:, :], [], [self.Bc])
        self.dma(self.cb[:], self.cb_d[:, 0:CB_SEL], [], [self.Bc], eng="pool")

    def load_x(self, s):
        for tt in range(16):
            xi, xiB = self.xin[tt % 2], self.xinB[tt % 2]
            self.dma(xi[:], self.x[s, tt * 128:(tt + 1) * 128, :], [], xiB)
            g = tt // 4
            for half in range(2):
                pb, pbB = self.bank()
                for j in range(4):
                    f = half * 4 + j
                    self.A("pe", "transpose", xiB + [self.Bc], [pbB], pb[:, j * 128:(j + 1) * 128],
                           xi[:, f * 128:(f + 1) * 128], self.ident_f)
                dst = self.XT[:, half * 4:half * 4 + 4, tt * 128:(tt + 1) * 128]
                src = pb[:, :].rearrange("p (j t) -> p j t", j=4)
                wr = [self.XB[half * 4 + j][g] for j in range(4)]
                self.A("dve", "tensor_copy", [pbB], wr, out=dst, in_=src)

    def store_x(self, s):
        for tt in range(16):
            xi, xiB = self.xin[tt % 2], self.xinB[tt % 2]
            g = tt // 4
            for half in range(2):
                pb, pbB = self.bank()
                for j in range(4):
                    f = half * 4 + j
                    self.A("pe", "transpose", [self.XB[f][g], self.Bc], [pbB], pb[:, j * 128:(j + 1) * 128],
                           self.XT[:, f, tt * 128:(tt + 1) * 128], self.ident_f)
                dst = xi[:, half * 512:(half + 1) * 512]
                if self.evac_eng() == "act":
                    self.A("act", "activation", [pbB], xiB, out=dst, in_=pb[:, :], func=AF.Identity)
                else:
                    self.A("dve", "tensor_copy", [pbB], xiB, out=dst, in_=pb[:, :])
            self.dma(self.out[s, tt * 128:(tt + 1) * 128, :], xi[:], xiB, [])

    def rstd_from_ss(self, pb, pbB):
        k = self.rstd_rr
        self.rstd_rr ^= 1
        r, rB = self.rstd[k], self.rstdB[k]
        self.A("act", "activation", [pbB, self.Bc], [self.BlnB], out=self.lnb[:], in_=pb[:, :], func=AF.Ln,
               scale=1.0 / D, bias=self.vec[:, V_EPS:V_EPS + 1])
        self.A("act", "activation", [self.BlnB], [rB], out=r[:], in_=self.lnb[:], func=AF.Exp, scale=-0.5)
        return r, rB

    def prenorm(self, g, gcol, dst, dstB):
        ts = slice(g * 512, (g + 1) * 512)
        pb, pbB = self.bank()
        for f in range(8):
            k = self.sq_rr
            self.sq_rr = (k + 1) % 3
            self.A("act", "activation", [self.XB[f][g]], [self.sqB[k]], out=self.sq[k][:], in_=self.XT[:, f, ts],
                   func=AF.Square)
            self.A("pe", "matmul", [self.sqB[k], self.Bc], [pbB], pb[:, :], lhsT=self.ones_b, rhs=self.sq[k][:],
                   start=(f == 0), stop=(f == 7))
        r, rB = self.rstd_from_ss(pb, pbB)
        for f in range(8):
            self.A("dve", "scalar_tensor_tensor", [self.XB[f][g], rB, self.Bc], [dstB(f)], out=dst(f),
                   in0=self.XT[:, f, ts], scalar=self.vec[:, gcol + f:gcol + f + 1], in1=r[:],
                   op0=ALU.mult, op1=ALU.mult)

    def postnorm_apply(self, g, gcol, ys, ysB, ssb, ssbB):
        ts = slice(g * 512, (g + 1) * 512)
        r, rB = self.rstd_from_ss(ssb, ssbB)
        for f in range(8):
            k = self.tmpf_rr
            self.tmpf_rr ^= 1
            t, tB = self.tmpf[k], self.tmpfB[k]
            self.A("dve", "tensor_tensor", [ysB(f), rB], [tB], out=t[:], in0=ys(f), in1=r[:], op=ALU.mult)
            self.A("dve", "scalar_tensor_tensor", [tB, self.XB[f][g], self.Bc], [self.XB[f][g]],
                   out=self.XT[:, f, ts], in0=t[:], scalar=self.vec[:, gcol + f:gcol + f + 1],
                   in1=self.XT[:, f, ts], op0=ALU.mult, op1=ALU.add)


    def mem_prep(self, s):
        self.S.barrier()
        R0 = self.R0
        memT = self.alloc_once("mp_memT", [128, 8, NMEM], F32, R0)
        mB = Buf()
        for mt in range(2):
            xi, xiB = self.xin[mt], self.xinB[mt]
            self.dma(xi[:], self.mem[s, mt * 128:(mt + 1) * 128, :], [], xiB)
            for half in range(2):
                pb, pbB = self.bank()
                for j in range(4):
                    f = half * 4 + j
                    self.A("pe", "transpose", xiB + [self.Bc], [pbB], pb[:, j * 128:(j + 1) * 128],
                           xi[:, f * 128:(f + 1) * 128], self.ident_f)
                self.A("dve", "tensor_copy", [pbB], [mB], out=memT[:, half * 4:half * 4 + 4, mt * 128:(mt + 1) * 128],
                       in_=pb[:, :].rearrange("p (j t) -> p j t", j=4))
        self.S.barrier()
        pb, pbB = self.bank()
        for f in range(8):
            k = self.sq_rr
            self.sq_rr = (k + 1) % 3
            self.A("act", "activation", [mB], [self.sqB[k]], out=self.sq[k][:, 0:NMEM], in_=memT[:, f, :], func=AF.Square)
            self.A("pe", "matmul", [self.sqB[k], self.Bc], [pbB], pb[:, 0:NMEM], lhsT=self.ones_b, rhs=self.sq[k][:, 0:NMEM],
                   start=(f == 0), stop=(f == 7))
        self.A("act", "activation", [pbB, self.Bc], [self.BlnB], out=self.lnb[:, 0:NMEM], in_=pb[:, 0:NMEM], func=AF.Ln,
               scale=1.0 / D, bias=self.vec[:, V_EPS:V_EPS + 1])
        r, rB = self.rstd[0], self.rstdB[0]
        self.A("act", "activation", [self.BlnB], [rB], out=r[:, 0:NMEM], in_=self.lnb[:, 0:NMEM], func=AF.Exp, scale=-0.5)
        for f in range(8):
            self.A("dve", "scalar_tensor_tensor", [mB, rB, self.Bc], [self.BmemnT], out=self.memnT[:, f, :],
                   in0=memT[:, f, :], scalar=self.vec[:, V_MEMG + f:V_MEMG + f + 1], in1=r[:, 0:NMEM],
                   op0=ALU.mult, op1=ALU.mult)

    def alloc_once(self, name, shape, dtype, at):
        if not hasattr(self, "_once"):
            self._once = {}
        if name not in self._once:
            self._once[name] = self.alloc(name, shape, dtype, at=at)
        return self._once[name]

    def copy_evac(self, reads, writes, out, in_, scale=None):
        if self.evac_eng() == "act":
            if scale is None:
                self.A("act", "activation", reads, writes, out=out, in_=in_, func=AF.Identity)
            else:
                self.A("act", "activation", reads, writes, out=out, in_=in_, func=AF.Identity, scale=scale)
        else:
            if scale is None:
                self.A("dve", "tensor_copy", reads, writes, out=out, in_=in_)
            else:
                self.A("dve", "tensor_scalar", reads, writes, out=out, in0=in_, scalar1=scale, scalar2=None, op0=ALU.mult)

    def mem_kv(self, layer, kTm, vm, kvB, w, wB):
        wd = self.w_mem_kv[layer].rearrange("(k p) n -> p k n", p=128)
        self.dma(w[0][:], wd[:, :, 0:256], [], [wB[0]], eng="pool")
        self.dma(w[1][:], wd[:, :, 256:512], [], [wB[1]], eng="pool")
        for x in range(4):
            pb, pbB = self.bank()
            po = 64 * (x % 2)
            for kc in range(8):
                self.A("pe", "matmul", [wB[0], self.BmemnT], [pbB], pb[po:po + 64, 0:NMEM],
                       lhsT=w[0][:, kc, x * 64:(x + 1) * 64], rhs=self.memnT[:, kc, :], start=(kc == 0), stop=(kc == 7))
            self.copy_evac([pbB], [kvB], kTm[po:po + 64, x // 2, :], pb[po:po + 64, 0:NMEM])
        for mt in range(2):
            pb, pbB = self.bank()
            for kc in range(8):
                self.A("pe", "matmul", [wB[1], self.BmemnT], [pbB], pb[:, 0:256],
                       lhsT=self.memnT[:, kc, mt * 128:(mt + 1) * 128], rhs=w[1][:, kc, :], start=(kc == 0), stop=(kc == 7))
            self.copy_evac([pbB], [kvB], vm[:, mt, :], pb[:, 0:256])

    def mem_attn(self, memqT, mqB, kTm, vm, kvB, pT, pTB, rec, recB):
        for g in range(4):
            ts = slice(g * 512, (g + 1) * 512)
            for x in range(4):
                po = 64 * (x % 2)
                tl = x // 2
                ob, obB = self.reserve()
                db, dbB = self.reserve()
                for mt in range(2):
                    zb, zbB = self.bank()
                    self.A("pe", "matmul", [kvB, mqB[tl][g]], [zbB], zb[:, :],
                           lhsT=kTm[po:po + 64, tl, mt * 128:(mt + 1) * 128], rhs=memqT[po:po + 64, tl, ts],
                           start=True, stop=True)
                    self.A("act", "activation", [zbB], [pTB[mt]], out=pT[mt][:], in_=zb[:, :], func=AF.Exp, scale=0.125)
                    self.A("pe", "matmul", [kvB, pTB[mt]], [obB], ob[po:po + 64, :],
                           lhsT=vm[:, mt, x * 64:(x + 1) * 64], rhs=pT[mt][:], start=(mt == 0), stop=(mt == 1))
                    self.A("pe", "matmul", [self.Bc, pTB[mt]], [dbB], db[po:po + 64, :],
                           lhsT=self.ones_b[:, 0:64], rhs=pT[mt][:], start=(mt == 0), stop=(mt == 1))
                self.A("dve", "reciprocal", [dbB], [recB], out=rec[po:po + 64, :], in_=db[po:po + 64, :])
                self.A("dve", "tensor_tensor", [obB, recB, mqB[tl][g]], [mqB[tl][g]], out=memqT[po:po + 64, tl, ts],
                       in0=ob[po:po + 64, :], in1=rec[po:po + 64, :], op=ALU.mult)
                self.release(ob)
                self.release(db)

    def out_proj(self, layer, mixK, ys, ysB, w, wB):
        gpost = V_GAIN + (layer * 4 + 1) * 8
        wd = self.w_out[layer].rearrange("(k p) n -> p k n", p=128)
        ssb = [self.reserve() for _ in range(4)]
        for fq in range(4):
            k = fq % 2
            self.dma(w[k][:], wd[:, :, fq * 256:(fq + 1) * 256], [], [wB[k]], eng="pool")
            for ff in range(2):
                f = fq * 2 + ff
                for g in range(4):
                    ts = slice(g * 512, (g + 1) * 512)
                    pb, pbB = self.bank()
                    for kc in range(8):
                        ap, bufs = mixK(kc)
                        self.A("pe", "matmul", [wB[k], bufs[g]], [pbB], pb[:, :],
                               lhsT=w[k][:, kc, ff * 128:(ff + 1) * 128], rhs=ap[:, ts], start=(kc == 0), stop=(kc == 7))
                    self.A("dve", "tensor_copy", [pbB], [ysB[f][g]], out=ys[:, f, ts], in_=pb[:, :])
                    ks = self.sq_rr
                    self.sq_rr = (ks + 1) % 3
                    self.A("act", "activation", [ysB[f][g]], [self.sqB[ks]], out=self.sq[ks][:], in_=ys[:, f, ts], func=AF.Square)
                    self.A("pe", "matmul", [self.sqB[ks], self.Bc], [ssb[g][1]], ssb[g][0][:, :],
                           lhsT=self.ones_b, rhs=self.sq[ks][:], start=(f == 0), stop=(f == 7))
        for g in range(4):
            self.postnorm_apply(g, gpost, lambda f, g=g: ys[:, f, g * 512:(g + 1) * 512], lambda f, g=g: ysB[f][g],
                                ssb[g][0], ssb[g][1])
        for g in range(4):
            self.release(ssb[g][0])


    def mixer_a(self, layer):
        self.S.barrier()
        A = self.A
        R0 = self.R0
        o = R0
        hT = self.alloc_once("a_hT", [128, 8, 1024], BF16, o); o += 16384
        mixT = self.alloc_once("a_mix", [128, 6, SEQ], BF16, o); o += 24576
        mq = self.alloc_once("a_mq", [128, 2, SEQ], BF16, o); o += 8192
        w = [self.alloc_once(f"a_w{i}", [128, 8, 256], BF16, o + 4096 * i) for i in range(2)]; o += 8192
        rows = {n: self.alloc_once("a_r" + n, [6, 1024], BF16, o + 2048 * i) for i, n in enumerate(["Em", "Ep", "KD", "BE", "be"])}
        o += 10240
        rt_at = o
        rtmp = [self.alloc_once(f"a_rt{i}", [6, 1024], F32, o + 4096 * i) for i in range(3)]; o += 12288
        raw = self.alloc_once("a_raw", [128, 3, 1028], BF16, o); o += 6176
        hist = self.alloc_once("a_hist", [128, 18, 4], BF16, o); o += 160
        dg = self.alloc_once("a_dg", [128, 12, 128], BF16, o); o += 3072
        c_at = o
        cqkv = self.alloc_once("a_c", [128, 3, 512], BF16, o); o += 3072
        scl = self.alloc_once("a_s", [128, 5, 512], BF16, o); o += 5120
        tok = self.alloc_once("a_tok", [128, 3, 512], BF16, o); o += 3072
        UWb = self.alloc_once("a_uw", [128, 4, 256], BF16, o); o += 2048
        MT = self.alloc_once("a_mt", [128, 8, 128], BF16, rt_at + 4096)
        QpT = self.alloc_once("a_qp", [128, 512], BF16, rt_at + 8192)
        junk = self.alloc_once("a_junk", [128, 128], BF16, SB_END - 256)
        junkB = Buf()
        self.yp_at = o
        YP = [self.alloc_once(f"a_yp{i}", [128, 4, 512], BF16, o + 4096 * i) for i in range(2)]; o += 8192
        itT = self.alloc_once("a_it", [128, 512], BF16, o); o += 1024
        Emb = self.alloc_once("a_emb", [128, 512], F32, o); o += 2048
        gs = self.alloc_once("a_gs", [128, 512], BF16, o); o += 1024
        Sbfa = self.alloc_once("a_Sb", [128, 12, 128], BF16, o); o += 3072
        spar = [0] * 6
        ont = self.alloc_once("a_on", [128, 4, 128], BF16, o); o += 1024
        st = self.alloc_once("a_st", [128, 8], F32, o); o += 32
        nA = self.alloc_once("a_nA", [6, 8], F32, o); o += 32
        assert o <= SB_END, o
        o2 = self.yp_at
        kTm = self.alloc_once("a_kTm", [128, 2, NMEM], BF16, o2); o2 += 1024
        vm = self.alloc_once("a_vm", [128, 2, 256], BF16, o2); o2 += 1024
        pT = [self.alloc_once(f"a_p{i}", [128, 512], BF16, o2 + 1024 * i) for i in range(2)]; o2 += 2048
        rec = self.alloc_once("a_rec", [128, 512], F32, o2); o2 += 2048
        B = lambda: Buf()
        wB = [B(), B()]; hB = [[B() for _ in range(2)] for _ in range(8)]
        mixB = [[B() for _ in range(4)] for _ in range(6)]; mqB = [[B() for _ in range(4)] for _ in range(2)]
        rowB = B(); rtB = [B(), B(), B()]; rawB = B(); histB = B(); dgB = B(); cB = B(); sclB = B(); tokB = B()
        ypB = [B(), B()]; itB = B(); embB = B(); gsB = B(); S32B = B(); SbB = B(); rhsB = B(); vnB = B(); onB = B()
        stB = B(); nAB = B(); kvB = B(); pTB = [B(), B()]; recB = B()
        cbt = self.alloc_once("a_cb2", [128, CB_N - CB_SEL], BF16, o)
        o += (CB_N - CB_SEL) * 2
        assert o <= SB_END, o
        self.dma(cbt[:], self.cb_d[:, CB_SEL:CB_N], [], [self.Bc], eng="pool")
        sel = lambda h: cbt[0:6, h * 128:(h + 1) * 128]
        mY = cbt[:, CB_MY - CB_SEL:CB_MY - CB_SEL + 512]; mP = cbt[:, CB_MP - CB_SEL:CB_MP - CB_SEL + 512]
        mI = cbt[:, CB_MI - CB_SEL:CB_MI - CB_SEL + 512]
        I4 = cbt[:, CB_I4 - CB_SEL:CB_I4 - CB_SEL + 512]
        gpre = V_GAIN + (layer * 4 + 0) * 8
        wd = self.w_in_a[0].rearrange("(k p) n -> p k n", p=128)
        W = 768
        vec = self.vec
        A("act", "activation", [self.Bc], [nAB], out=nA[:, 0:1], in_=vec[0:6, V_ALOG:V_ALOG + 1], func=AF.Exp)
        A("pool", "memset", [], [histB], hist[:], 0.0)
        wc = 0
        for half in range(2):
            for gg in range(2):
                self.prenorm(half * 2 + gg, gpre, lambda f, gg=gg: hT[:, f, gg * 512:(gg + 1) * 512], lambda f, gg=gg: hB[f][gg])
            k = wc % 2; wc += 1
            self.dma(w[k][:], wd[:, :, 3084:3340], [], [wB[k]], eng="pool")
            for mm in range(2):
                for gg in range(2):
                    g = half * 2 + gg
                    pb, pbB = self.bank()
                    for kc in range(8):
                        A("pe", "matmul", [wB[k], hB[kc][gg]], [pbB], pb[:, :], lhsT=w[k][:, kc, mm * 128:(mm + 1) * 128],
                          rhs=hT[:, kc, gg * 512:(gg + 1) * 512], start=(kc == 0), stop=(kc == 7))
                    self.copy_evac([pbB], [mqB[mm][g]], mq[:, mm, g * 512:(g + 1) * 512], pb[:, :])
            k = wc % 2; wc += 1
            self.dma(w[k][:, :, 0:12], wd[:, :, 3072:3084], [], [wB[k]], eng="pool")
            for gg in range(2):
                gsl = slice(gg * 512, (gg + 1) * 512)
                pb, pbB = self.bank()
                pa, paB = self.bank()
                for kc in range(8):
                    A("pe", "matmul", [wB[k], hB[kc][gg]], [pbB], pb[0:6, :], lhsT=w[k][:, kc, 0:6], rhs=hT[:, kc, gsl],
                      start=(kc == 0), stop=(kc == 7))
                for kc in range(8):
                    A("pe", "matmul", [wB[k], hB[kc][gg]], [paB], pa[0:6, :], lhsT=w[k][:, kc, 6:12], rhs=hT[:, kc, gsl],
                      start=(kc == 0), stop=(kc == 7))
                A("act", "activation", [pbB], [rtB[0]], out=rtmp[0][:, gsl], in_=pb[0:6, :], func=AF.Exp, scale=-1.0)
                A("dve", "tensor_scalar", [rtB[0]], [rtB[0]], out=rtmp[0][:, gsl], in0=rtmp[0][:, gsl], scalar1=1.0, scalar2=None, op0=ALU.add)
                A("dve", "reciprocal", [rtB[0]], [rtB[0]], out=rtmp[0][:, gsl], in_=rtmp[0][:, gsl])
                A("act", "activation", [paB, self.Bc], [rtB[1]], out=rtmp[1][:, gsl], in_=pa[0:6, :], func=AF.Exp,
                  bias=vec[0:6, V_DTB:V_DTB + 1])
                A("act", "activation", [rtB[1], self.Bc], [rtB[1]], out=rtmp[1][:, gsl], in_=rtmp[1][:, gsl], func=AF.Ln,
                  bias=vec[0:6, V_ONE:V_ONE + 1])
                A("dve", "tensor_scalar", [rtB[1], nAB], [rtB[1]], out=rtmp[1][:, gsl], in0=rtmp[1][:, gsl], scalar1=nA[:, 0:1],
                  scalar2=-1.0, op0=ALU.mult, op1=ALU.mult)
            for c in range(16):
                cs = slice(c * 64, (c + 1) * 64)
                A("dve", "tensor_tensor_scan", [rtB[1], self.Bc], [rtB[2]], out=rtmp[2][:, cs], data0=self.cf[0:6, 256:320],
                  data1=rtmp[1][:, cs], initial=0.0, op0=ALU.mult, op1=ALU.add)
            gc3 = rtmp[2][:, :].rearrange("p (c t) -> p c t", t=64)
            A("act", "activation", [rtB[2]], [rowB], out=rows["Em"][:], in_=rtmp[2][:], func=AF.Exp)
            A("act", "activation", [rtB[2]], [rowB], out=rows["Ep"][:], in_=rtmp[2][:], func=AF.Exp, scale=-1.0)
            A("dve", "tensor_copy", [rtB[0]], [rowB], out=rows["be"][:], in_=rtmp[0][:])
            A("act", "activation", [rtB[2]], [rtB[1]], out=rtmp[1][:], in_=rtmp[2][:], func=AF.Exp)
            A("dve", "tensor_tensor", [rtB[0], rtB[1]], [rowB], out=rows["BE"][:], in0=rtmp[0][:], in1=rtmp[1][:], op=ALU.mult)
            A("dve", "tensor_tensor", [rtB[2]], [rtB[1]], out=rtmp[1][:, :].rearrange("p (c t) -> p c t", t=64),
              in0=gc3[:, :, 63:64].to_broadcast([6, 16, 64]), in1=gc3, op=ALU.subtract)
            A("act", "activation", [rtB[1]], [rowB], out=rows["KD"][:], in_=rtmp[1][:], func=AF.Exp)
            for h in range(6):
                for ti, tile in enumerate([h, 6 + h, 12 + h]):
                    for tap in range(4):
                        col = V_CONV + tile * 4 + tap
                        A("dve", "tensor_scalar", [self.Bc], [dgB], out=dg[:, ti * 4 + tap, :], in0=self.ident_b,
                          scalar1=vec[:, col:col + 1], scalar2=None, op0=ALU.mult)
                for ti in range(3):
                    A("dve", "tensor_copy", [histB], [rawB], out=raw[:, ti, 0:3], in_=hist[:, h * 3 + ti, 0:3])
                for ti, tile in enumerate([h, 6 + h, 12 + h]):
                    k = wc % 2; wc += 1
                    self.dma(w[k][:, :, 0:128], wd[:, :, tile * 128:(tile + 1) * 128], [], [wB[k]], eng="pool")
                    for gg in range(2):
                        pb, pbB = self.bank()
                        for kc in range(8):
                            A("pe", "matmul", [wB[k], hB[kc][gg]], [pbB], pb[:, :], lhsT=w[k][:, kc, 0:128],
                              rhs=hT[:, kc, gg * 512:(gg + 1) * 512], start=(kc == 0), stop=(kc == 7))
                        self.copy_evac([pbB], [rawB], raw[:, ti, 3 + gg * 512:3 + (gg + 1) * 512], pb[:, :])
                for ti in range(3):
                    A("dve", "tensor_copy", [rawB], [histB], out=hist[:, h * 3 + ti, 0:3], in_=raw[:, ti, 1024:1027])
                for gg in range(2):
                    g = half * 2 + gg
                    gsl = slice(gg * 512, (gg + 1) * 512)
                    for ti in range(3):
                        pb, pbB = self.bank()
                        for tap in range(4):
                            A("pe", "matmul", [dgB, rawB], [pbB], pb[:, :], lhsT=dg[:, ti * 4 + tap, :],
                              rhs=raw[:, ti, gg * 512 + tap:gg * 512 + tap + 512], start=(tap == 0), stop=(tap == 3))
                        A("act", "activation", [pbB], [cB], out=cqkv[:, ti, :], in_=pb[:, :], func=AF.Silu)
                    for ti in range(2):
                        ks = self.sq_rr
                        self.sq_rr = (ks + 1) % 3
                        A("dve", "tensor_tensor", [cB], [self.sqB[ks]], out=self.sq[ks][:], in0=cqkv[:, ti, :], in1=cqkv[:, ti, :], op=ALU.mult)
                        pb, pbB = self.bank()
                        A("pe", "matmul", [self.sqB[ks], self.Bc], [pbB], pb[:, :], lhsT=self.ones_b, rhs=self.sq[ks][:], start=True, stop=True)
                        A("act", "activation", [pbB, self.Bc], [self.BlnB], out=self.lnb[:], in_=pb[:, :], func=AF.Ln,
                          bias=vec[:, V_EPS:V_EPS + 1])
                        A("act", "activation", [self.BlnB], [self.rstdB[ti]], out=self.rstd[ti][:], in_=self.lnb[:], func=AF.Exp, scale=-0.5)
                    def bc(name):
                        pb, pbB = self.bank()
                        A("pe", "matmul", [rowB, self.Bc], [pbB], pb[:, :], lhsT=sel(h), rhs=rows[name][:, gsl], start=True, stop=True)
                        return pb, pbB
                    pb, pbB = bc("Em")
                    A("act", "activation", [pbB], [embB], out=Emb[:], in_=pb[:, :], func=AF.Identity)
                    A("dve", "tensor_tensor", [pbB, self.rstdB[0]], [self.tmpfB[0]], out=self.tmpf[0][:], in0=self.rstd[0][:], in1=pb[:, :], op=ALU.mult)
                    A("dve", "scalar_tensor_tensor", [cB, self.tmpfB[0]], [sclB], out=scl[:, 0, :], in0=cqkv[:, 0, :], scalar=128.0 ** -0.5,
                      in1=self.tmpf[0][:], op0=ALU.mult, op1=ALU.mult)
                    for si, name in [(1, "Ep"), (2, "BE"), (3, "KD")]:
                        pb, pbB = bc(name)
                        tk = si % 2
                        A("dve", "tensor_tensor", [pbB, self.rstdB[1]], [self.tmpfB[tk]], out=self.tmpf[tk][:], in0=self.rstd[1][:], in1=pb[:, :], op=ALU.mult)
                        A("dve", "tensor_tensor", [cB, self.tmpfB[tk]], [sclB], out=scl[:, si, :], in0=cqkv[:, 1, :], in1=self.tmpf[tk][:], op=ALU.mult)
                    pb, pbB = bc("be")
                    A("dve", "tensor_tensor", [cB, pbB], [sclB], out=scl[:, 4, :], in0=cqkv[:, 2, :], in1=pb[:, :], op=ALU.mult)
                    for j, si in enumerate([3, 4]):
                        for blk in range(4):
                            A("pe", "transpose", [sclB, self.Bc], [self.pbhB], self.pbh[:, j * 512 + blk * 128:j * 512 + (blk + 1) * 128],
                              scl[:, si, blk * 128:(blk + 1) * 128], self.ident_b)
                    A("dve", "tensor_copy", [self.pbhB], [tokB], out=tok[:, 0:2, :], in_=self.pbh[:, :].rearrange("p (j t) -> p j t", j=2))
                    for blk in range(4):
                        A("pe", "transpose", [sclB, self.Bc], [self.pbhB], self.pbh[:, blk * 128:(blk + 1) * 128],
                          scl[:, 2, blk * 128:(blk + 1) * 128], self.ident_b)
                    A("dve", "tensor_copy", [self.pbhB], [tokB], out=tok[:, 2, :], in_=self.pbh[:, 0:512])
                    Y, P, R, Q = 0, 1, 2, 3
                    cur, nxt = 0, 1
                    for (la, ra, msk, dst, dB) in [(2, 1, mY, YP[cur][:, Y, :], ypB[cur]), (1, 2, mP, YP[cur][:, P, :], ypB[cur]),
                                                   (1, 0, mI, itT[:, :], itB)]:
                        pb, pbB = self.bank()
                        for blk in range(4):
                            bs = slice(blk * 128, (blk + 1) * 128)
                            A("pe", "matmul", [sclB], [pbB], pb[:, bs], lhsT=scl[:, la, bs], rhs=scl[:, ra, bs], start=True, stop=True)
                        A("dve", "tensor_tensor", [pbB, self.Bc], [dB], out=dst, in0=pb[:, :], in1=msk, op=ALU.mult)
                    uwB = rhsB
                    qpB = rtB[2]
                    mtB = rtB[1]
                    def Zv(i):
                        return YP[i][:, 2:4, :].rearrange("p a (b c) -> p (a b) c", c=256)
                    for j in (1, 2):
                        A("dve", "tensor_copy", [tokB], [ypB[cur]], out=Zv(cur)[:, :, (j - 1) * 128:j * 128],
                          in_=tok[:, j, :].rearrange("p (b c) -> p b c", c=128))
                    for lev in range(6):
                        c_, n_ = YP[cur], YP[nxt]
                        last = (lev == 5)
                        if not last:
                            for (la, ra, dsti) in [(P, Y, Y), (Y, P, P)]:
                                pb, pbB = self.bank()
                                for blk in range(4):
                                    bs = slice(blk * 128, (blk + 1) * 128)
                                    A("pe", "matmul", [ypB[cur]], [pbB], pb[:, bs], lhsT=c_[:, la, bs], rhs=c_[:, ra, bs], start=True, stop=True)
                                self.copy_evac([pbB], [ypB[nxt]], n_[:, dsti, :], pb[:, :])
                        for half2 in range(2):
                            pb, pbB = self.bank()
                            for b2 in range(2):
                                blk = half2 * 2 + b2
                                bs = slice(blk * 128, (blk + 1) * 128)
                                A("pe", "matmul", [ypB[cur]], [pbB], pb[:, b2 * 256:(b2 + 1) * 256], lhsT=c_[:, P, bs], rhs=Zv(cur)[:, blk, :],
                                  start=True, stop=True)
                            if last:
                                dstz = UWb[:, half2 * 2:half2 * 2 + 2, :].rearrange("p a b -> p (a b)")
                                dB = uwB
                            else:
                                dstz = YP[nxt][:, 2 + half2, :]
                                dB = ypB[nxt]
                            A("dve", "tensor_tensor", [pbB, ypB[cur]], [dB], out=dstz, in0=pb[:, :], in1=YP[cur][:, 2 + half2, :], op=ALU.add)
                        cur, nxt = nxt, cur
                    k = wc % 2; wc += 1
                    self.dma(w[k][:, :, 0:128], wd[:, :, 3 * W + h * 128:3 * W + (h + 1) * 128], [], [wB[k]], eng="pool")
                    pb, pbB = self.bank()
                    for kc in range(8):
                        A("pe", "matmul", [wB[k], hB[kc][gg]], [pbB], pb[:, :], lhsT=w[k][:, kc, 0:128], rhs=hT[:, kc, gsl],
                          start=(kc == 0), stop=(kc == 7))
                    A("act", "activation", [pbB], [gsB], out=gs[:], in_=pb[:, :], func=AF.Silu)
                    if half == 0 and gg == 0:
                        spar[h] = 0
                        A("pool", "memset", [], [SbB], Sbfa[:, 2 * h, :], 0.0)
                    import os
                    SK2 = os.environ.get("GDN_SKIP", "")
                    pb, pbB = self.bank()
                    for blk in range(4 if "qp" not in SK2 else 0):
                        bs = slice(blk * 128, (blk + 1) * 128)
                        A("pe", "matmul", [uwB, itB], [pbB], pb[:, bs], lhsT=UWb[:, blk, 128:256], rhs=itT[:, bs], start=True, stop=True)
                    if "qp" not in SK2:
                        A("dve", "tensor_tensor", [pbB, sclB], [qpB], out=QpT[:, :], in0=scl[:, 0, :], in1=pb[:, :], op=ALU.subtract)
                    for par in range(2):
                        pb, pbB = self.bank()
                        po = 64 * par
                        for c2 in range(4):
                            ci = c2 * 2 + par
                            blk = ci // 2
                            A("pe", "matmul", [uwB, tokB], [pbB], pb[:, c2 * 128:(c2 + 1) * 128], lhsT=UWb[po:po + 64, blk, 128:256],
                              rhs=tok[po:po + 64, 0, blk * 128:(blk + 1) * 128], start=True, stop=True)
                        for c2 in range(4):
                            ci = c2 * 2 + par
                            A("dve", "scalar_tensor_tensor", [pbB, embB, self.Bc], [mtB], out=MT[:, ci, :], in0=self.ident_b,
                              scalar=Emb[:, ci * 64 + 63:ci * 64 + 64], in1=pb[:, c2 * 128:(c2 + 1) * 128], op0=ALU.mult, op1=ALU.subtract)
                    for ci in range(8 if "chain" not in SK2 else 0):
                        blk, hb = ci // 2, ci % 2
                        po = 64 * hb
                        cs = slice(ci * 64, (ci + 1) * 64)
                        bcs = slice(blk * 128 + po, blk * 128 + po + 64)
                        Sold = Sbfa[:, 2 * h + spar[h], :]
                        Snew = Sbfa[:, 2 * h + 1 - spar[h], :]
                        spar[h] = 1 - spar[h]
                        import os
                        SK = os.environ.get("GDN_SKIP", "")
                        p3, p3B = self.bank()
                        A("pe", "matmul", [qpB, SbB], [p3B], p3[po:po + 64, 0:128], lhsT=QpT[:, cs], rhs=Sold, start=True, stop=False)
                        A("pe", "matmul", [itB, uwB], [p3B], p3[po:po + 64, 0:128], lhsT=itT[po:po + 64, bcs], rhs=UWb[po:po + 64, blk, 0:128],
                          start=False, stop=True)
                        if "state" not in SK:
                            p4, p4B = self.bank()
                            A("pe", "matmul", [mtB, SbB], [p4B], p4[:, 0:128], lhsT=MT[:, ci, :], rhs=Sold, start=True, stop=False)
                            A("pe", "matmul", [tokB, uwB], [p4B], p4[:, 0:128], lhsT=tok[po:po + 64, 0, blk * 128:(blk + 1) * 128],
                              rhs=UWb[po:po + 64, blk, 0:128], start=False, stop=True)
                            A("act", "activation", [p4B], [SbB], out=Snew, in_=p4[:, 0:128], func=AF.Identity)
                        if "out" in SK:
                            continue
                        A("act", "activation", [p3B], [junkB, stB], out=junk[po:po + 64, 0:128], in_=p3[po:po + 64, 0:128], func=AF.Square,
                          accum_out=st[po:po + 64, 0:1])
                        A("act", "activation", [stB, self.Bc], [stB], out=st[po:po + 64, 1:2], in_=st[po:po + 64, 0:1], func=AF.Ln, scale=1.0 / 128,
                          bias=vec[po:po + 64, V_EPS:V_EPS + 1])
                        A("act", "activation", [stB], [stB], out=st[po:po + 64, 2:3], in_=st[po:po + 64, 1:2], func=AF.Exp, scale=-0.5)
                        A("dve", "scalar_tensor_tensor", [p3B, stB, self.Bc], [onB], out=ont[po:po + 64, blk, :], in0=p3[po:po + 64, 0:128],
                          scalar=st[po:po + 64, 2:3], in1=self.onorm_bc[po:po + 64, :], op0=ALU.mult, op1=ALU.mult)
                    for blk in range(4):
                        A("pe", "transpose", [onB, self.Bc], [self.pbhB], self.pbh[:, blk * 128:(blk + 1) * 128], ont[:, blk, :], self.ident_b)
                    A("dve", "tensor_tensor", [self.pbhB, gsB], [mixB[h][g]], out=mixT[:, h, g * 512:(g + 1) * 512], in0=self.pbh[:, 0:512],
                      in1=gs[:], op=ALU.mult)
        self.S.barrier()
        self.mem_kv(layer, kTm, vm, kvB, w, wB)
        self.mem_attn(mq, mqB, kTm, vm, kvB, pT, pTB, rec, recB)
        self.S.barrier()
        ys = self.alloc_once("a_ys", [128, 8, SEQ], BF16, self.R0 + 16384 + 24576 + 8192 + 8192)
        ysB = [[Buf() for _ in range(4)] for _ in range(8)]

        def mixK(kc):
            if kc < 6:
                return mixT[:, kc, :], mixB[kc]
            return mq[:, kc - 6, :], mqB[kc - 6]
        self.out_proj(layer, mixK, ys, ysB, w, wB)

    def mixer_b(self, layer):
        self.S.barrier()
        R0 = self.R0
        o = R0
        hT = self.alloc_once("b_hT", [128, 8, SEQ], BF16, o); o += 32768
        qT = self.alloc_once("b_qT", [128, 6, SEQ], BF16, o); o += 24576
        kT = self.alloc_once("b_kT", [128, 6, SEQ], BF16, o); o += 24576
        vt = self.alloc_once("b_v", [128, 16, 768], BF16, o); o += 24576
        mq = self.alloc_once("b_mq", [128, 2, SEQ], BF16, o); o += 8192
        w = [self.alloc_once(f"b_w{i}", [128, 8, 256], BF16, o + 4096 * i) for i in range(2)]
        o += 8192
        self.mix_end = o
        assert o <= SB_END
        wB = [Buf(), Buf()]
        hB = [[Buf() for _ in range(4)] for _ in range(8)]
        qB = [[Buf() for _ in range(4)] for _ in range(6)]
        kB = [[Buf() for _ in range(4)] for _ in range(6)]
        vB = [Buf() for _ in range(16)]
        mqB = [[Buf() for _ in range(4)] for _ in range(2)]
        gpre = V_GAIN + (layer * 4 + 0) * 8
        for g in range(4):
            self.prenorm(g, gpre, lambda f, g=g: hT[:, f, g * 512:(g + 1) * 512], lambda f, g=g: hB[f][g])
        wd = self.w_in_b[0].rearrange("(k p) n -> p k n", p=128)
        wc = 0
        for piece in list(range(0, 6)) + [9]:
            k = wc % 2
            wc += 1
            self.dma(w[k][:], wd[:, :, piece * 256:(piece + 1) * 256], [], [wB[k]], eng="pool")
            for mm in range(2):
                tile = piece * 2 + mm
                for g in range(4):
                    ts = slice(g * 512, (g + 1) * 512)
                    pb, pbB = self.bank()
                    for kc in range(8):
                        self.A("pe", "matmul", [wB[k], hB[kc][g]], [pbB], pb[:, :], lhsT=w[k][:, kc, mm * 128:(mm + 1) * 128],
                               rhs=hT[:, kc, ts], start=(kc == 0), stop=(kc == 7))
                    if tile < 6:
                        self.copy_evac([pbB], [qB[tile][g]], qT[:, tile, ts], pb[:, :], scale=0.125)
                    elif tile < 12:
                        self.copy_evac([pbB], [kB[tile - 6][g]], kT[:, tile - 6, ts], pb[:, :])
                    else:
                        self.copy_evac([pbB], [mqB[tile - 18][g]], mq[:, tile - 18, ts], pb[:, :])
        for piece in range(6, 9):
            k = wc % 2
            wc += 1
            self.dma(w[k][:], wd[:, :, piece * 256:(piece + 1) * 256], [], [wB[k]], eng="pool")
            for tt in range(16):
                pb, pbB = self.bank()
                for kc in range(8):
                    self.A("pe", "matmul", [wB[k], hB[kc][tt // 4]], [pbB], pb[:, 0:256],
                           lhsT=hT[:, kc, tt * 128:(tt + 1) * 128], rhs=w[k][:, kc, :], start=(kc == 0), stop=(kc == 7))
                self.copy_evac([pbB], [vB[tt]], vt[:, tt, (piece - 6) * 256:(piece - 5) * 256], pb[:, 0:256])
        self.S.barrier()
        o = R0
        NH = 4
        eT = [self.alloc_once(f"b_e{i}", [128, 512], F32, o + 2048 * i) for i in range(NH)]; o += 2048 * NH
        nl = [self.alloc_once(f"b_nl{i}", [128, 512], BF16, o + 1024 * i) for i in range(NH)]; o += 1024 * NH
        Sb = [self.alloc_once(f"b_S{i}", [128, 512], BF16, o + 1024 * i) for i in range(NH)]; o += 1024 * NH
        aT = [self.alloc_once(f"b_a{i}", [128, 512], BF16, o + 1024 * i) for i in range(NH)]; o += 1024 * NH
        pT = [self.alloc_once(f"b_p{i}", [128, 512], BF16, o + 1024 * i) for i in range(2)]; o += 2048
        rec = self.alloc_once("b_rec", [128, 512], F32, o); o += 2048
        kTm = self.alloc_once("b_kTm", [128, 2, NMEM], BF16, o); o += 1024
        vm = self.alloc_once("b_vm", [128, 2, 256], BF16, o); o += 1024
        assert o <= R0 + 32768
        eB = [Buf() for _ in range(NH)]; nlB = [Buf() for _ in range(NH)]; SbB = [Buf() for _ in range(NH)]
        aB = [Buf() for _ in range(NH)]
        pTB = [Buf(), Buf()]; recB = Buf(); kvB = Buf()
        self.mem_kv(layer, kTm, vm, kvB, w, wB)
        self.mem_attn(mq, mqB, kTm, vm, kvB, pT, pTB, rec, recB)
        A = self.A
        for hg in range(12 // NH):
            heads = [hg * NH + i for i in range(NH)]
            for g in range(4):
                ts = slice(g * 512, (g + 1) * 512)
                bmax = 4 * g + 3
                obs = {}
                for h in heads:
                    if h // 2 not in obs:
                        obs[h // 2] = self.reserve()
                for i in range(NH):
                    A("pool", "memset", [], [SbB[i]], Sb[i][:], 0.0)
                for b in range(bmax, -1, -1):
                    r = b - 4 * g
                    bs = slice(b * 128, (b + 1) * 128)
                    c0 = 128 * r if r > 0 else 0
                    cs = slice(c0, 512)
                    tsc = slice(g * 512 + c0, (g + 1) * 512)
                    msk = self.maskbase[:, 384 - 128 * r + c0:384 - 128 * r + 512] if r >= 0 else None
                    zbs = []
                    for i, h in enumerate(heads):
                        tl, po = h // 2, 64 * (h % 2)
                        zb, zbB = self.bank()
                        zbs.append((zb, zbB))
                        A("pe", "matmul", [kB[tl][b // 4], qB[tl][g]], [zbB], zb[:, cs], lhsT=kT[po:po + 64, tl, bs],
                          rhs=qT[po:po + 64, tl, tsc], start=True, stop=(r < 0))
                        if r >= 0:
                            A("pe", "matmul", [self.Bc], [zbB], zb[:, cs], lhsT=self.ident_b, rhs=msk, start=False, stop=True)
                    for i, h in enumerate(heads):
                        A("act", "activation", [zbs[i][1]], [eB[i]], out=eT[i][:, cs], in_=zbs[i][0][:, cs], func=AF.Exp)
                    for i, h in enumerate(heads):
                        A("act", "activation", [eB[i], self.Bc], [nlB[i]], out=nl[i][:, cs], in_=eT[i][:, cs], func=AF.Ln,
                          bias=self.vec[:, V_ONE:V_ONE + 1])
                    abs_ = []
                    for i, h in enumerate(heads):
                        tl, po = h // 2, 64 * (h % 2)
                        ab, abB = self.bank()
                        abs_.append((ab, abB))
                        A("pe", "matmul", [kB[tl][b // 4], qB[tl][g]], [abB], ab[:, cs], lhsT=kT[po:po + 64, tl, bs],
                          rhs=qT[po:po + 64, tl, tsc], start=True, stop=False)
                        if r >= 0:
                            A("pe", "matmul", [self.Bc], [abB], ab[:, cs], lhsT=self.ident_b, rhs=msk, start=False, stop=False)
                        A("pe", "matmul", [self.Bc, nlB[i]], [abB], ab[:, cs], lhsT=self.negU_b, rhs=nl[i][:, cs],
                          start=False, stop=(b == bmax))
                        if b < bmax:
                            A("pe", "matmul", [self.Bc, SbB[i]], [abB], ab[:, cs], lhsT=self.negones_b, rhs=Sb[i][:, cs],
                              start=False, stop=True)
                    for i, h in enumerate(heads):
                        A("act", "activation", [abs_[i][1]], [aB[i]], out=aT[i][:, cs], in_=abs_[i][0][:, cs], func=AF.Exp)
                    for i, h in enumerate(heads):
                        tl, po = h // 2, 64 * (h % 2)
                        ob, obB = obs[tl]
                        A("pe", "matmul", [vB[b], aB[i]], [obB], ob[po:po + 64, cs], lhsT=vt[:, b, h * 64:(h + 1) * 64],
                          rhs=aT[i][:, cs], start=(b == bmax), stop=(b == 0))
                        if b > 0:
                            A("dve", "tensor_tensor", [nlB[i], SbB[i]], [SbB[i]], out=Sb[i][:, cs], in0=Sb[i][:, cs], in1=nl[i][:, cs],
                              op=ALU.add)
                for tl, (ob, obB) in obs.items():
                    A("dve", "tensor_copy", [obB, qB[tl][g]], [qB[tl][g]], out=qT[:, tl, ts], in_=ob[:, :])
                    self.release(ob)
        self.S.barrier()
        ys = hT
        ysB = hB

        def mixK(kc):
            if kc < 6:
                return qT[:, kc, :], qB[kc]
            return mq[:, kc - 6, :], mqB[kc - 6]
        self.out_proj(layer, mixK, ys, ysB, w, wB)

    def mlp(self, layer):
        self.S.barrier()
        R0 = self.R0
        if not hasattr(self, "mlp_bufs"):
            o = R0
            hT = self.alloc("m_hT", [128, 8, 1024], BF16, at=o); o += 16384
            uT = self.alloc("m_uT", [128, 32, 1024], BF16, at=o); o += 65536
            wu = []
            for i in range(2):
                wu.append(self.alloc(f"m_wu{i}", [128, 8, 512], BF16, at=o)); o += 8192
            wd = []
            for i in range(2):
                wd.append(self.alloc(f"m_wd{i}", [128, 32, 128], BF16, at=o)); o += 8192
            rl = []
            for i in range(2):
                rl.append(self.alloc(f"m_rl{i}", [128, 512], BF16, at=o)); o += 1024
            self.mlp_bufs = dict(hT=hT, uT=uT, wu=wu, wd=wd, rl=rl,
                                 hTB=[[Buf() for _ in range(2)] for _ in range(8)],
                                 uTB=[[Buf() for _ in range(2)] for _ in range(32)],
                                 wuB=[Buf(), Buf()], wdB=[Buf(), Buf()], rlB=[Buf(), Buf()])
            self.mlp_end = o
        mb = self.mlp_bufs
        hT, uT = mb["hT"], mb["uT"]
        gpre = V_GAIN + (layer * 4 + 2) * 8
        gpost = V_GAIN + (layer * 4 + 3) * 8
        wu_d = self.w_up[layer].rearrange("(k p) n -> p k n", p=128)
        wd_d = self.w_down[layer].rearrange("(k p) n -> p k n", p=128)
        wcnt = 0
        rcnt = 0
        for half in range(2):
            for gg in range(2):
                g = half * 2 + gg
                self.prenorm(g, gpre, lambda f, gg=gg: hT[:, f, gg * 512:(gg + 1) * 512],
                             lambda f, gg=gg: mb["hTB"][f][gg])
            for jq in range(8):
                k = wcnt % 2
                wcnt += 1
                w, wB = mb["wu"][k], mb["wuB"][k]
                self.dma(w[:], wu_d[:, :, jq * 512:(jq + 1) * 512], [], [wB], eng="pool")
                for jj in range(4):
                    j = jq * 4 + jj
                    for gg in range(2):
                        pb, pbB = self.bank()
                        for kc in range(8):
                            self.A("pe", "matmul", [wB, mb["hTB"][kc][gg]], [pbB], pb[:, :],
                                   lhsT=w[:, kc, jj * 128:(jj + 1) * 128],
                                   rhs=hT[:, kc, gg * 512:(gg + 1) * 512], start=(kc == 0), stop=(kc == 7))
                        r = rcnt % 2
                        rcnt += 1
                        self.A("act", "activation", [pbB], [mb["rlB"][r]], out=mb["rl"][r][:], in_=pb[:, :],
                               func=AF.Relu)
                        self.A("dve", "tensor_tensor", [mb["rlB"][r]], [mb["uTB"][j][gg]],
                               out=uT[:, j, gg * 512:(gg + 1) * 512], in0=mb["rl"][r][:], in1=mb["rl"][r][:],
                               op=ALU.mult)
            ssb = [self.reserve(), self.reserve()]
            for fq in range(8):
                k = wcnt % 2
                wcnt += 1
                w, wB = mb["wd"][k], mb["wdB"][k]
                self.dma(w[:], wd_d[:, :, fq * 128:(fq + 1) * 128], [], [wB], eng="pool")
                for ff in range(1):
                    f = fq
                    for gg in range(2):
                        pb, pbB = self.bank()
                        for kc in range(32):
                            self.A("pe", "matmul", [wB, mb["uTB"][kc][gg]], [pbB], pb[:, :],
                                   lhsT=w[:, kc, ff * 128:(ff + 1) * 128],
                                   rhs=uT[:, kc, gg * 512:(gg + 1) * 512], start=(kc == 0), stop=(kc == 31))
                        ks = self.sq_rr
                        self.sq_rr = (ks + 1) % 3
                        self.A("dve", "tensor_copy", [pbB], [mb["hTB"][f][gg]],
                               out=hT[:, f, gg * 512:(gg + 1) * 512], in_=pb[:, :])
                        self.A("act", "activation", [mb["hTB"][f][gg]], [self.sqB[ks]], out=self.sq[ks][:],
                               in_=hT[:, f, gg * 512:(gg + 1) * 512], func=AF.Square)
                        self.A("pe", "matmul", [self.sqB[ks], self.Bc], [ssb[gg][1]], ssb[gg][0][:, :],
                               lhsT=self.ones_b, rhs=self.sq[ks][:], start=(f == 0), stop=(f == 7))
            for gg in range(2):
                g = half * 2 + gg
                self.postnorm_apply(g, gpost, lambda f, gg=gg: hT[:, f, gg * 512:(gg + 1) * 512],
                                    lambda f, gg=gg: mb["hTB"][f][gg], ssb[gg][0], ssb[gg][1])
            for gg in range(2):
                self.release(ssb[gg][0])


def make_inputs_common(inp):
    vec, cf, cb = host_consts(inp)
    com = {k: np.ascontiguousarray(np.asarray(inp[k], np.float32)) for k in
           ["w_in_a", "w_in_b", "w_mem_kv", "w_out", "w_up", "w_down"]}
    com["vecs"] = vec
    com["cf"] = cf
    com["cb"] = cb
    return com


def kernel(**inputs):
    n = 8
    x = np.asarray(inputs["x"], np.float32)
    mem = np.asarray(inputs["mem"], np.float32)
    nseq = x.shape[0] // n
    prog = Prog(nseq)
    nc = prog.build()
    com = make_inputs_common(inputs)
    in_maps = []
    for c in range(n):
        m = dict(com)
        m["x"] = np.ascontiguousarray(x[c * nseq:(c + 1) * nseq])
        m["mem"] = np.ascontiguousarray(mem[c * nseq:(c + 1) * nseq])
        in_maps.append(m)
    res = run_bass_kernel_spmd(nc, in_maps, core_ids=list(range(n)))
    return np.concatenate([r["out"] for r in res.results], axis=0)
```
